# Optimizing a Trainium2 kernel written in Bass

```python
import jax, jax.numpy as jnp
from jax import lax
import numpy as np

D_MODEL = 1024
BATCH = 16
SEQ = 2048
DEPTH = 1

GRID_W = 64
NA_HEADS = 8
NA_HEAD_DIM = 64
NA_WIN_H_MAX = 8
NA_WIN_W = 16
NA_WIDTH = NA_HEADS * NA_HEAD_DIM
MLA_HEADS = 8
MLA_Q_RANK = 384
MLA_KV_RANK = 256
MLA_NOPE_DIM = 64
MLA_ROPE_DIM = 32
MLA_V_DIM = 64
MLA_WIDTH = MLA_HEADS * MLA_V_DIM
ROPE_BASE = 10000.0
Q_BLOCK = 128
MIX_WIDTH = NA_WIDTH + MLA_WIDTH
IN_PROJ_WIDTH = 3 * NA_WIDTH + MLA_Q_RANK + MLA_KV_RANK + MLA_ROPE_DIM
PEER_HEADS = 8
PEER_N_KEYS = 128
PEER_N_EXPERTS = PEER_N_KEYS * PEER_N_KEYS
PEER_KEY_DIM = 256
PEER_TOPK = 16
PEER_CHUNK = 128
PLE_DIM = 256
DN_ALPHA = float((2 * DEPTH) ** 0.25)
DN_BETA = float((8 * DEPTH) ** -0.25)
LN_EPS = 1e-5

kernel_name = "hybrid_na_mla_peer_deepnorm_block"


def layer_norm(x, g, b):
    xf = x.astype(jnp.float32)
    mu = jnp.mean(xf, axis=-1, keepdims=True)
    var = jnp.mean(jnp.square(xf - mu), axis=-1, keepdims=True)
    y = (xf - mu) * lax.rsqrt(var + LN_EPS)
    return (y * g.astype(jnp.float32) + b.astype(jnp.float32)).astype(x.dtype)


def rms_norm(x, g):
    xf = x.astype(jnp.float32)
    y = xf * lax.rsqrt(jnp.mean(jnp.square(xf), axis=-1, keepdims=True) + LN_EPS)
    return (y * g.astype(jnp.float32)).astype(x.dtype)


def rope_2d_tables(seq, dtype):
    t = jnp.arange(seq)
    row = (t // GRID_W).astype(jnp.float32)
    col = (t % GRID_W).astype(jnp.float32)
    axis_dim = MLA_ROPE_DIM // 2
    inv = ROPE_BASE ** (-jnp.arange(0, axis_dim, 2, dtype=jnp.float32) / axis_dim)
    ang = jnp.concatenate([row[:, None] * inv[None, :], col[:, None] * inv[None, :]], axis=-1)
    return jnp.cos(ang).astype(dtype), jnp.sin(ang).astype(dtype)


def apply_rope(x, cos, sin):
    xp = x.reshape(x.shape[:-1] + (MLA_ROPE_DIM // 2, 2))
    x1, x2 = xp[..., 0], xp[..., 1]
    out = jnp.stack([x1 * cos - x2 * sin, x1 * sin + x2 * cos], axis=-1)
    return out.reshape(x.shape)


def neighbourhood_attention(q, k, v, rpb):
    b, s, h, dh = q.shape
    rows = s // GRID_W
    kh = min(NA_WIN_H_MAX, rows)
    kw = NA_WIN_W
    qg = q.reshape(b, rows, GRID_W, h, dh)
    kg = k.reshape(b, rows, GRID_W, h, dh)
    vg = v.reshape(b, rows, GRID_W, h, dh)
    cols = jnp.arange(GRID_W)
    col_start = jnp.clip(cols - kw // 2, 0, GRID_W - kw)
    col_idx = col_start[:, None] + jnp.arange(kw)[None, :]
    dj = col_idx - cols[:, None] + (NA_WIN_W - 1)
    row_ids = jnp.arange(rows)
    row_start = jnp.clip(row_ids - kh // 2, 0, rows - kh)
    scale = dh ** -0.5

    def one_row(args):
        q_row, r, rs = args
        k_band = lax.dynamic_slice_in_dim(kg, rs, kh, axis=1)
        v_band = lax.dynamic_slice_in_dim(vg, rs, kh, axis=1)
        k_win = k_band[:, :, col_idx]
        v_win = v_band[:, :, col_idx]
        di = rs + jnp.arange(kh) - r + (NA_WIN_H_MAX - 1)
        bias = rpb[:, di][:, :, dj]
        bias = jnp.transpose(bias, (0, 2, 1, 3)).astype(jnp.float32)
        sc = jnp.einsum('bqhd,biqjhd->bhqij', q_row * scale, k_win).astype(jnp.float32) + bias[None]
        pr = jax.nn.softmax(sc.reshape(b, h, GRID_W, kh * kw), axis=-1)
        pr = pr.reshape(b, h, GRID_W, kh, kw).astype(v.dtype)
        return jnp.einsum('bhqij,biqjhd->bqhd', pr, v_win)

    out = lax.map(one_row, (jnp.moveaxis(qg, 1, 0), row_ids, row_start))
    return jnp.moveaxis(out, 0, 1).reshape(b, s, h * dh)


def latent_attention(c_q, c_kv, k_r, q_norm_g, kv_norm_g, w_uq, w_ukv, cos, sin):
    b, s, _ = c_kv.shape
    qd = MLA_NOPE_DIM + MLA_ROPE_DIM
    q = (rms_norm(c_q, q_norm_g) @ w_uq).reshape(b, s, MLA_HEADS, qd)
    q_nope, q_rope = q[..., :MLA_NOPE_DIM], q[..., MLA_NOPE_DIM:]
    q_rope = apply_rope(q_rope, cos[:, None, :], sin[:, None, :])
    kv = (rms_norm(c_kv, kv_norm_g) @ w_ukv).reshape(b, s, MLA_HEADS, MLA_NOPE_DIM + MLA_V_DIM)
    k_nope, v = kv[..., :MLA_NOPE_DIM], kv[..., MLA_NOPE_DIM:]
    k_rope = apply_rope(k_r, cos, sin)
    k_rope = jnp.broadcast_to(k_rope[:, :, None, :], (b, s, MLA_HEADS, MLA_ROPE_DIM))
    q = jnp.concatenate([q_nope, q_rope], axis=-1) * (qd ** -0.5)
    k = jnp.concatenate([k_nope, k_rope], axis=-1)
    nq = s // Q_BLOCK
    qb = jnp.moveaxis(q.reshape(b, nq, Q_BLOCK, MLA_HEADS, qd), 1, 0)

    def one_block(q_blk):
        sc = jnp.einsum('bqhd,bkhd->bhqk', q_blk, k).astype(jnp.float32)
        pr = jax.nn.softmax(sc, axis=-1).astype(v.dtype)
        return jnp.einsum('bhqk,bkhd->bqhd', pr, v)

    out = lax.map(one_block, qb)
    return jnp.moveaxis(out, 0, 1).reshape(b, s, MLA_WIDTH)


def peer_ffn(x, w_q, sub_keys, u_table, v_table):
    b, s, d = x.shape
    half = PEER_KEY_DIM // 2
    xc = x.reshape((b * s) // PEER_CHUNK, PEER_CHUNK, d)

    def one_chunk(xb):
        q = (xb @ w_q).reshape(PEER_CHUNK, PEER_HEADS, PEER_KEY_DIM)
        s1 = jnp.einsum('thd,kd->thk', q[..., :half], sub_keys[0])
        s2 = jnp.einsum('thd,kd->thk', q[..., half:], sub_keys[1])
        v1, i1 = lax.top_k(s1, PEER_TOPK)
        v2, i2 = lax.top_k(s2, PEER_TOPK)
        cand = (v1[..., :, None] + v2[..., None, :]).reshape(PEER_CHUNK, PEER_HEADS, PEER_TOPK * PEER_TOPK)
        cv, ci = lax.top_k(cand, PEER_TOPK)
        e1 = jnp.take_along_axis(i1, ci // PEER_TOPK, axis=-1)
        e2 = jnp.take_along_axis(i2, ci % PEER_TOPK, axis=-1)
        idx = (e1 * PEER_N_KEYS + e2).reshape(PEER_CHUNK, PEER_HEADS * PEER_TOPK)
        g = jax.nn.softmax(cv.astype(jnp.float32), axis=-1).reshape(PEER_CHUNK, -1).astype(xb.dtype)
        u = u_table[idx]
        hpre = jnp.einsum('td,ted->te', xb, u)
        act = g * jax.nn.gelu(hpre, approximate=False)
        return jnp.einsum('te,ted->td', act, v_table[idx])

    return lax.map(one_chunk, xc).reshape(b, s, d)


def setup_inputs(seed: int = 0) -> dict:
    key = jax.random.key(seed)
    ks = jax.random.split(key, 24)
    f32 = jnp.float32
    nrm = lambda k, shape, sc: jax.random.normal(k, shape, f32) * sc
    D = D_MODEL
    qd = MLA_NOPE_DIM + MLA_ROPE_DIM
    return {
        "x": nrm(ks[0], (BATCH, SEQ, D), 1.0),
        "p": nrm(ks[1], (DEPTH, BATCH, SEQ, PLE_DIM), 1.0),
        "emb_ln_g": 1.0 + nrm(ks[2], (D,), 0.02),
        "emb_ln_b": nrm(ks[3], (D,), 0.02),
        "w_in": nrm(ks[4], (DEPTH, D, IN_PROJ_WIDTH), D ** -0.5),
        "mla_q_norm_g": 1.0 + nrm(ks[5], (DEPTH, MLA_Q_RANK), 0.02),
        "mla_kv_norm_g": 1.0 + nrm(ks[6], (DEPTH, MLA_KV_RANK), 0.02),
        "w_uq": nrm(ks[7], (DEPTH, MLA_Q_RANK, MLA_HEADS * qd), MLA_Q_RANK ** -0.5),
        "w_ukv": nrm(ks[8], (DEPTH, MLA_KV_RANK, MLA_HEADS * (MLA_NOPE_DIM + MLA_V_DIM)), MLA_KV_RANK ** -0.5),
        "na_rpb": nrm(ks[9], (DEPTH, NA_HEADS, 2 * NA_WIN_H_MAX - 1, 2 * NA_WIN_W - 1), 0.5),
        "w_o": nrm(ks[10], (DEPTH, MIX_WIDTH, D), MIX_WIDTH ** -0.5 * DN_BETA),
        "ln1_g": 1.0 + nrm(ks[11], (DEPTH, D), 0.02),
        "ln1_b": nrm(ks[12], (DEPTH, D), 0.02),
        "peer_w_q": nrm(ks[13], (DEPTH, D, PEER_HEADS * PEER_KEY_DIM), D ** -0.5),
        "peer_sub_keys": nrm(ks[14], (DEPTH, 2, PEER_N_KEYS, PEER_KEY_DIM // 2), (PEER_KEY_DIM // 2) ** -0.5),
        "peer_u": nrm(ks[15], (DEPTH, PEER_N_EXPERTS, D), D ** -0.5),
        "peer_v": nrm(ks[16], (DEPTH, PEER_N_EXPERTS, D), DN_BETA * PEER_HEADS ** -0.5),
        "ple_w": nrm(ks[17], (DEPTH, PLE_DIM, D), PLE_DIM ** -0.5 * DN_BETA),
        "ple_gate_w": nrm(ks[18], (DEPTH, D, D), D ** -0.5),
        "ple_gate_b": nrm(ks[19], (DEPTH, D), 0.02),
        "ln2_g": 1.0 + nrm(ks[20], (DEPTH, D), 0.02),
        "ln2_b": nrm(ks[21], (DEPTH, D), 0.02),
    }


def reference(x, p, emb_ln_g, emb_ln_b, w_in, mla_q_norm_g, mla_kv_norm_g, w_uq, w_ukv,
              na_rpb, w_o, ln1_g, ln1_b, peer_w_q, peer_sub_keys, peer_u, peer_v,
              ple_w, ple_gate_w, ple_gate_b, ln2_g, ln2_b):
    b, s, _ = x.shape
    cos, sin = rope_2d_tables(s, x.dtype)
    split_points = list(np.cumsum([NA_WIDTH, NA_WIDTH, NA_WIDTH, MLA_Q_RANK, MLA_KV_RANK]))
    h = layer_norm(x, emb_ln_g, emb_ln_b)
    for i in range(DEPTH):
        z = h @ w_in[i]
        q_na, k_na, v_na, c_q, c_kv, k_r = jnp.split(z, split_points, axis=-1)
        hd = (b, s, NA_HEADS, NA_HEAD_DIM)
        a_na = neighbourhood_attention(q_na.reshape(hd), k_na.reshape(hd), v_na.reshape(hd), na_rpb[i])
        a_mla = latent_attention(c_q, c_kv, k_r, mla_q_norm_g[i], mla_kv_norm_g[i], w_uq[i], w_ukv[i], cos, sin)
        mix = jnp.concatenate([a_na, a_mla], axis=-1) @ w_o[i]
        h = layer_norm(DN_ALPHA * h + mix, ln1_g[i], ln1_b[i])
        ffn = peer_ffn(h, peer_w_q[i], peer_sub_keys[i], peer_u[i], peer_v[i])
        gate = jax.nn.sigmoid(h @ ple_gate_w[i] + ple_gate_b[i])
        ple = gate * (p[i] @ ple_w[i])
        h = layer_norm(DN_ALPHA * h + ffn + ple, ln2_g[i], ln2_b[i])
    return h
```

```python
import numpy as np
import concourse.bass as bass
import concourse.mybir as mybir
from concourse.bass_utils import run_bass_kernel_spmd
from contextlib import ExitStack

F32 = mybir.dt.float32
BF16 = mybir.dt.bfloat16
U32 = mybir.dt.uint32
ALU = mybir.AluOpType
AF = mybir.ActivationFunctionType
AX = mybir.AxisListType

ALPHA = float(2.0 ** 0.25)
EPS = 1e-5
NEG = -30000.0


class Trk:
    __slots__ = ("name", "w", "r")

    def __init__(self, name=""):
        self.name = name
        self.w = None
        self.r = {}


class _Rec:
    def __getattr__(self, name):
        def f(*args, **kw):
            self.call = (name, args, kw)
        return f


class Prog:
    ENG = ("pe", "act", "dve", "pool", "sp")

    def __init__(self, nc, es):
        self.nc = nc
        self.es = es
        self.ops = {e: [] for e in self.ENG}
        self.cnt = {}
        self.sem = {}
        self.seen = {e: {} for e in self.ENG}
        for e in self.ENG:
            self._mksem(e)
        self.ndma = {e: 0 for e in self.ENG}

    def _mksem(self, key):
        self.sem[key] = self.es.enter_context(self.nc.semaphore(name=f"s_{key}"))
        self.cnt[key] = 0

    def _deps(self, eng, r, w):
        deps = {}
        for t in r:
            if t.w is not None and t.w[1] > deps.get(t.w[0], 0):
                deps[t.w[0]] = t.w[1]
        for t in w:
            if t.w is not None and t.w[1] > deps.get(t.w[0], 0):
                deps[t.w[0]] = t.w[1]
            for k, v in t.r.items():
                if v > deps.get(k, 0):
                    deps[k] = v
        waits = []
        for k, v in deps.items():
            if k == eng and eng == "pe":
                continue
            if self.seen[eng].get(k, 0) >= v:
                continue
            self.seen[eng][k] = v
            waits.append((k, v))
        return waits

    def op(self, eng, fn, r=(), w=()):
        rec = _Rec()
        fn(rec)
        call = rec.call
        fn = lambda e, call=call: getattr(e, call[0])(*call[1], **call[2])
        waits = self._deps(eng, r, w)
        self.cnt[eng] += 1
        n = self.cnt[eng]
        self.ops[eng].append((fn, waits, eng, 1))
        for t in r:
            t.r[eng] = n
        for t in w:
            t.w = (eng, n)
            t.r = {}

    def dma(self, q, out, in_, r=(), w=(), **kw):
        waits = self._deps(q, r, w)
        key = f"d{q}{self.ndma[q] % (8 if q == 'pool' else 24)}"
        self.ndma[q] += 1
        if key not in self.sem:
            self._mksem(key)
        prev = self.cnt[key]
        if prev and self.seen[q].get(key, 0) < prev:
            self.seen[q][key] = prev
            waits.append((key, prev))
        self.cnt[key] += 16
        n = self.cnt[key]
        self.ops[q].append((lambda e: e.dma_start(out=out, in_=in_, **kw), waits, key, 16))
        for t in r:
            t.r[key] = n
        for t in w:
            t.w = (key, n)
            t.r = {}

    def barrier(self):
        snap = dict(self.cnt)
        for e in self.ENG:
            waits = []
            for k, v in snap.items():
                if v and k != e and self.seen[e].get(k, 0) < v:
                    self.seen[e][k] = v
                    waits.append((k, v))
            if waits:
                self.ops[e].append((None, waits, None, 0))

    def finish(self):
        fin = [(k, v) for k, v in self.cnt.items() if k.startswith("d") and k not in self.ENG and v > 0]
        nc = self.nc
        names = {"pe": "tensor", "act": "scalar", "dve": "vector", "pool": "gpsimd", "sp": "sync"}
        with nc.Block() as block:
            for e in self.ENG:
                def body(eng, e=e):
                    for fn, waits, key, amt in self.ops[e]:
                        for k, v in waits:
                            eng.wait_ge(self.sem[k], v)
                        if fn is not None:
                            ins = fn(eng)
                            ins.then_inc(self.sem[key], amt)
                    if e == "sp":
                        for k, v in fin:
                            eng.wait_ge(self.sem[k], v)
                getattr(block, names[e])(body)


def tl(n, name=""):
    return [Trk(f"{name}{i}") for i in range(n)]


def build(nseq=2, do_b=True, debug=False):
    nc = bass.Bass("TRN2", target_bir_lowering=False)
    NT = nseq * 2048

    def din(name, shape, dt=F32):
        return nc.dram_tensor(name, list(shape), dt, kind="ExternalInput").ap()

    def dscr(name, shape, dt):
        return nc.dram_tensor(name, list(shape), dt, kind="Internal").ap()

    x = din("x", [4096, 1024])
    pT = din("pT", [2, 256, 2048])
    lnp = din("lnp", [128, 6, 1024])
    bgd = din("bg", [128, 1024])
    ident = din("ident", [128, 128])
    iota = din("iota", [128, 128])
    rope = din("rope", [2, 32, 2048])
    w_in = din("w_in", [1024, 2208])
    w_kr = din("w_kr", [1024, 2 * 96])
    w_uq = din("w_uq", [384, 2 * 768])
    w_ukv = din("w_ukv", [256, 1024])
    qg = din("qg", [128, 3])
    kvg = din("kvg", [128, 2])
    nab = din("nab", [5 * 8 * 5 * 128, 128])
    w_o = din("w_o", [1024, 1024])
    w_q = din("w_q", [1024, 2048])
    skT = din("skT", [128, 256])
    u_l = din("u_l", [16384, 1024])
    v_l = din("v_l", [16384, 1024])
    w_ple = din("w_ple", [256, 1024])
    w_g = din("w_g", [512, 2048])
    out = nc.dram_tensor("out", [4096, 1024], F32, kind="ExternalOutput").ap()
    if debug:
        h1d = nc.dram_tensor("h1d", [4096, 1024], F32, kind="ExternalOutput").ap()
        catd = nc.dram_tensor("catd", [4096, 1024], F32, kind="ExternalOutput").ap()
    else:
        h1d = dscr("h1d", [4096, 1024], F32)
        catd = None
    b_win = dscr("b_win", [1024, 2208], BF16)
    b_wkr = dscr("b_wkr", [1024, 192], BF16)
    b_wuq = dscr("b_wuq", [384, 1536], BF16)
    b_wukv = dscr("b_wukv", [256, 1024], BF16)
    b_nab = dscr("b_nab", [5 * 8 * 5 * 128, 128], BF16)
    b_wo = dscr("b_wo", [1024, 1024], BF16)
    b_wq = dscr("b_wq", [1024, 2048], BF16)
    b_u = dscr("b_u", [16384, 1024], BF16)
    b_v = dscr("b_v", [16384, 1024], BF16)
    b_wple = dscr("b_wple", [256, 1024], BF16)
    b_wg = dscr("b_wg", [512, 2048], BF16)

    with ExitStack() as es:
        p = Prog(nc, es)

        uniq = [0]

        def sbuf(st, name, shape, dt):
            uniq[0] += 1
            return st.enter_context(nc.sbuf_tensor(f"{name}_{uniq[0]}", list(shape), dt))

        PS = es.enter_context(nc.psum_tensor("ps", [128, 8, 512], F32))
        PT = tl(8, "bank")

        def bank(i):
            return PS[:, i, :]

        def bank_bf(i):
            return PS[:, i, 0:256].bitcast(BF16).rearrange("p (a b) -> p a b", b=128)

        scr = {}

        def cast_dram(dst, src, rows, name, rows_per=512):
            scr[name] = []
            a = rows_per // 128
            dv = dst.rearrange("(n p a) m -> n p a m", p=128, a=a)
            sv = src.rearrange("(n p a) m -> n p a m", p=128, a=a)
            for i in range(rows // rows_per):
                t = Trk(name)
                scr[name].append(t)
                p.dma("pool", dv[i], sv[i], w=[t])

        cast_dram(b_win[:, 0:1104], w_in[:, 0:1104], 1024, "win", 512)
        win0 = scr["win"]
        cast_dram(b_win[:, 1104:2208], w_in[:, 1104:2208], 1024, "win", 512)
        scr["win"] = win0 + scr["win"]
        cast_dram(b_wkr, w_kr, 1024, "wkr", 1024)
        cast_dram(b_wuq, w_uq, 384, "wuq", 128)
        cast_dram(b_wukv, w_ukv, 256, "wukv", 256)
        cast_dram(b_nab, nab, 5 * 8 * 5 * 128, "nab", 1280)
        cast_dram(b_wo, w_o, 1024, "wo", 512)
        def cast_gen():
            for dst, src, rows, name, per in ((b_wq, w_q, 1024, "wq", 256), (b_wg, w_g, 512, "wg", 256),
                                              (b_wple, w_ple, 256, "wple", 256), (b_u, u_l, 16384, "u", 512),
                                              (b_v, v_l, 16384, "v", 512)):
                scr[name] = []
                a_ = per // 128
                dv = dst.rearrange("(n p a) m -> n p a m", p=128, a=a_)
                sv = src.rearrange("(n p a) m -> n p a m", p=128, a=a_)
                for i in range(rows // per):
                    t = Trk(name)
                    scr[name].append(t)
                    dep = yield
                    p.dma("pool", dv[i], sv[i], r=([dep] if dep is not None else []), w=[t])

        cg = [cast_gen() if do_b else None]
        if do_b:
            next(cg[0])

        def advance_casts(n, dep=None):
            for _ in range(n):
                if cg[0] is None:
                    return
                try:
                    cg[0].send(dep)
                except StopIteration:
                    cg[0] = None

        cs = es
        identf = sbuf(cs, "identf", [128, 128], F32); identf_t = Trk()
        identb = sbuf(cs, "identb", [128, 128], BF16); identb_t = Trk()
        onesb = sbuf(cs, "onesb", [128, 128], BF16); onesb_t = Trk()
        LN = {}
        p.dma("sp", identf[:], ident, w=[identf_t])

        def load_lnp(st, lo, hi):
            LN["sb"] = sbuf(st, "lnp_sb", [128, hi - lo, 1024], F32)
            LN["t"] = Trk()
            LN["base"] = lo
            p.dma("sp", LN["sb"][:], lnp[:, lo:hi, :], w=[LN["t"]])
        p.op("dve", lambda e: e.tensor_copy(out=identb[:], in_=identf[:]), r=[identf_t], w=[identb_t])
        p.op("dve", lambda e: e.memset(onesb[:], 1.0), w=[onesb_t])
        st6 = sbuf(cs, "st6", [128, 2, 6], F32); st6_t = Trk()
        mv = sbuf(cs, "mv", [128, 2], F32); mv_t = Trk()
        rstd = sbuf(cs, "rstd", [128, 1], F32); rstd_t = Trk()

        def layer_norm(src, src_t, gi, dst=None, dst_t=None):
            if dst is None:
                dst, dst_t = src, src_t
            for c in range(2):
                p.op("dve", lambda e, c=c: e.bn_stats(out=st6[:, c, :], in_=src[:, c * 512:(c + 1) * 512]),
                     r=[src_t], w=[st6_t])
            p.op("dve", lambda e: e.bn_aggr(out=mv[:], in_=st6[:].rearrange("p a b -> p (a b)")), r=[st6_t], w=[mv_t])
            p.op("act", lambda e: e.activation(out=rstd[:], in_=mv[:, 1:2], func=AF.Sqrt, bias=EPS, scale=1.0),
                 r=[mv_t], w=[rstd_t])
            p.op("dve", lambda e: e.reciprocal(out=rstd[:], in_=rstd[:]), r=[rstd_t], w=[rstd_t])
            p.op("dve", lambda e: e.tensor_scalar(out=dst, in0=src, scalar1=mv[:, 0:1], scalar2=rstd[:, 0:1],
                                                  op0=ALU.subtract, op1=ALU.mult), r=[src_t, mv_t, rstd_t], w=[dst_t])
            gi -= LN["base"]
            p.op("dve", lambda e: e.tensor_tensor(out=dst, in0=dst, in1=LN["sb"][:, gi, :], op=ALU.mult),
                 r=[dst_t, LN["t"]], w=[dst_t])
            p.op("dve", lambda e: e.tensor_tensor(out=dst, in0=dst, in1=LN["sb"][:, gi + 1, :], op=ALU.add),
                 r=[dst_t, LN["t"]], w=[dst_t])

        def transpose_to(src_bf, src_t, dst_fn, dst_t, nchunk, banks=(6, 7), eng="dve"):
            for i, c0 in enumerate(range(0, nchunk, 4)):
                b = banks[i % len(banks)]
                n = min(4, nchunk - c0)
                for k in range(n):
                    p.op("pe", lambda e, k=k, c0=c0, b=b: e.transpose(out=bank_bf(b)[:, k, :],
                                                                      in_=src_bf[:, (c0 + k) * 128:(c0 + k + 1) * 128],
                                                                      identity=identb[:]),
                         r=[src_t, identb_t], w=[PT[b]])
                if eng == "dve":
                    p.op("dve", lambda e, c0=c0, n=n, b=b: e.tensor_copy(out=dst_fn(c0, n), in_=bank_bf(b)[:, 0:n, :]),
                         r=[PT[b]], w=[dst_t])
                else:
                    p.op("act", lambda e, c0=c0, n=n, b=b: e.copy(out=dst_fn(c0, n), in_=bank_bf(b)[:, 0:n, :]),
                         r=[PT[b]], w=[dst_t])

        h1_t = tl(32, "h1d")
        phA = ExitStack()
        load_lnp(phA, 0, 4)
        for s in range(nseq):
            with ExitStack() as ss:
                cat = sbuf(ss, "cat", [128, 16, 1024], BF16)
                cat_t = tl(16, "cat")
                xt = [sbuf(ss, f"xt{i}", [128, 1024], F32) for i in range(2)]
                xt_t = tl(2, "xt")
                hb = sbuf(ss, "hb", [128, 1024], BF16); hb_t = Trk()
                h0T = sbuf(ss, "h0T", [128, 8, 512], BF16); h0T_t = Trk()
                xcnt = [0]

                def load_h0T(g):
                    for j in range(4):
                        i = xcnt[0] % 2
                        xcnt[0] += 1
                        row = s * 2048 + g * 512 + j * 128
                        p.dma("sp", xt[i][:], x[row:row + 128, :], w=[xt_t[i]])
                        layer_norm(xt[i][:], xt_t[i], 0)
                        p.op("act", lambda e, i=i: e.copy(out=hb[:], in_=xt[i][:]), r=[xt_t[i]], w=[hb_t])
                        transpose_to(hb, hb_t, lambda c0, n, j=j: h0T[:, c0:c0 + n, j * 128:(j + 1) * 128], h0T_t, 8)

                with ExitStack() as sa:
                    wna = sbuf(sa, "wna", [128, 8, 1536], BF16); wna_t = Trk()
                    qT = sbuf(sa, "qT", [128, 4, 2048], BF16); qT_t = Trk()
                    kT = sbuf(sa, "kT", [128, 4, 2048], BF16); kT_t = Trk()
                    vna = sbuf(sa, "vna", [128, 16, 8, 65], BF16); vna_t = Trk()
                    bias = [sbuf(sa, f"nbias{i}", [128, 40, 128], BF16) for i in range(2)]
                    bias_t = tl(2, "nbias")
                    PTn = [sbuf(sa, f"PTn{i}", [128, 5, 128], BF16) for i in range(2)]
                    PTn_t = tl(2, "PTn")
                    rc = sbuf(sa, "rc", [128, 8], F32); rc_t = Trk()
                    p.dma("sp", wna[:], b_win[:, 0:1536].rearrange("(k p) m -> p k m", p=128), r=scr["win"], w=[wna_t])
                    p.op("dve", lambda e: e.memset(vna[:, :, :, 64:65], 1.0), w=[vna_t])
                    for g in range(4):
                        load_h0T(g)
                        for c in range(8):
                            b = 4 + (c % 2)
                            for k in range(8):
                                p.op("pe", lambda e, c=c, k=k, b=b: e.matmul(out=bank(b), lhsT=wna[:, k, c * 128:(c + 1) * 128],
                                                                             rhs=h0T[:, k, :], start=(k == 0), stop=(k == 7)),
                                     r=[wna_t, h0T_t], w=[PT[b]])
                            if c < 4:
                                p.op("act", lambda e, c=c, b=b, g=g: e.mul(out=qT[:, c, g * 512:(g + 1) * 512], in_=bank(b), mul=0.125),
                                     r=[PT[b]], w=[qT_t])
                            else:
                                p.op("act", lambda e, c=c, b=b, g=g: e.copy(out=kT[:, c - 4, g * 512:(g + 1) * 512], in_=bank(b)),
                                     r=[PT[b]], w=[kT_t])
                        for j in range(4):
                            b = 4 + (j % 2)
                            for k in range(8):
                                p.op("pe", lambda e, j=j, k=k, b=b: e.matmul(out=bank(b), lhsT=h0T[:, k, j * 128:(j + 1) * 128],
                                                                             rhs=wna[:, k, 1024:1536], start=(k == 0), stop=(k == 7)),
                                     r=[wna_t, h0T_t], w=[PT[b]])
                            p.op("dve", lambda e, j=j, b=b, g=g: e.tensor_copy(out=vna[:, g * 4 + j, :, 0:64],
                                                                               in_=bank(b).rearrange("p (h d) -> p h d", d=64)),
                                 r=[PT[b]], w=[vna_t])
                    def na_info(ti):
                        r0 = 2 * ti
                        return {0: 0, 2: 1, 28: 3, 30: 4}.get(r0, 2), na_kt0(r0)

                    def na_bias_load(ti):
                        typ, _ = na_info(ti)
                        bi = ti % 2
                        p.dma("sp", bias[bi][:], b_nab[typ * 5120:(typ + 1) * 5120, :].rearrange("(a p) q -> p a q", p=128),
                              r=scr["nab"], w=[bias_t[bi]])

                    def na_S(step):
                        ti, h = step // 8, step % 8
                        _, kt0 = na_info(ti)
                        bi = ti % 2
                        pr, po = h // 2, (h % 2) * 64
                        sb_ = 2 * (step % 2)
                        for kc in range(5):
                            bk = sb_ + (kc // 4)
                            oap = PS[:, bk, (kc % 4) * 128:(kc % 4 + 1) * 128]
                            p.op("pe", lambda e: e.matmul(
                                out=oap, lhsT=kT[po:po + 64, pr, (kt0 + kc) * 128:(kt0 + kc + 1) * 128],
                                rhs=qT[po:po + 64, pr, ti * 128:(ti + 1) * 128], start=True, stop=False),
                                r=[kT_t, qT_t], w=[PT[bk]])
                            p.op("pe", lambda e: e.matmul(
                                out=oap, lhsT=identb[:], rhs=bias[bi][:, h * 5 + kc, :], start=False, stop=True),
                                r=[identb_t, bias_t[bi]], w=[PT[bk]])

                    na_bias_load(0)
                    na_bias_load(1)
                    na_S(0)
                    for step in range(128):
                        ti, h = step // 8, step % 8
                        _, kt0 = na_info(ti)
                        if step + 1 < 128:
                            na_S(step + 1)
                        sb_ = 2 * (step % 2)
                        pi = step % 2
                        p.op("act", lambda e: e.activation(
                            out=PTn[pi][:].rearrange("p a b -> p (a b)"),
                            in_=PS[:, sb_:sb_ + 2, :].rearrange("p a b -> p (a b)")[:, 0:640], func=AF.Exp),
                            r=[PT[sb_], PT[sb_ + 1]], w=[PTn_t[pi]])
                        ob0 = 4 + 2 * (ti % 2)
                        ob = ob0 + h // 4
                        oo = (h % 4) * 65
                        for kc in range(5):
                            p.op("pe", lambda e: e.matmul(
                                out=PS[:, ob, oo:oo + 65], lhsT=PTn[pi][:, kc, :], rhs=vna[:, kt0 + kc, h, :],
                                start=(kc == 0), stop=(kc == 4)),
                                r=[PTn_t[pi], vna_t], w=[PT[ob]])
                        if h == 7:
                            if ti + 2 < 16:
                                na_bias_load(ti + 2)
                            tick = Trk()
                            for ob in (ob0, ob0 + 1):
                                o4 = PS[:, ob, 0:260].rearrange("p (h d) -> p h d", d=65)
                                hh = (ob - ob0) * 4
                                p.op("dve", lambda e: e.reciprocal(out=rc[:, hh:hh + 4], in_=o4[:, :, 64]),
                                     r=[PT[ob]], w=[rc_t])
                                p.op("dve", lambda e: e.tensor_tensor(
                                    out=cat[:, ti, hh * 64:(hh + 4) * 64].rearrange("p (h d) -> p h d", d=64),
                                    in0=o4[:, :, 0:64], in1=rc[:, hh:hh + 4].unsqueeze(2).broadcast_to([128, 4, 64]), op=ALU.mult),
                                    r=[PT[ob], rc_t], w=[cat_t[ti], tick])
                            advance_casts(2, tick)
                p.barrier()

                with ExitStack() as sa:
                    wcq = sbuf(sa, "wcq", [128, 8, 640], BF16); wcq_t = Trk()
                    wkr = sbuf(sa, "wkr", [128, 8, 192], BF16); wkr_t = Trk()
                    wuq = sbuf(sa, "wuq", [128, 3, 1536], BF16); wuq_t = Trk()
                    wukv = sbuf(sa, "wukv", [128, 2, 1024], BF16); wukv_t = Trk()
                    qg_sb = sbuf(sa, "qg_sb", [128, 3], F32); kvg_sb = sbuf(sa, "kvg_sb", [128, 2], F32); g_t = Trk()
                    cqT = sbuf(sa, "cqT", [128, 3, 2048], BF16); cqT_t = Trk()
                    Rq = sbuf(sa, "Rq", [128, 2048], F32); Rq_t = Trk()
                    KT = sbuf(sa, "KT", [96, 8, 2048], BF16); KT_t = Trk()
                    V = sbuf(sa, "Vm", [128, 16, 8, 65], BF16); V_t = Trk()
                    Qg = sbuf(sa, "Qg", [96, 8, 512], BF16); Qg_t = Trk()
                    ckv = sbuf(sa, "ckv", [128, 2, 512], BF16); ckv_t = Trk()
                    sq = sbuf(sa, "sq", [128, 512], BF16); sq_t = Trk()
                    Rkv = sbuf(sa, "Rkv", [128, 512], F32); Rkv_t = Trk()
                    rcol = sbuf(sa, "rcol", [128, 1], F32); rcol_t = Trk()
                    tab = sbuf(sa, "tab", [96, 2, 512], F32); tab_t = Trk()
                    c1 = sbuf(sa, "c1", [96, 2, 512], F32); c1_t = Trk()
                    t1 = sbuf(sa, "t1", [96, 512], F32); t1_t = Trk()
                    t2 = sbuf(sa, "t2", [96, 512], F32); t2_t = Trk()
                    PTm = [sbuf(sa, f"PTm{i}", [128, 512], BF16) for i in range(2)]
                    PTm_t = tl(2, "PTm")
                    rc4 = sbuf(sa, "rc4", [128, 4], F32); rc4_t = Trk()
                    p.dma("sp", wcq[:], b_win[:, 1536:2176].rearrange("(k p) m -> p k m", p=128), r=scr["win"], w=[wcq_t])
                    p.dma("sp", wkr[:], b_wkr.rearrange("(k p) m -> p k m", p=128), r=scr["wkr"], w=[wkr_t])
                    p.dma("sp", wuq[:], b_wuq.rearrange("(k p) m -> p k m", p=128), r=scr["wuq"], w=[wuq_t])
                    p.dma("sp", wukv[:], b_wukv.rearrange("(k p) m -> p k m", p=128), r=scr["wukv"], w=[wukv_t])
                    p.dma("sp", qg_sb[:], qg, w=[g_t])
                    p.dma("sp", kvg_sb[:], kvg, w=[g_t])
                    for c in range(3):
                        p.op("dve", lambda e, c=c: e.tensor_scalar(out=wuq[:, c, :], in0=wuq[:, c, :], scalar1=qg_sb[:, c:c + 1],
                                                                   scalar2=None, op0=ALU.mult), r=[wuq_t, g_t], w=[wuq_t])
                    for c in range(2):
                        p.op("dve", lambda e, c=c: e.tensor_scalar(out=wukv[:, c, :], in0=wukv[:, c, :], scalar1=kvg_sb[:, c:c + 1],
                                                                   scalar2=None, op0=ALU.mult), r=[wukv_t, g_t], w=[wukv_t])
                    p.op("dve", lambda e: e.memset(V[:, :, :, 64:65], 1.0), w=[V_t])

                    def rms_bcast(src_fn, nchunk, nfeat, dst, dst_t, extra):
                        for c in range(nchunk):
                            p.op("dve", lambda e, c=c: e.tensor_tensor(out=sq[:], in0=src_fn(c), in1=src_fn(c), op=ALU.mult),
                                 r=[cqT_t, ckv_t], w=[sq_t])
                            p.op("pe", lambda e, c=c: e.matmul(out=bank(3), lhsT=onesb[:], rhs=sq[:], start=(c == 0),
                                                               stop=(c == nchunk - 1)), r=[onesb_t, sq_t], w=[PT[3]])
                        p.op("act", lambda e: e.activation(out=dst, in_=bank(3), func=AF.Sqrt, bias=EPS, scale=1.0 / nfeat),
                             r=[PT[3]], w=[dst_t])
                        p.op("dve", lambda e: e.reciprocal(out=dst, in_=dst), r=[dst_t], w=[dst_t])
                        if extra != 1.0:
                            p.op("dve", lambda e: e.tensor_scalar(out=dst, in0=dst, scalar1=extra, scalar2=None, op0=ALU.mult),
                                 r=[dst_t], w=[dst_t])

                    def load_tab(g):
                        p.dma("sp", tab[64:96, :, :], rope[:, :, g * 512:(g + 1) * 512].rearrange("a f t -> f a t"), w=[tab_t])

                    for g in range(4):
                        gs = slice(g * 512, (g + 1) * 512)
                        load_h0T(g)
                        load_tab(g)
                        for c in range(5):
                            b = 4 + (c % 2)
                            for k in range(8):
                                p.op("pe", lambda e, c=c, k=k, b=b: e.matmul(out=bank(b), lhsT=wcq[:, k, c * 128:(c + 1) * 128],
                                                                             rhs=h0T[:, k, :], start=(k == 0), stop=(k == 7)),
                                     r=[wcq_t, h0T_t], w=[PT[b]])
                            if c < 3:
                                p.op("act", lambda e, c=c, b=b: e.copy(out=cqT[:, c, gs], in_=bank(b)), r=[PT[b]], w=[cqT_t])
                            else:
                                p.op("act", lambda e, c=c, b=b: e.copy(out=ckv[:, c - 3, :], in_=bank(b)), r=[PT[b]], w=[ckv_t])
                        rms_bcast(lambda c: cqT[:, c, gs], 3, 384.0, Rq[:, gs], Rq_t, 96.0 ** -0.5)
                        rms_bcast(lambda c: ckv[:, c, :], 2, 256.0, Rkv[:], Rkv_t, 1.0)
                        for h in range(8):
                            b = 4 + (h % 2)
                            for c in range(2):
                                p.op("pe", lambda e, h=h, c=c, b=b: e.matmul(out=PS[0:64, b, :], lhsT=wukv[:, c, h * 128:h * 128 + 64],
                                                                             rhs=ckv[:, c, :], start=(c == 0), stop=(c == 1)),
                                     r=[wukv_t, ckv_t], w=[PT[b]])
                            p.op("dve", lambda e, h=h, b=b: e.tensor_tensor(out=KT[0:64, h, gs], in0=PS[0:64, b, :], in1=Rkv[0:64, :],
                                                                            op=ALU.mult), r=[PT[b], Rkv_t], w=[KT_t])
                        for v_ in range(2):
                            b = 4 + v_
                            for k in range(8):
                                p.op("pe", lambda e, v_=v_, k=k, b=b: e.matmul(out=PS[0:96, b, :], lhsT=wkr[:, k, v_ * 96:(v_ + 1) * 96],
                                                                               rhs=h0T[:, k, :], start=(k == 0), stop=(k == 7)),
                                     r=[wkr_t, h0T_t], w=[PT[b]])
                        p.op("dve", lambda e: e.tensor_tensor(out=t1[64:96, :], in0=PS[64:96, 4, :], in1=tab[64:96, 0, :], op=ALU.mult),
                             r=[PT[4], tab_t], w=[t1_t])
                        p.op("dve", lambda e: e.tensor_tensor(out=t2[64:96, :], in0=PS[64:96, 5, :], in1=tab[64:96, 1, :], op=ALU.mult),
                             r=[PT[5], tab_t], w=[t2_t])
                        p.op("dve", lambda e: e.tensor_tensor(out=t1[64:96, :], in0=t1[64:96, :], in1=t2[64:96, :], op=ALU.add),
                             r=[t1_t, t2_t], w=[t1_t])
                        for h in range(8):
                            p.op("act", lambda e, h=h: e.copy(out=KT[64:96, h, gs], in_=t1[64:96, :]), r=[t1_t], w=[KT_t])
                        for j in range(4):
                            p.op("pe", lambda e, j=j: e.transpose(out=PS[:, 6, 0:128], in_=Rkv[:, j * 128:(j + 1) * 128], identity=identf[:]),
                                 r=[Rkv_t, identf_t], w=[PT[6]])
                            p.op("dve", lambda e: e.tensor_copy(out=rcol[:], in_=PS[:, 6, 0:1]), r=[PT[6]], w=[rcol_t])
                            b = 4 + (j % 2)
                            for c in range(2):
                                p.op("pe", lambda e, j=j, c=c, b=b: e.matmul(
                                    out=bank(b), lhsT=ckv[:, c, j * 128:(j + 1) * 128],
                                    rhs=wukv[:, c, :].rearrange("p (h d) -> p h d", d=128)[:, :, 64:128],
                                    start=(c == 0), stop=(c == 1)), r=[wukv_t, ckv_t], w=[PT[b]])
                            p.op("dve", lambda e, j=j, b=b, g=g: e.tensor_scalar(
                                out=V[:, g * 4 + j, :, 0:64], in0=bank(b).rearrange("p (h d) -> p h d", d=64),
                                scalar1=rcol[:, 0:1], scalar2=None, op0=ALU.mult), r=[PT[b], rcol_t], w=[V_t])
                    for g in range(4):
                        gs = slice(g * 512, (g + 1) * 512)
                        load_tab(g)
                        for a in range(2):
                            p.op("dve", lambda e, a=a: e.tensor_tensor(out=c1[64:96, a, :], in0=tab[64:96, a, :], in1=Rq[64:96, gs],
                                                                       op=ALU.mult), r=[tab_t, Rq_t], w=[c1_t])
                        for h in range(8):
                            for v_ in range(2):
                                b = 6 + v_
                                for c in range(3):
                                    p.op("pe", lambda e, h=h, v_=v_, c=c, b=b: e.matmul(
                                        out=PS[0:96, b, :], lhsT=wuq[:, c, v_ * 768 + h * 96:v_ * 768 + (h + 1) * 96],
                                        rhs=cqT[:, c, gs], start=(c == 0), stop=(c == 2)), r=[wuq_t, cqT_t], w=[PT[b]])
                            p.op("dve", lambda e, h=h: e.tensor_tensor(out=Qg[0:64, h, :], in0=PS[0:64, 6, :], in1=Rq[0:64, gs], op=ALU.mult),
                                 r=[PT[6], Rq_t], w=[Qg_t])
                            p.op("dve", lambda e: e.tensor_tensor(out=t1[64:96, :], in0=PS[64:96, 6, :], in1=c1[64:96, 0, :], op=ALU.mult),
                                 r=[PT[6], c1_t], w=[t1_t])
                            p.op("dve", lambda e: e.tensor_tensor(out=t2[64:96, :], in0=PS[64:96, 7, :], in1=c1[64:96, 1, :], op=ALU.mult),
                                 r=[PT[7], c1_t], w=[t2_t])
                            p.op("dve", lambda e, h=h: e.tensor_tensor(out=Qg[64:96, h, :], in0=t1[64:96, :], in1=t2[64:96, :], op=ALU.add),
                                 r=[t1_t, t2_t], w=[Qg_t])
                        def mla_S(st_):
                            h, kc = st_ // 16, st_ % 16
                            sbk = 4 + (st_ % 2)
                            p.op("pe", lambda e: e.matmul(
                                out=bank(sbk), lhsT=KT[0:96, h, kc * 128:(kc + 1) * 128], rhs=Qg[0:96, h, :], start=True, stop=True),
                                r=[KT_t, Qg_t], w=[PT[sbk]])

                        mla_S(0)
                        for st_ in range(128):
                            h, kc = st_ // 16, st_ % 16
                            if st_ + 1 < 128:
                                mla_S(st_ + 1)
                            sbk = 4 + (st_ % 2)
                            pi = st_ % 2
                            p.op("act", lambda e: e.activation(out=PTm[pi][:], in_=bank(sbk), func=AF.Exp),
                                 r=[PT[sbk]], w=[PTm_t[pi]])
                            for j in range(4):
                                p.op("pe", lambda e: e.matmul(
                                    out=PS[:, j, 0:65], lhsT=PTm[pi][:, j * 128:(j + 1) * 128], rhs=V[:, kc, h, :],
                                    start=(kc == 0), stop=(kc == 15)), r=[PTm_t[pi], V_t], w=[PT[j]])
                            if kc == 15:
                                tick = Trk()
                                for j in range(4):
                                    p.op("dve", lambda e: e.reciprocal(out=rc4[:, j:j + 1], in_=PS[:, j, 64:65]), r=[PT[j]], w=[rc4_t])
                                    p.op("dve", lambda e: e.tensor_scalar(
                                        out=cat[:, g * 4 + j, 512 + h * 64:512 + (h + 1) * 64], in0=PS[:, j, 0:64],
                                        scalar1=rc4[:, j:j + 1], scalar2=None, op0=ALU.mult), r=[PT[j], rc4_t], w=[cat_t[g * 4 + j], tick])
                                advance_casts(1, tick)
                p.barrier()

                with ExitStack() as sa:
                    wo = sbuf(sa, "wo", [128, 8, 1024], BF16); wo_t = Trk()
                    cT = sbuf(sa, "cT", [128, 8, 128], BF16); cT_t = Trk()
                    yt = [sbuf(sa, f"yt{i}", [128, 1024], F32) for i in range(2)]
                    yt_t = tl(2, "yt")
                    catf = sbuf(sa, "catf", [128, 1024], F32); catf_t = Trk()
                    p.dma("sp", wo[:], b_wo.rearrange("(k p) m -> p k m", p=128), r=scr["wo"], w=[wo_t])
                    for ti in range(16):
                        i = ti % 2
                        row = s * 2048 + ti * 128
                        p.dma("sp", xt[i][:], x[row:row + 128, :], w=[xt_t[i]])
                        layer_norm(xt[i][:], xt_t[i], 0)
                        if debug:
                            p.op("act", lambda e, ti=ti: e.copy(out=catf[:], in_=cat[:, ti, :]), r=[cat_t[ti]], w=[catf_t])
                            p.dma("sp", catd[row:row + 128, :], catf[:], r=[catf_t])
                        transpose_to(cat[:, ti, :], cat_t[ti], lambda c0, n: cT[:, c0:c0 + n, :], cT_t, 8)
                        for half in range(2):
                            b = 4 + half
                            for k in range(8):
                                p.op("pe", lambda e, k=k, half=half, b=b: e.matmul(out=bank(b), lhsT=cT[:, k, :],
                                                                                   rhs=wo[:, k, half * 512:(half + 1) * 512],
                                                                                   start=(k == 0), stop=(k == 7)),
                                     r=[cT_t, wo_t], w=[PT[b]])
                            p.op("dve", lambda e, i=i, half=half, b=b: e.scalar_tensor_tensor(
                                out=yt[i][:, half * 512:(half + 1) * 512], in0=xt[i][:, half * 512:(half + 1) * 512], scalar=ALPHA,
                                in1=bank(b), op0=ALU.mult, op1=ALU.add), r=[xt_t[i], PT[b]], w=[yt_t[i]])
                        layer_norm(yt[i][:], yt_t[i], 2)
                        p.dma("pool", h1d[row:row + 128, :], yt[i][:], r=[yt_t[i]], w=[h1_t[row // 128]])
                p.barrier()

        phA.close()
        advance_casts(1000)
        if do_b:
            load_lnp(es, 4, 6)
            phase_b(nc, p, es, sbuf, PS, PT, bank, bank_bf, NT, dict(
                h1d=h1d, h1_t=h1_t, out=out, pT=pT, bgd=bgd, iota=iota, skT=skT, b_wq=b_wq, b_u=b_u, b_v=b_v,
                b_wple=b_wple, b_wg=b_wg, scr=scr, identf=identf, identf_t=identf_t, identb=identb, identb_t=identb_t,
                layer_norm=layer_norm, transpose_to=transpose_to, LN=LN,
                lnscr=(st6, st6_t, mv, mv_t, rstd, rstd_t)))
        p.finish()
    return nc


def na_kt0(r0):
    rs = min(max(r0 - 4, 0), 24)
    bs = min(rs, 23)
    return min(bs // 2, 11)


def phase_b(nc, p, es, sbuf, PS, PT, bank, bank_bf, NT, a):
    h1d, h1_t, out, pT = a["h1d"], a["h1_t"], a["out"], a["pT"]
    scr = a["scr"]
    identf, identf_t = a["identf"], a["identf_t"]
    layer_norm, transpose_to = a["layer_norm"], a["transpose_to"]
    NG = NT // 256
    with ExitStack() as sb_:
        iota_f = sbuf(sb_, "iota_f", [128, 128], F32); iota_b = sbuf(sb_, "iota_b", [128, 128], BF16); iota_t = Trk()
        skTf = sbuf(sb_, "skTf", [128, 256], F32); skTb = sbuf(sb_, "skTb", [128, 256], BF16); sk_t = Trk()
        bg_sb = sbuf(sb_, "bg_sb", [128, 1024], F32); bg_t = Trk()
        wple = sbuf(sb_, "wple", [128, 2, 1024], BF16); wple_t = Trk()
        Gall = sbuf(sb_, "Gall", [128, 256, 128], BF16); G_t = Trk()
        h1t_ = [sbuf(sb_, f"h1t{i}", [128, 1024], F32) for i in range(2)]
        h1t = [h1t_, h1t_]
        h1tt_ = tl(2)
        h1t_t = [h1tt_, h1tt_]
        hst = sbuf(sb_, "hst", [128, 1024], F32); hst_t = Trk()
        h1b = sbuf(sb_, "h1b", [128, 1024], BF16); h1b_t = Trk()
        h1T = [sbuf(sb_, f"h1T{q}", [128, 8, 256], BF16) for q in range(3)]; h1T_t = tl(3)
        qTp = sbuf(sb_, "qTp", [128, 16, 256], BF16); qTp_t = Trk()
        wqs = [sbuf(sb_, f"wqs{i}", [128, 8, 256], BF16) for i in range(3)]; wqs_t = tl(3)
        ub = [sbuf(sb_, f"ub{i}", [128, 4096], BF16) for i in range(2)]; ub_t = tl(2)
        vb = [sbuf(sb_, f"vb{i}", [128, 4096], BF16) for i in range(2)]; vb_t = tl(2)
        scs = [sbuf(sb_, f"sc{j}", [128, 16, 128], F32) for j in range(2)]; scs_t = tl(2)
        v16 = sbuf(sb_, "v16", [128, 16, 16], F32); v16_t = Trk()
        ix = sbuf(sb_, "ix", [128, 16, 16], U32); ix_t = Trk()
        ixf = sbuf(sb_, "ixf", [128, 16, 16], F32); ixf_t = Trk()
        cv = sbuf(sb_, "cv", [128, 8, 16], F32); cv_t = Trk()
        ci = sbuf(sb_, "ci", [128, 8, 16], U32); ci_t = Trk()
        ai = sbuf(sb_, "ai", [128, 2, 128], U32); ai_t = Trk()
        af = sbuf(sb_, "af", [128, 2, 8, 16], F32); af_t = Trk()
        EEs = [sbuf(sb_, f"EE{j}", [128, 3, 128], F32) for j in range(2)]; EEs_t = tl(2)
        gs = sbuf(sb_, "gs", [128, 8], F32); gs_t = Trk()
        ETs = [sbuf(sb_, f"ETs{j}", [128, 3, 128], BF16) for j in range(2)]; ETs_t = tl(2)
        OH2 = [sbuf(sb_, f"OH2_{i}", [128, 128], BF16) for i in range(4)]; OH2_t = tl(4)
        OH1 = [sbuf(sb_, f"OH1_{i}", [128, 128], BF16) for i in range(4)]; OH1_t = tl(4)
        gl = [sbuf(sb_, f"gl{i}", [128, 256], BF16) for i in range(2)]; gl_t = tl(2)
        Am = [sbuf(sb_, f"Am{i}", [128, 256], BF16) for i in range(2)]; Am_t = tl(2)
        pTb1 = sbuf(sb_, "pTb", [128, 2, 256], BF16); pTb1_t = Trk()
        pTb = [pTb1, pTb1]; pTb_t = [pTb1_t, pTb1_t]
        gate = [sbuf(sb_, f"gate{i}", [128, 512], F32) for i in range(2)]; gate_t = tl(2)

        p.dma("sp", iota_f[:], a["iota"], w=[iota_t])
        p.op("dve", lambda e: e.tensor_copy(out=iota_b[:], in_=iota_f[:]), r=[iota_t], w=[iota_t])
        p.dma("sp", skTf[:], a["skT"], w=[sk_t])
        p.op("dve", lambda e: e.tensor_copy(out=skTb[:], in_=skTf[:]), r=[sk_t], w=[sk_t])
        p.dma("sp", bg_sb[:], a["bgd"], w=[bg_t])
        p.dma("sp", wple[:], a["b_wple"].rearrange("(k p) m -> p k m", p=128), r=scr["wple"], w=[wple_t])

        def b1_front(gi):
            row = gi * 256
            seq, tok0 = row // 2048, row % 2048
            q_ = gi % 2

            def wq_load(blk):
                p.dma("sp", wqs[blk % 3][:].rearrange("p k m -> p (k m)"), a["b_wq"][blk * 128:(blk + 1) * 128, :],
                      r=scr["wq"], w=[wqs_t[blk % 3]])

            wq_load(0)
            wq_load(1)
            for j in range(2):
                p.dma("sp", hst[:], h1d[row + j * 128:row + (j + 1) * 128, :], r=[h1_t[row // 128 + j]], w=[hst_t])
                yield
                yield
                p.op("act", lambda e: e.copy(out=h1b[:], in_=hst[:]), r=[hst_t], w=[h1b_t])
                yield
                for half in range(2):
                    bb = 6 + half
                    for k in range(4):
                        kk = half * 4 + k
                        p.op("pe", lambda e: e.transpose(out=bank_bf(bb)[:, k, :], in_=h1b[:, kk * 128:(kk + 1) * 128],
                                                         identity=a["identb"][:]), r=[h1b_t, a["identb_t"]], w=[PT[bb]])
                yield
                for half in range(2):
                    bb = 6 + half
                    p.op("act", lambda e: e.copy(out=h1T[gi % 3][:, half * 4:(half + 1) * 4, j * 128:(j + 1) * 128],
                                                 in_=bank_bf(bb)[:, 0:4, :]), r=[PT[bb]], w=[h1T_t[gi % 3]])
                yield
            pend = None
            for blk in range(8):
                i = blk % 3
                if blk + 2 < 8:
                    wq_load(blk + 2)
                for cc in range(2):
                    hc = blk * 2 + cc
                    b = 6 + hc % 2
                    for k in range(8):
                        p.op("pe", lambda e: e.matmul(out=PS[:, b, 0:256], lhsT=wqs[i][:, k, cc * 128:(cc + 1) * 128],
                                                      rhs=h1T[gi % 3][:, k, :], start=(k == 0), stop=(k == 7)),
                             r=[wqs_t[i], h1T_t[gi % 3]], w=[PT[b]])
                    if pend is not None:
                        hc0, b0 = pend
                        p.op("act", lambda e: e.copy(out=qTp[:, hc0, :], in_=PS[:, b0, 0:256]), r=[PT[b0]], w=[qTp_t])
                    pend = (hc, b)
                    yield
            hc0, b0 = pend
            p.op("act", lambda e: e.copy(out=qTp[:, hc0, :], in_=PS[:, b0, 0:256]), r=[PT[b0]], w=[qTp_t])
            yield
            pend = None
            for j in range(2):
                sc, sc_t = scs[j], scs_t[j]
                for qd in range(4):
                    b = 6 + qd % 2
                    for u in range(4):
                        hc = qd * 4 + u
                        half = hc % 2
                        p.op("pe", lambda e: e.matmul(out=PS[:, b, u * 128:(u + 1) * 128], lhsT=qTp[:, hc, j * 128:(j + 1) * 128],
                                                      rhs=skTb[:, half * 128:(half + 1) * 128], start=True, stop=True),
                             r=[qTp_t, sk_t], w=[PT[b]])
                    if pend is not None:
                        sc0, sct0, qd0, b0 = pend
                        p.op("act", lambda e: e.copy(out=sc0[:, 4 * qd0:4 * qd0 + 4, :].rearrange("p a b -> p (a b)"), in_=bank(b0)),
                             r=[PT[b0]], w=[sct0])
                    pend = (sc, sc_t, qd, b)
                    yield
            sc0, sct0, qd0, b0 = pend
            p.op("act", lambda e: e.copy(out=sc0[:, 4 * qd0:4 * qd0 + 4, :].rearrange("p a b -> p (a b)"), in_=bank(b0)),
                 r=[PT[b0]], w=[sct0])
            yield

        def b1_chains(gi):
            for j in range(2):
                sc, sc_t = scs[j], scs_t[j]
                oh, oh_t = sc[:].rearrange("p (h a) (b c) -> p h (a b) c", a=2, c=16), sc_t
                EE, EE_t = EEs[j], EEs_t[j]
                for gq in range(16):
                    for rnd in range(2):
                        vs = v16[:, gq, rnd * 8:(rnd + 1) * 8]
                        p.op("dve", lambda e: e.max(out=vs, in_=sc[:, gq, :]), r=[sc_t], w=[v16_t])
                        p.op("dve", lambda e: e.max_index(out=ix[:, gq, rnd * 8:(rnd + 1) * 8], in_max=vs, in_values=sc[:, gq, :]),
                             r=[sc_t, v16_t], w=[ix_t])
                        if rnd == 0:
                            p.op("dve", lambda e: e.match_replace(out=sc[:, gq, :], in_to_replace=vs, in_values=sc[:, gq, :],
                                                                  imm_value=-1e30), r=[v16_t, sc_t], w=[sc_t])
                    yield
                p.op("dve", lambda e: e.tensor_copy(out=ixf[:], in_=ix[:]), r=[ix_t], w=[ixf_t])
                v4 = v16[:].rearrange("p (h two) k -> p h two k", two=2)
                ix4 = ixf[:].rearrange("p (h two) k -> p h two k", two=2)
                cand4 = sc[:].rearrange("p (h a) (b c) -> p h (a b) c", a=2, c=16)
                candf = sc[:].rearrange("p (h a) b -> p h (a b)", a=2)
                p.op("dve", lambda e: e.tensor_tensor(out=cand4, in0=v4[:, :, 0, :].unsqueeze(3).broadcast_to([128, 8, 16, 16]),
                                                      in1=v4[:, :, 1, :].unsqueeze(2).broadcast_to([128, 8, 16, 16]), op=ALU.add),
                     r=[v16_t, sc_t], w=[sc_t])
                yield
                for h in range(8):
                    for rnd in range(2):
                        vs = cv[:, h, rnd * 8:(rnd + 1) * 8]
                        p.op("dve", lambda e: e.max(out=vs, in_=candf[:, h, :]), r=[sc_t], w=[cv_t])
                        p.op("dve", lambda e: e.max_index(out=ci[:, h, rnd * 8:(rnd + 1) * 8], in_max=vs, in_values=candf[:, h, :]),
                             r=[sc_t, cv_t], w=[ci_t])
                        if rnd == 0:
                            p.op("dve", lambda e: e.match_replace(out=candf[:, h, :], in_to_replace=vs, in_values=candf[:, h, :],
                                                                  imm_value=-1e30), r=[cv_t, sc_t], w=[sc_t])
                    yield
                gg3 = EE[:, 2, :].rearrange("p (h k) -> p h k", k=16)
                p.op("dve", lambda e: e.tensor_tensor(out=gg3, in0=cv[:], in1=cv[:, :, 0:1].broadcast_to([128, 8, 16]), op=ALU.subtract),
                     r=[cv_t], w=[EE_t])
                cif = ci[:].rearrange("p h k -> p (h k)")
                p.op("dve", lambda e: e.tensor_single_scalar(out=ai[:, 0, :], in_=cif, scalar=4, op=ALU.logical_shift_right),
                     r=[ci_t], w=[ai_t])
                p.op("dve", lambda e: e.tensor_single_scalar(out=ai[:, 1, :], in_=cif, scalar=15, op=ALU.bitwise_and),
                     r=[ci_t], w=[ai_t])
                p.op("dve", lambda e: e.tensor_copy(out=af[:].rearrange("p a h k -> p a (h k)"), in_=ai[:]), r=[ai_t], w=[af_t])
                yield
                for w_ in range(2):
                    p.op("dve", lambda e: e.tensor_tensor(
                        out=oh, in0=iota_f[:, 0:16].unsqueeze(1).unsqueeze(1).broadcast_to([128, 8, 16, 16]),
                        in1=af[:, w_, :, :].unsqueeze(3).broadcast_to([128, 8, 16, 16]), op=ALU.is_equal),
                        r=[iota_t, af_t], w=[oh_t])
                    yield
                    p.op("dve", lambda e: e.tensor_tensor(
                        out=oh, in0=oh, in1=ix4[:, :, w_, :].unsqueeze(2).broadcast_to([128, 8, 16, 16]), op=ALU.mult),
                        r=[oh_t, ixf_t], w=[oh_t])
                    yield
                    p.op("dve", lambda e: e.tensor_reduce(out=EE[:, 1 - w_, :].rearrange("p (h k) -> p h k", k=16), in_=oh,
                                                          axis=AX.X, op=ALU.add), r=[oh_t], w=[EE_t])
                    yield

        def b1_tail(gi):
            for _ in range(6):
                yield
            for j in range(2):
                EE, EE_t = EEs[j], EEs_t[j]
                p.op("act", lambda e: e.activation(out=EE[:, 2, :], in_=EE[:, 2, :], func=AF.Exp), r=[EE_t], w=[EE_t])
            yield
            yield
            for j in range(2):
                EE, EE_t = EEs[j], EEs_t[j]
                gg3 = EE[:, 2, :].rearrange("p (h k) -> p h k", k=16)
                p.op("dve", lambda e: e.tensor_reduce(out=gs[:], in_=gg3, axis=AX.X, op=ALU.add), r=[EE_t], w=[gs_t])
                p.op("dve", lambda e: e.reciprocal(out=gs[:], in_=gs[:]), r=[gs_t], w=[gs_t])
                p.op("dve", lambda e: e.tensor_tensor(out=gg3, in0=gg3, in1=gs[:].unsqueeze(2).broadcast_to([128, 8, 16]), op=ALU.mult),
                     r=[EE_t, gs_t], w=[EE_t])
            for _ in range(4):
                yield
            for j in range(2):
                EE, EE_t = EEs[j], EEs_t[j]
                bb = 6 + j
                for q in range(3):
                    p.op("pe", lambda e: e.transpose(out=PS[:, bb, q * 128:(q + 1) * 128], in_=EE[:, q, :], identity=identf[:]),
                         r=[EE_t, identf_t], w=[PT[bb]])
            yield
            yield
            for j in range(2):
                bb = 6 + j
                p.op("act", lambda e: e.copy(out=ETs[j][:].rearrange("p a b -> p (a b)"), in_=PS[:, bb, 0:384]), r=[PT[bb]], w=[ETs_t[j]])
            yield

        def b1e(gi):
            for j in range(2):
                for tt in range(128):
                    r4 = tt % 4
                    b = 4 + (tt // 4) % 2
                    p.op("dve", lambda e: e.tensor_scalar(out=OH2[r4][:], in0=iota_b[:], scalar1=ETs[j][:, 0, tt:tt + 1],
                                                          scalar2=None, op0=ALU.is_equal), r=[iota_t, ETs_t[j]], w=[OH2_t[r4]])
                    p.op("dve", lambda e: e.tensor_scalar(out=OH1[r4][:], in0=iota_b[:], scalar1=ETs[j][:, 1, tt:tt + 1],
                                                          scalar2=ETs[j][:, 2, tt:tt + 1], op0=ALU.is_equal, op1=ALU.mult),
                         r=[iota_t, ETs_t[j]], w=[OH1_t[r4]])
                    p.op("pe", lambda e: e.matmul(out=PS[:, b, r4 * 128:(r4 + 1) * 128], lhsT=OH2[r4][:], rhs=OH1[r4][:],
                                                  start=True, stop=True), r=[OH2_t[r4], OH1_t[r4]], w=[PT[b]])
                    if r4 == 3:
                        t0 = j * 128 + tt - 3
                        p.op("act", lambda e: e.copy(out=Gall[:, t0:t0 + 4, :].rearrange("p a b -> p (a b)"), in_=bank(b)),
                             r=[PT[b]], w=[G_t])
                        yield

        def b2_load(cb):
            i = cb % 2
            uv = ub[i][:].rearrange("p (c f) -> p c f", f=1024)
            vv = vb[i][:].rearrange("p (c f) -> p c f", f=1024)
            p.dma("sp", uv, a["b_u"][cb * 512:(cb + 1) * 512, :].rearrange("(c p) f -> p c f", p=128), r=scr["u"], w=[ub_t[i]])
            p.dma("sp", vv, a["b_v"][cb * 512:(cb + 1) * 512, :].rearrange("(c p) f -> p c f", p=128), r=scr["v"], w=[vb_t[i]])

        def b2(gi, preloaded):
            q_ = gi % 2

            def U(c):
                i = (c // 4) % 2
                uv = ub[i][:].rearrange("p (c f) -> p c f", f=1024)
                hb_ = 4 + c % 2
                for k in range(8):
                    p.op("pe", lambda e: e.matmul(out=PS[:, hb_, 0:256], lhsT=uv[:, c % 4, k * 128:(k + 1) * 128],
                                                  rhs=h1T[gi % 3][:, k, :], start=(k == 0), stop=(k == 7)),
                         r=[ub_t[i], h1T_t[gi % 3]], w=[PT[hb_]])

            if not preloaded:
                b2_load(0)
                b2_load(1)
            U(0)
            for c in range(128):
                if c + 1 < 128:
                    U(c + 1)
                i = (c // 4) % 2
                vv = vb[i][:].rearrange("p (c f) -> p c f", f=1024)
                hb_ = 4 + c % 2
                ci_ = c % 2
                p.op("act", lambda e: e.activation(out=gl[ci_][:], in_=PS[:, hb_, 0:256], func=AF.Gelu),
                     r=[PT[hb_]], w=[gl_t[ci_]])
                p.op("pool", lambda e: e.tensor_tensor(out=Am[ci_][:], in0=gl[ci_][:], in1=Gall[:, :, c], op=ALU.mult),
                     r=[gl_t[ci_], G_t], w=[Am_t[ci_]])
                for j in range(2):
                    for half in range(2):
                        ob = j * 2 + half
                        p.op("pe", lambda e: e.matmul(
                            out=bank(ob), lhsT=Am[ci_][:, j * 128:(j + 1) * 128], rhs=vv[:, c % 4, half * 512:(half + 1) * 512],
                            start=(c == 0), stop=(c == 127)), r=[Am_t[ci_], vb_t[i]], w=[PT[ob]])
                if c % 4 == 3:
                    if c // 4 + 2 < 32:
                        b2_load(c // 4 + 2)
                    elif gi + 1 < NG:
                        b2_load(c // 4 + 2 - 32)
                yield

        def b3_loads(gi):
            row = gi * 256
            seq, tok0 = row // 2048, row % 2048
            for j in range(2):
                p.dma("sp", h1t_[j][:], h1d[row + j * 128:row + (j + 1) * 128, :], r=[h1_t[row // 128 + j]], w=[h1tt_[j]])
            p.dma("pool", pTb1[:], pT[seq, :, tok0:tok0 + 256].rearrange("(k p) t -> p k t", p=128), w=[pTb1_t])
            yield

        def b3a(gi):
            q_ = gi % 2
            for j in range(2):
                for half in range(2):
                    hs = slice(half * 512, (half + 1) * 512)
                    ob = j * 2 + half
                    p.op("dve", lambda e: e.scalar_tensor_tensor(out=h1t[q_][j][:, hs], in0=h1t[q_][j][:, hs], scalar=ALPHA, in1=bank(ob),
                                                                 op0=ALU.mult, op1=ALU.add), r=[h1t_t[q_][j], PT[ob]], w=[h1t_t[q_][j]])

        def b3b(gi):
            row = gi * 256
            q_ = gi % 2

            def wg_load(qd):
                p.dma("sp", wqs[qd % 3][:].rearrange("p k m -> p (k m)"), a["b_wg"][qd * 128:(qd + 1) * 128, :],
                      r=scr["wg"], w=[wqs_t[qd % 3]])

            for qd in range(3):
                wg_load(qd)
            yield
            yield
            for qd in range(4):
                cs = slice(qd * 256, (qd + 1) * 256)
                wv = wqs[qd % 3]
                for j in range(2):
                    for k in range(8):
                        p.op("pe", lambda e: e.matmul(out=PS[:, 6, 0:256], lhsT=h1T[gi % 3][:, k, j * 128:(j + 1) * 128], rhs=wv[:, k, :],
                                                      start=(k == 0), stop=(k == 7)), r=[wqs_t[qd % 3], h1T_t[gi % 3]], w=[PT[6]])
                    for k in range(2):
                        p.op("pe", lambda e: e.matmul(out=PS[:, 7, 0:256], lhsT=pTb[q_][:, k, j * 128:(j + 1) * 128], rhs=wple[:, k, cs],
                                                      start=(k == 0), stop=(k == 1)), r=[pTb_t[q_], wple_t], w=[PT[7]])
                    yield
                    p.op("dve", lambda e: e.tensor_tensor(out=gate[j][:, 0:256], in0=PS[:, 6, 0:256], in1=bg_sb[:, cs], op=ALU.add),
                         r=[PT[6], bg_t], w=[gate_t[j]])
                    yield
                    p.op("act", lambda e: e.activation(out=gate[j][:, 0:256], in_=gate[j][:, 0:256], func=AF.Sigmoid),
                         r=[gate_t[j]], w=[gate_t[j]])
                    yield
                    p.op("dve", lambda e: e.tensor_tensor(out=gate[j][:, 0:256], in0=gate[j][:, 0:256], in1=PS[:, 7, 0:256], op=ALU.mult),
                         r=[gate_t[j], PT[7]], w=[gate_t[j]])
                    p.op("dve", lambda e: e.tensor_tensor(out=h1t[q_][j][:, cs], in0=h1t[q_][j][:, cs], in1=gate[j][:, 0:256], op=ALU.add),
                         r=[h1t_t[q_][j], gate_t[j]], w=[h1t_t[q_][j]])
                if qd == 0:
                    wg_load(3)
            yield
            for j in range(2):
                src, src_t = h1t[q_][j][:], h1t_t[q_][j]
                st6, st6_t, mv, mv_t, rstd, rstd_t = a["lnscr"]
                for c in range(2):
                    p.op("dve", lambda e: e.bn_stats(out=st6[:, c, :], in_=src[:, c * 512:(c + 1) * 512]), r=[src_t], w=[st6_t])
                p.op("dve", lambda e: e.bn_aggr(out=mv[:], in_=st6[:].rearrange("p a b -> p (a b)")), r=[st6_t], w=[mv_t])
                yield
                yield
                p.op("act", lambda e: e.activation(out=rstd[:], in_=mv[:, 1:2], func=AF.Sqrt, bias=EPS, scale=1.0), r=[mv_t], w=[rstd_t])
                yield
                LN = a["LN"]
                p.op("dve", lambda e: e.reciprocal(out=rstd[:], in_=rstd[:]), r=[rstd_t], w=[rstd_t])
                p.op("dve", lambda e: e.tensor_scalar(out=src, in0=src, scalar1=mv[:, 0:1], scalar2=rstd[:, 0:1],
                                                      op0=ALU.subtract, op1=ALU.mult), r=[src_t, mv_t, rstd_t], w=[src_t])
                p.op("dve", lambda e: e.tensor_tensor(out=src, in0=src, in1=LN["sb"][:, 4 - LN["base"], :], op=ALU.mult),
                     r=[src_t, LN["t"]], w=[src_t])
                p.op("dve", lambda e: e.tensor_tensor(out=src, in0=src, in1=LN["sb"][:, 5 - LN["base"], :], op=ALU.add),
                     r=[src_t, LN["t"]], w=[src_t])
                yield
            for _ in range(6):
                yield
            for j in range(2):
                p.dma("pool", out[row + j * 128:row + (j + 1) * 128, :], h1t[q_][j][:], r=[h1t_t[q_][j]])
            for _ in range(6):
                yield

        def run(gen):
            for _ in gen:
                pass

        def chain(*gens):
            for g_ in gens:
                if g_ is not None:
                    yield from g_

        def interleave(main, bg, n_main, n_bg):
            acc = 0.0
            alive = bg is not None
            for _ in main:
                acc += n_bg / n_main
                while alive and acc >= 1.0:
                    acc -= 1.0
                    try:
                        next(bg)
                    except StopIteration:
                        alive = False
            if alive:
                for _ in bg:
                    pass

        def merge(ga, gb):
            alive = [ga, gb]
            while alive:
                for g_ in list(alive):
                    if g_ is None:
                        alive.remove(g_)
                        continue
                    try:
                        yield next(g_)
                    except StopIteration:
                        alive.remove(g_)

        run(chain(b1_front(0), b1_chains(0), b1_tail(0), b3_loads(0)))
        run(b1e(0))
        for gi in range(NG):
            nxt = gi + 1 < NG
            bg = chain(b1_front(gi + 1) if nxt else None,
                       merge(b1_chains(gi + 1) if nxt else None, b3b(gi - 1) if gi > 0 else None),
                       b1_tail(gi + 1) if nxt else None,
                       b3_loads(gi) if gi > 0 else None)
            interleave(b2(gi, gi > 0), bg, 128, 150 if gi > 0 else 120)
            b3a(gi)
            if nxt:
                run(b1e(gi + 1))
        run(b3b(NG - 1))


def na_bias_tables(rpb):
    W, KH, KW, ROWS = 64, 8, 16, 32
    outs = []
    for r0 in (0, 2, 4, 28, 30):
        kt0 = na_kt0(r0)
        key_tok = kt0 * 128 + np.arange(640)
        krow, kcol = key_tok // W, key_tok % W
        q = np.arange(128)
        qrow, qcol = r0 + q // W, q % W
        rs = np.clip(qrow - KH // 2, 0, ROWS - KH)
        cs = np.clip(qcol - KW // 2, 0, W - KW)
        di = krow[:, None] - qrow[None, :] + (KH - 1)
        dj = kcol[:, None] - qcol[None, :] + (KW - 1)
        ok = ((krow[:, None] >= rs[None, :]) & (krow[:, None] < rs[None, :] + KH)
              & (kcol[:, None] >= cs[None, :]) & (kcol[:, None] < cs[None, :] + KW))
        dic = np.clip(di, 0, 14)
        djc = np.clip(dj, 0, 30)
        g = rpb[:, dic, djc]
        g = np.where(ok[None], g, np.float32(NEG)).astype(np.float32)
        outs.append(g.reshape(8, 5, 128, 128))
    return np.stack(outs, 0)


def rope_tables():
    t = np.arange(2048)
    row = (t // 64).astype(np.float32)
    col = (t % 64).astype(np.float32)
    inv = (10000.0 ** (-np.arange(0, 16, 2, dtype=np.float32) / 16)).astype(np.float32)
    ang = np.concatenate([row[:, None] * inv[None, :], col[:, None] * inv[None, :]], axis=-1)
    cos = np.cos(ang).astype(np.float32)
    sin = np.sin(ang).astype(np.float32)
    tab = np.zeros((2, 32, 2048), np.float32)
    tab[0] = np.repeat(cos, 2, axis=1).T
    sgn = np.tile(np.array([-1.0, 1.0], np.float32), 16)
    tab[1] = (np.repeat(sin, 2, axis=1) * sgn[None, :]).T
    return tab


def pair_swap_cols(w, cols):
    w2 = w.copy()
    w2[:, cols[0::2]] = w[:, cols[1::2]]
    w2[:, cols[1::2]] = w[:, cols[0::2]]
    return w2


def host_layout(inputs):
    f = lambda k: np.asarray(inputs[k], dtype=np.float32)
    w_in = f("w_in")[0]
    w_uq = f("w_uq")[0]
    sh = {}
    sh["lnp"] = np.ascontiguousarray(np.broadcast_to(np.stack(
        [f("emb_ln_g"), f("emb_ln_b"), f("ln1_g")[0], f("ln1_b")[0], f("ln2_g")[0], f("ln2_b")[0]], 0)[None], (128, 6, 1024)))
    sh["bg"] = np.ascontiguousarray(np.broadcast_to(f("ple_gate_b")[0][None], (128, 1024)))
    sh["ident"] = np.eye(128, dtype=np.float32)
    sh["iota"] = np.ascontiguousarray(np.broadcast_to(np.arange(128, dtype=np.float32)[None], (128, 128)))
    sh["rope"] = rope_tables()
    sh["w_in"] = np.ascontiguousarray(w_in)
    kr96 = w_in[:, 2112:2208]
    sh["w_kr"] = np.ascontiguousarray(np.concatenate([kr96, pair_swap_cols(kr96, np.arange(64, 96))], axis=1))
    rope_cols = np.concatenate([h * 96 + 64 + np.arange(32) for h in range(8)])
    sh["w_uq"] = np.ascontiguousarray(np.concatenate([w_uq, pair_swap_cols(w_uq, rope_cols)], axis=1))
    sh["w_ukv"] = np.ascontiguousarray(f("w_ukv")[0])
    sh["qg"] = np.ascontiguousarray(f("mla_q_norm_g")[0].reshape(3, 128).T)
    sh["kvg"] = np.ascontiguousarray(f("mla_kv_norm_g")[0].reshape(2, 128).T)
    sh["nab"] = np.ascontiguousarray(na_bias_tables(f("na_rpb")[0]).reshape(5 * 8 * 5 * 128, 128))
    sh["w_o"] = np.ascontiguousarray(f("w_o")[0])
    sh["w_q"] = np.ascontiguousarray(f("peer_w_q")[0].reshape(8, 128, 8, 256).transpose(2, 1, 0, 3)).reshape(1024, 2048)
    sk = f("peer_sub_keys")[0]
    sh["skT"] = np.ascontiguousarray(np.concatenate([sk[0].T, sk[1].T], axis=1))
    U = f("peer_u")[0]
    sh["u_l"] = np.ascontiguousarray(U.reshape(128, 128, 8, 128).transpose(0, 3, 2, 1)).reshape(16384, 1024)
    sh["v_l"] = np.ascontiguousarray(f("peer_v")[0])
    sh["w_ple"] = np.ascontiguousarray(f("ple_w")[0])
    sh["w_g"] = np.ascontiguousarray(f("ple_gate_w")[0].reshape(8, 128, 4, 256).transpose(2, 1, 0, 3)).reshape(512, 2048)
    return sh


def kernel(**inputs):
    sh = host_layout(inputs)
    x = np.asarray(inputs["x"], dtype=np.float32)
    pp = np.asarray(inputs["p"], dtype=np.float32)[0]
    in_maps = []
    for c in range(8):
        m = dict(sh)
        m["x"] = np.ascontiguousarray(x[2 * c:2 * c + 2].reshape(4096, 1024))
        m["pT"] = np.ascontiguousarray(pp[2 * c:2 * c + 2].transpose(0, 2, 1))
        in_maps.append(m)
    nc = build()
    res = run_bass_kernel_spmd(nc, in_maps, core_ids=list(range(8)))
    return np.concatenate([r["out"].reshape(2, 2048, 1024) for r in res.results], axis=0)
```

```python
import numpy as np
import concourse.bass as bass
import concourse.mybir as mybir
from concourse.bass_utils import run_bass_kernel_spmd
from contextlib import ExitStack

F32 = mybir.dt.float32
BF16 = mybir.dt.bfloat16
U32 = mybir.dt.uint32
ALU = mybir.AluOpType
AF = mybir.ActivationFunctionType
AX = mybir.AxisListType

ALPHA = float(2.0 ** 0.25)
EPS = 1e-5
NEG = -30000.0


class Trk:
    __slots__ = ("name", "w", "r")

    def __init__(self, name=""):
        self.name = name
        self.w = None
        self.r = {}


class _Rec:
    def __getattr__(self, name):
        def f(*args, **kw):
            self.call = (name, args, kw)
        return f


class Prog:
    ENG = ("pe", "act", "dve", "pool", "sp")

    def __init__(self, nc, es):
        self.nc = nc
        self.es = es
        self.ops = {e: [] for e in self.ENG}
        self.cnt = {}
        self.sem = {}
        self.seen = {e: {} for e in self.ENG}
        for e in self.ENG:
            self._mksem(e)
        self.ndma = {e: 0 for e in self.ENG}

    def _mksem(self, key):
        self.sem[key] = self.es.enter_context(self.nc.semaphore(name=f"s_{key}"))
        self.cnt[key] = 0

    def _deps(self, eng, r, w):
        deps = {}
        for t in r:
            if t.w is not None and t.w[1] > deps.get(t.w[0], 0):
                deps[t.w[0]] = t.w[1]
        for t in w:
            if t.w is not None and t.w[1] > deps.get(t.w[0], 0):
                deps[t.w[0]] = t.w[1]
            for k, v in t.r.items():
                if v > deps.get(k, 0):
                    deps[k] = v
        waits = []
        for k, v in deps.items():
            if k == eng and eng == "pe":
                continue
            if self.seen[eng].get(k, 0) >= v:
                continue
            self.seen[eng][k] = v
            waits.append((k, v))
        return waits

    def op(self, eng, fn, r=(), w=()):
        rec = _Rec()
        fn(rec)
        call = rec.call
        fn = lambda e, call=call: getattr(e, call[0])(*call[1], **call[2])
        waits = self._deps(eng, r, w)
        self.cnt[eng] += 1
        n = self.cnt[eng]
        self.ops[eng].append((fn, waits, eng, 1))
        for t in r:
            t.r[eng] = n
        for t in w:
            t.w = (eng, n)
            t.r = {}

    def dma(self, q, out, in_, r=(), w=(), **kw):
        waits = self._deps(q, r, w)
        key = f"d{q}{self.ndma[q] % (8 if q == 'pool' else 24)}"
        self.ndma[q] += 1
        if key not in self.sem:
            self._mksem(key)
        prev = self.cnt[key]
        if prev and self.seen[q].get(key, 0) < prev:
            self.seen[q][key] = prev
            waits.append((key, prev))
        self.cnt[key] += 16
        n = self.cnt[key]
        self.ops[q].append((lambda e: e.dma_start(out=out, in_=in_, **kw), waits, key, 16))
        for t in r:
            t.r[key] = n
        for t in w:
            t.w = (key, n)
            t.r = {}

    def barrier(self):
        snap = dict(self.cnt)
        for e in self.ENG:
            waits = []
            for k, v in snap.items():
                if v and k != e and self.seen[e].get(k, 0) < v:
                    self.seen[e][k] = v
                    waits.append((k, v))
            if waits:
                self.ops[e].append((None, waits, None, 0))

    def finish(self):
        fin = [(k, v) for k, v in self.cnt.items() if k.startswith("d") and k not in self.ENG and v > 0]
        nc = self.nc
        names = {"pe": "tensor", "act": "scalar", "dve": "vector", "pool": "gpsimd", "sp": "sync"}
        with nc.Block() as block:
            for e in self.ENG:
                def body(eng, e=e):
                    for fn, waits, key, amt in self.ops[e]:
                        for k, v in waits:
                            eng.wait_ge(self.sem[k], v)
                        if fn is not None:
                            ins = fn(eng)
                            ins.then_inc(self.sem[key], amt)
                    if e == "sp":
                        for k, v in fin:
                            eng.wait_ge(self.sem[k], v)
                getattr(block, names[e])(body)


def tl(n, name=""):
    return [Trk(f"{name}{i}") for i in range(n)]


def build(nseq=2, do_b=True, debug=False):
    nc = bass.Bass("TRN2", target_bir_lowering=False)
    NT = nseq * 2048

    def din(name, shape, dt=F32):
        return nc.dram_tensor(name, list(shape), dt, kind="ExternalInput").ap()

    def dscr(name, shape, dt):
        return nc.dram_tensor(name, list(shape), dt, kind="Internal").ap()

    x = din("x", [4096, 1024])
    pT = din("pT", [2, 256, 2048])
    lnp = din("lnp", [128, 6, 1024])
    bgd = din("bg", [128, 1024])
    ident = din("ident", [128, 128])
    iota = din("iota", [128, 128])
    rope = din("rope", [2, 32, 2048])
    w_in = din("w_in", [1024, 2208])
    w_kr = din("w_kr", [1024, 2 * 96])
    w_uq = din("w_uq", [384, 2 * 768])
    w_ukv = din("w_ukv", [256, 1024])
    qg = din("qg", [128, 3])
    kvg = din("kvg", [128, 2])
    nab = din("nab", [5 * 8 * 5 * 128, 128])
    w_o = din("w_o", [1024, 1024])
    w_q = din("w_q", [1024, 2048])
    skT = din("skT", [128, 256])
    u_l = din("u_l", [16384, 1024])
    v_l = din("v_l", [16384, 1024])
    w_ple = din("w_ple", [256, 1024])
    w_g = din("w_g", [512, 2048])
    out = nc.dram_tensor("out", [4096, 1024], F32, kind="ExternalOutput").ap()
    if debug:
        h1d = nc.dram_tensor("h1d", [4096, 1024], F32, kind="ExternalOutput").ap()
        catd = nc.dram_tensor("catd", [4096, 1024], F32, kind="ExternalOutput").ap()
    else:
        h1d = dscr("h1d", [4096, 1024], F32)
        catd = None
    h0d = dscr("h0d", [4096, 1024], F32)
    h0Td = dscr("h0Td", [2 * 4 * 128, 4096], BF16)
    b_win = dscr("b_win", [1024, 2208], BF16)
    b_wkr = dscr("b_wkr", [1024, 192], BF16)
    b_wuq = dscr("b_wuq", [384, 1536], BF16)
    b_wukv = dscr("b_wukv", [256, 1024], BF16)
    b_nab = dscr("b_nab", [5 * 8 * 5 * 128, 128], BF16)
    b_wo = dscr("b_wo", [1024, 1024], BF16)
    b_wq = dscr("b_wq", [1024, 2048], BF16)
    b_u = dscr("b_u", [16384, 1024], BF16)
    b_v = dscr("b_v", [16384, 1024], BF16)
    b_wple = dscr("b_wple", [256, 1024], BF16)
    b_wg = dscr("b_wg", [512, 2048], BF16)

    with ExitStack() as es:
        p = Prog(nc, es)

        uniq = [0]

        def sbuf(st, name, shape, dt):
            uniq[0] += 1
            return st.enter_context(nc.sbuf_tensor(f"{name}_{uniq[0]}", list(shape), dt))

        PS = es.enter_context(nc.psum_tensor("ps", [128, 8, 512], F32))
        PT = tl(8, "bank")

        def bank(i):
            return PS[:, i, :]

        def bank_bf(i):
            return PS[:, i, 0:256].bitcast(BF16).rearrange("p (a b) -> p a b", b=128)

        scr = {}

        def cast_dram(dst, src, rows, name, rows_per=512):
            scr[name] = []
            a = rows_per // 128
            dv = dst.rearrange("(n p a) m -> n p a m", p=128, a=a)
            sv = src.rearrange("(n p a) m -> n p a m", p=128, a=a)
            for i in range(rows // rows_per):
                t = Trk(name)
                scr[name].append(t)
                p.dma("pool", dv[i], sv[i], w=[t])

        cast_dram(b_win[:, 0:1104], w_in[:, 0:1104], 1024, "win", 512)
        win0 = scr["win"]
        cast_dram(b_win[:, 1104:2208], w_in[:, 1104:2208], 1024, "win", 512)
        scr["win"] = win0 + scr["win"]
        cast_dram(b_wkr, w_kr, 1024, "wkr", 1024)
        cast_dram(b_wuq, w_uq, 384, "wuq", 128)
        cast_dram(b_wukv, w_ukv, 256, "wukv", 256)
        cast_dram(b_nab, nab, 5 * 8 * 5 * 128, "nab", 1280)
        cast_dram(b_wo, w_o, 1024, "wo", 512)
        def cast_gen():
            for dst, src, rows, name, per in ((b_wq, w_q, 1024, "wq", 256), (b_wg, w_g, 512, "wg", 256),
                                              (b_wple, w_ple, 256, "wple", 256), (b_u, u_l, 16384, "u", 512),
                                              (b_v, v_l, 16384, "v", 512)):
                scr[name] = []
                a_ = per // 128
                dv = dst.rearrange("(n p a) m -> n p a m", p=128, a=a_)
                sv = src.rearrange("(n p a) m -> n p a m", p=128, a=a_)
                for i in range(rows // per):
                    t = Trk(name)
                    scr[name].append(t)
                    dep = yield
                    p.dma("pool", dv[i], sv[i], r=([dep] if dep is not None else []), w=[t])

        cg = [cast_gen() if do_b else None]
        if do_b:
            next(cg[0])

        def advance_casts(n, dep=None):
            for _ in range(n):
                if cg[0] is None:
                    return
                try:
                    cg[0].send(dep)
                except StopIteration:
                    cg[0] = None

        cs = es
        identf = sbuf(cs, "identf", [128, 128], F32); identf_t = Trk()
        identb = sbuf(cs, "identb", [128, 128], BF16); identb_t = Trk()
        onesb = sbuf(cs, "onesb", [128, 128], BF16); onesb_t = Trk()
        LN = {}
        p.dma("sp", identf[:], ident, w=[identf_t])

        def load_lnp(st, lo, hi):
            LN["sb"] = sbuf(st, "lnp_sb", [128, hi - lo, 1024], F32)
            LN["t"] = Trk()
            LN["base"] = lo
            p.dma("sp", LN["sb"][:], lnp[:, lo:hi, :], w=[LN["t"]])
        p.op("dve", lambda e: e.tensor_copy(out=identb[:], in_=identf[:]), r=[identf_t], w=[identb_t])
        p.op("dve", lambda e: e.memset(onesb[:], 1.0), w=[onesb_t])
        st6 = sbuf(cs, "st6", [128, 2, 6], F32); st6_t = Trk()
        mv = sbuf(cs, "mv", [128, 2], F32); mv_t = Trk()
        rstd = sbuf(cs, "rstd", [128, 1], F32); rstd_t = Trk()

        def layer_norm(src, src_t, gi, dst=None, dst_t=None):
            if dst is None:
                dst, dst_t = src, src_t
            for c in range(2):
                p.op("dve", lambda e, c=c: e.bn_stats(out=st6[:, c, :], in_=src[:, c * 512:(c + 1) * 512]),
                     r=[src_t], w=[st6_t])
            p.op("dve", lambda e: e.bn_aggr(out=mv[:], in_=st6[:].rearrange("p a b -> p (a b)")), r=[st6_t], w=[mv_t])
            p.op("act", lambda e: e.activation(out=rstd[:], in_=mv[:, 1:2], func=AF.Sqrt, bias=EPS, scale=1.0),
                 r=[mv_t], w=[rstd_t])
            p.op("dve", lambda e: e.reciprocal(out=rstd[:], in_=rstd[:]), r=[rstd_t], w=[rstd_t])
            p.op("dve", lambda e: e.tensor_scalar(out=dst, in0=src, scalar1=mv[:, 0:1], scalar2=rstd[:, 0:1],
                                                  op0=ALU.subtract, op1=ALU.mult), r=[src_t, mv_t, rstd_t], w=[dst_t])
            gi -= LN["base"]
            p.op("dve", lambda e: e.tensor_tensor(out=dst, in0=dst, in1=LN["sb"][:, gi, :], op=ALU.mult),
                 r=[dst_t, LN["t"]], w=[dst_t])
            p.op("dve", lambda e: e.tensor_tensor(out=dst, in0=dst, in1=LN["sb"][:, gi + 1, :], op=ALU.add),
                 r=[dst_t, LN["t"]], w=[dst_t])

        def transpose_to(src_bf, src_t, dst_fn, dst_t, nchunk, banks=(6, 7), eng="dve"):
            for i, c0 in enumerate(range(0, nchunk, 4)):
                b = banks[i % len(banks)]
                n = min(4, nchunk - c0)
                for k in range(n):
                    p.op("pe", lambda e, k=k, c0=c0, b=b: e.transpose(out=bank_bf(b)[:, k, :],
                                                                      in_=src_bf[:, (c0 + k) * 128:(c0 + k + 1) * 128],
                                                                      identity=identb[:]),
                         r=[src_t, identb_t], w=[PT[b]])
                if eng == "dve":
                    p.op("dve", lambda e, c0=c0, n=n, b=b: e.tensor_copy(out=dst_fn(c0, n), in_=bank_bf(b)[:, 0:n, :]),
                         r=[PT[b]], w=[dst_t])
                else:
                    p.op("act", lambda e, c0=c0, n=n, b=b: e.copy(out=dst_fn(c0, n), in_=bank_bf(b)[:, 0:n, :]),
                         r=[PT[b]], w=[dst_t])

        h1_t = tl(32, "h1d")
        h0d_t = tl(32, "h0d")
        h0Td_t = tl(8, "h0Td")
        phA = ExitStack()
        load_lnp(phA, 0, 4)
        for s in range(nseq):
            with ExitStack() as ss:
                cat = sbuf(ss, "cat", [128, 16, 1024], BF16)
                cat_t = tl(16, "cat")
                xt = [sbuf(ss, f"xt{i}", [128, 1024], F32) for i in range(2)]
                xt_t = tl(2, "xt")
                hb = sbuf(ss, "hb", [128, 1024], BF16); hb_t = Trk()
                h0T = sbuf(ss, "h0T", [128, 8, 512], BF16); h0T_t = Trk()
                xcnt = [0]

                def load_h0T(g):
                    for j in range(4):
                        i = xcnt[0] % 2
                        xcnt[0] += 1
                        row = s * 2048 + g * 512 + j * 128
                        p.dma("sp", xt[i][:], x[row:row + 128, :], w=[xt_t[i]])
                        layer_norm(xt[i][:], xt_t[i], 0)
                        p.dma("pool", h0d[row:row + 128, :], xt[i][:], r=[xt_t[i]], w=[h0d_t[row // 128]])
                        p.op("act", lambda e, i=i: e.copy(out=hb[:], in_=xt[i][:]), r=[xt_t[i]], w=[hb_t])
                        transpose_to(hb, hb_t, lambda c0, n, j=j: h0T[:, c0:c0 + n, j * 128:(j + 1) * 128], h0T_t, 8)
                    sg = s * 4 + g
                    p.dma("pool", h0Td[sg * 128:(sg + 1) * 128, :], h0T[:].rearrange("p k t -> p (k t)"), r=[h0T_t], w=[h0Td_t[sg]])

                def load_h0T_scratch(g):
                    sg = s * 4 + g
                    p.dma("sp", h0T[:].rearrange("p k t -> p (k t)"), h0Td[sg * 128:(sg + 1) * 128, :], r=[h0Td_t[sg]], w=[h0T_t])

                with ExitStack() as sa:
                    wna = sbuf(sa, "wna", [128, 8, 1536], BF16); wna_t = Trk()
                    qT = sbuf(sa, "qT", [128, 4, 2048], BF16); qT_t = Trk()
                    kT = sbuf(sa, "kT", [128, 4, 2048], BF16); kT_t = Trk()
                    vna = sbuf(sa, "vna", [128, 16, 8, 65], BF16); vna_t = Trk()
                    bias = [sbuf(sa, f"nbias{i}", [128, 40, 128], BF16) for i in range(2)]
                    bias_t = tl(2, "nbias")
                    PTn = [sbuf(sa, f"PTn{i}", [128, 5, 128], BF16) for i in range(2)]
                    PTn_t = tl(2, "PTn")
                    rc = sbuf(sa, "rc", [128, 8], F32); rc_t = Trk()
                    p.dma("sp", wna[:], b_win[:, 0:1536].rearrange("(k p) m -> p k m", p=128), r=scr["win"], w=[wna_t])
                    p.op("dve", lambda e: e.memset(vna[:, :, :, 64:65], 1.0), w=[vna_t])
                    for g in range(4):
                        load_h0T(g)
                        for c in range(8):
                            b = 4 + (c % 2)
                            for k in range(8):
                                p.op("pe", lambda e, c=c, k=k, b=b: e.matmul(out=bank(b), lhsT=wna[:, k, c * 128:(c + 1) * 128],
                                                                             rhs=h0T[:, k, :], start=(k == 0), stop=(k == 7)),
                                     r=[wna_t, h0T_t], w=[PT[b]])
                            if c < 4:
                                p.op("act", lambda e, c=c, b=b, g=g: e.mul(out=qT[:, c, g * 512:(g + 1) * 512], in_=bank(b), mul=0.125),
                                     r=[PT[b]], w=[qT_t])
                            else:
                                p.op("act", lambda e, c=c, b=b, g=g: e.copy(out=kT[:, c - 4, g * 512:(g + 1) * 512], in_=bank(b)),
                                     r=[PT[b]], w=[kT_t])
                        for j in range(4):
                            b = 4 + (j % 2)
                            for k in range(8):
                                p.op("pe", lambda e, j=j, k=k, b=b: e.matmul(out=bank(b), lhsT=h0T[:, k, j * 128:(j + 1) * 128],
                                                                             rhs=wna[:, k, 1024:1536], start=(k == 0), stop=(k == 7)),
                                     r=[wna_t, h0T_t], w=[PT[b]])
                            p.op("dve", lambda e, j=j, b=b, g=g: e.tensor_copy(out=vna[:, g * 4 + j, :, 0:64],
                                                                               in_=bank(b).rearrange("p (h d) -> p h d", d=64)),
                                 r=[PT[b]], w=[vna_t])
                    def na_info(ti):
                        r0 = 2 * ti
                        return {0: 0, 2: 1, 28: 3, 30: 4}.get(r0, 2), na_kt0(r0)

                    def na_bias_load(ti):
                        typ, _ = na_info(ti)
                        bi = ti % 2
                        p.dma("sp", bias[bi][:], b_nab[typ * 5120:(typ + 1) * 5120, :].rearrange("(a p) q -> p a q", p=128),
                              r=scr["nab"], w=[bias_t[bi]])

                    def na_S(step):
                        ti, h = step // 8, step % 8
                        _, kt0 = na_info(ti)
                        bi = ti % 2
                        pr, po = h // 2, (h % 2) * 64
                        sb_ = 2 * (step % 2)
                        for kc in range(5):
                            bk = sb_ + (kc // 4)
                            oap = PS[:, bk, (kc % 4) * 128:(kc % 4 + 1) * 128]
                            p.op("pe", lambda e: e.matmul(
                                out=oap, lhsT=kT[po:po + 64, pr, (kt0 + kc) * 128:(kt0 + kc + 1) * 128],
                                rhs=qT[po:po + 64, pr, ti * 128:(ti + 1) * 128], start=True, stop=False),
                                r=[kT_t, qT_t], w=[PT[bk]])
                            p.op("pe", lambda e: e.matmul(
                                out=oap, lhsT=identb[:], rhs=bias[bi][:, h * 5 + kc, :], start=False, stop=True),
                                r=[identb_t, bias_t[bi]], w=[PT[bk]])

                    na_bias_load(0)
                    na_bias_load(1)
                    na_S(0)
                    for step in range(128):
                        ti, h = step // 8, step % 8
                        _, kt0 = na_info(ti)
                        if step + 1 < 128:
                            na_S(step + 1)
                        sb_ = 2 * (step % 2)
                        pi = step % 2
                        p.op("act", lambda e: e.activation(
                            out=PTn[pi][:].rearrange("p a b -> p (a b)"),
                            in_=PS[:, sb_:sb_ + 2, :].rearrange("p a b -> p (a b)")[:, 0:640], func=AF.Exp),
                            r=[PT[sb_], PT[sb_ + 1]], w=[PTn_t[pi]])
                        ob0 = 4 + 2 * (ti % 2)
                        ob = ob0 + h // 4
                        oo = (h % 4) * 65
                        for kc in range(5):
                            p.op("pe", lambda e: e.matmul(
                                out=PS[:, ob, oo:oo + 65], lhsT=PTn[pi][:, kc, :], rhs=vna[:, kt0 + kc, h, :],
                                start=(kc == 0), stop=(kc == 4)),
                                r=[PTn_t[pi], vna_t], w=[PT[ob]])
                        if h == 7:
                            if ti + 2 < 16:
                                na_bias_load(ti + 2)
                            tick = Trk()
                            for ob in (ob0, ob0 + 1):
                                o4 = PS[:, ob, 0:260].rearrange("p (h d) -> p h d", d=65)
                                hh = (ob - ob0) * 4
                                p.op("dve", lambda e: e.reciprocal(out=rc[:, hh:hh + 4], in_=o4[:, :, 64]),
                                     r=[PT[ob]], w=[rc_t])
                                p.op("dve", lambda e: e.tensor_tensor(
                                    out=cat[:, ti, hh * 64:(hh + 4) * 64].rearrange("p (h d) -> p h d", d=64),
                                    in0=o4[:, :, 0:64], in1=rc[:, hh:hh + 4].unsqueeze(2).broadcast_to([128, 4, 64]), op=ALU.mult),
                                    r=[PT[ob], rc_t], w=[cat_t[ti], tick])
                            advance_casts(2, tick)
                p.barrier()

                with ExitStack() as sa:
                    wcq = sbuf(sa, "wcq", [128, 8, 640], BF16); wcq_t = Trk()
                    wkr = sbuf(sa, "wkr", [128, 8, 192], BF16); wkr_t = Trk()
                    wuq = sbuf(sa, "wuq", [128, 3, 1536], BF16); wuq_t = Trk()
                    wukv = sbuf(sa, "wukv", [128, 2, 1024], BF16); wukv_t = Trk()
                    qg_sb = sbuf(sa, "qg_sb", [128, 3], F32); kvg_sb = sbuf(sa, "kvg_sb", [128, 2], F32); g_t = Trk()
                    cqT = sbuf(sa, "cqT", [128, 3, 2048], BF16); cqT_t = Trk()
                    Rq = sbuf(sa, "Rq", [128, 2048], F32); Rq_t = Trk()
                    KT = sbuf(sa, "KT", [96, 8, 2048], BF16); KT_t = Trk()
                    V = sbuf(sa, "Vm", [128, 16, 8, 65], BF16); V_t = Trk()
                    Qg = sbuf(sa, "Qg", [96, 8, 512], BF16); Qg_t = Trk()
                    ckv = sbuf(sa, "ckv", [128, 2, 512], BF16); ckv_t = Trk()
                    sq = sbuf(sa, "sq", [128, 512], BF16); sq_t = Trk()
                    Rkv = sbuf(sa, "Rkv", [128, 512], F32); Rkv_t = Trk()
                    rcol = sbuf(sa, "rcol", [128, 1], F32); rcol_t = Trk()
                    tab = sbuf(sa, "tab", [96, 2, 512], F32); tab_t = Trk()
                    c1 = sbuf(sa, "c1", [96, 2, 512], F32); c1_t = Trk()
                    t1 = sbuf(sa, "t1", [96, 512], F32); t1_t = Trk()
                    t2 = sbuf(sa, "t2", [96, 512], F32); t2_t = Trk()
                    PTm = [sbuf(sa, f"PTm{i}", [128, 512], BF16) for i in range(2)]
                    PTm_t = tl(2, "PTm")
                    rc4 = sbuf(sa, "rc4", [128, 4], F32); rc4_t = Trk()
                    p.dma("sp", wcq[:], b_win[:, 1536:2176].rearrange("(k p) m -> p k m", p=128), r=scr["win"], w=[wcq_t])
                    p.dma("sp", wkr[:], b_wkr.rearrange("(k p) m -> p k m", p=128), r=scr["wkr"], w=[wkr_t])
                    p.dma("sp", wuq[:], b_wuq.rearrange("(k p) m -> p k m", p=128), r=scr["wuq"], w=[wuq_t])
                    p.dma("sp", wukv[:], b_wukv.rearrange("(k p) m -> p k m", p=128), r=scr["wukv"], w=[wukv_t])
                    p.dma("sp", qg_sb[:], qg, w=[g_t])
                    p.dma("sp", kvg_sb[:], kvg, w=[g_t])
                    for c in range(3):
                        p.op("dve", lambda e, c=c: e.tensor_scalar(out=wuq[:, c, :], in0=wuq[:, c, :], scalar1=qg_sb[:, c:c + 1],
                                                                   scalar2=None, op0=ALU.mult), r=[wuq_t, g_t], w=[wuq_t])
                    for c in range(2):
                        p.op("dve", lambda e, c=c: e.tensor_scalar(out=wukv[:, c, :], in0=wukv[:, c, :], scalar1=kvg_sb[:, c:c + 1],
                                                                   scalar2=None, op0=ALU.mult), r=[wukv_t, g_t], w=[wukv_t])
                    p.op("dve", lambda e: e.memset(V[:, :, :, 64:65], 1.0), w=[V_t])

                    def rms_bcast(src_fn, nchunk, nfeat, dst, dst_t, extra):
                        for c in range(nchunk):
                            p.op("dve", lambda e, c=c: e.tensor_tensor(out=sq[:], in0=src_fn(c), in1=src_fn(c), op=ALU.mult),
                                 r=[cqT_t, ckv_t], w=[sq_t])
                            p.op("pe", lambda e, c=c: e.matmul(out=bank(3), lhsT=onesb[:], rhs=sq[:], start=(c == 0),
                                                               stop=(c == nchunk - 1)), r=[onesb_t, sq_t], w=[PT[3]])
                        p.op("act", lambda e: e.activation(out=dst, in_=bank(3), func=AF.Sqrt, bias=EPS, scale=1.0 / nfeat),
                             r=[PT[3]], w=[dst_t])
                        p.op("dve", lambda e: e.reciprocal(out=dst, in_=dst), r=[dst_t], w=[dst_t])
                        if extra != 1.0:
                            p.op("dve", lambda e: e.tensor_scalar(out=dst, in0=dst, scalar1=extra, scalar2=None, op0=ALU.mult),
                                 r=[dst_t], w=[dst_t])

                    def load_tab(g):
                        p.dma("sp", tab[64:96, :, :], rope[:, :, g * 512:(g + 1) * 512].rearrange("a f t -> f a t"), w=[tab_t])

                    for g in range(4):
                        gs = slice(g * 512, (g + 1) * 512)
                        load_h0T_scratch(g)
                        load_tab(g)
                        for c in range(5):
                            b = 4 + (c % 2)
                            for k in range(8):
                                p.op("pe", lambda e, c=c, k=k, b=b: e.matmul(out=bank(b), lhsT=wcq[:, k, c * 128:(c + 1) * 128],
                                                                             rhs=h0T[:, k, :], start=(k == 0), stop=(k == 7)),
                                     r=[wcq_t, h0T_t], w=[PT[b]])
                            if c < 3:
                                p.op("act", lambda e, c=c, b=b: e.copy(out=cqT[:, c, gs], in_=bank(b)), r=[PT[b]], w=[cqT_t])
                            else:
                                p.op("act", lambda e, c=c, b=b: e.copy(out=ckv[:, c - 3, :], in_=bank(b)), r=[PT[b]], w=[ckv_t])
                        rms_bcast(lambda c: cqT[:, c, gs], 3, 384.0, Rq[:, gs], Rq_t, 96.0 ** -0.5)
                        rms_bcast(lambda c: ckv[:, c, :], 2, 256.0, Rkv[:], Rkv_t, 1.0)
                        for h in range(8):
                            b = 4 + (h % 2)
                            for c in range(2):
                                p.op("pe", lambda e, h=h, c=c, b=b: e.matmul(out=PS[0:64, b, :], lhsT=wukv[:, c, h * 128:h * 128 + 64],
                                                                             rhs=ckv[:, c, :], start=(c == 0), stop=(c == 1)),
                                     r=[wukv_t, ckv_t], w=[PT[b]])
                            p.op("dve", lambda e, h=h, b=b: e.tensor_tensor(out=KT[0:64, h, gs], in0=PS[0:64, b, :], in1=Rkv[0:64, :],
                                                                            op=ALU.mult), r=[PT[b], Rkv_t], w=[KT_t])
                        for v_ in range(2):
                            b = 4 + v_
                            for k in range(8):
                                p.op("pe", lambda e, v_=v_, k=k, b=b: e.matmul(out=PS[0:96, b, :], lhsT=wkr[:, k, v_ * 96:(v_ + 1) * 96],
                                                                               rhs=h0T[:, k, :], start=(k == 0), stop=(k == 7)),
                                     r=[wkr_t, h0T_t], w=[PT[b]])
                        p.op("dve", lambda e: e.tensor_tensor(out=t1[64:96, :], in0=PS[64:96, 4, :], in1=tab[64:96, 0, :], op=ALU.mult),
                             r=[PT[4], tab_t], w=[t1_t])
                        p.op("dve", lambda e: e.tensor_tensor(out=t2[64:96, :], in0=PS[64:96, 5, :], in1=tab[64:96, 1, :], op=ALU.mult),
                             r=[PT[5], tab_t], w=[t2_t])
                        p.op("dve", lambda e: e.tensor_tensor(out=t1[64:96, :], in0=t1[64:96, :], in1=t2[64:96, :], op=ALU.add),
                             r=[t1_t, t2_t], w=[t1_t])
                        for h in range(8):
                            p.op("act", lambda e, h=h: e.copy(out=KT[64:96, h, gs], in_=t1[64:96, :]), r=[t1_t], w=[KT_t])
                        for j in range(4):
                            p.op("pe", lambda e, j=j: e.transpose(out=PS[:, 6, 0:128], in_=Rkv[:, j * 128:(j + 1) * 128], identity=identf[:]),
                                 r=[Rkv_t, identf_t], w=[PT[6]])
                            p.op("dve", lambda e: e.tensor_copy(out=rcol[:], in_=PS[:, 6, 0:1]), r=[PT[6]], w=[rcol_t])
                            b = 4 + (j % 2)
                            for c in range(2):
                                p.op("pe", lambda e, j=j, c=c, b=b: e.matmul(
                                    out=bank(b), lhsT=ckv[:, c, j * 128:(j + 1) * 128],
                                    rhs=wukv[:, c, :].rearrange("p (h d) -> p h d", d=128)[:, :, 64:128],
                                    start=(c == 0), stop=(c == 1)), r=[wukv_t, ckv_t], w=[PT[b]])
                            p.op("dve", lambda e, j=j, b=b, g=g: e.tensor_scalar(
                                out=V[:, g * 4 + j, :, 0:64], in0=bank(b).rearrange("p (h d) -> p h d", d=64),
                                scalar1=rcol[:, 0:1], scalar2=None, op0=ALU.mult), r=[PT[b], rcol_t], w=[V_t])
                    for g in range(4):
                        gs = slice(g * 512, (g + 1) * 512)
                        load_tab(g)
                        for a in range(2):
                            p.op("dve", lambda e, a=a: e.tensor_tensor(out=c1[64:96, a, :], in0=tab[64:96, a, :], in1=Rq[64:96, gs],
                                                                       op=ALU.mult), r=[tab_t, Rq_t], w=[c1_t])
                        for h in range(8):
                            for v_ in range(2):
                                b = 6 + v_
                                for c in range(3):
                                    p.op("pe", lambda e, h=h, v_=v_, c=c, b=b: e.matmul(
                                        out=PS[0:96, b, :], lhsT=wuq[:, c, v_ * 768 + h * 96:v_ * 768 + (h + 1) * 96],
                                        rhs=cqT[:, c, gs], start=(c == 0), stop=(c == 2)), r=[wuq_t, cqT_t], w=[PT[b]])
                            p.op("dve", lambda e, h=h: e.tensor_tensor(out=Qg[0:64, h, :], in0=PS[0:64, 6, :], in1=Rq[0:64, gs], op=ALU.mult),
                                 r=[PT[6], Rq_t], w=[Qg_t])
                            p.op("dve", lambda e: e.tensor_tensor(out=t1[64:96, :], in0=PS[64:96, 6, :], in1=c1[64:96, 0, :], op=ALU.mult),
                                 r=[PT[6], c1_t], w=[t1_t])
                            p.op("dve", lambda e: e.tensor_tensor(out=t2[64:96, :], in0=PS[64:96, 7, :], in1=c1[64:96, 1, :], op=ALU.mult),
                                 r=[PT[7], c1_t], w=[t2_t])
                            p.op("dve", lambda e, h=h: e.tensor_tensor(out=Qg[64:96, h, :], in0=t1[64:96, :], in1=t2[64:96, :], op=ALU.add),
                                 r=[t1_t, t2_t], w=[Qg_t])
                        def mla_S(st_):
                            h, kc = st_ // 16, st_ % 16
                            sbk = 4 + (st_ % 2)
                            p.op("pe", lambda e: e.matmul(
                                out=bank(sbk), lhsT=KT[0:96, h, kc * 128:(kc + 1) * 128], rhs=Qg[0:96, h, :], start=True, stop=True),
                                r=[KT_t, Qg_t], w=[PT[sbk]])

                        mla_S(0)
                        for st_ in range(128):
                            h, kc = st_ // 16, st_ % 16
                            if st_ + 1 < 128:
                                mla_S(st_ + 1)
                            sbk = 4 + (st_ % 2)
                            pi = st_ % 2
                            p.op("act", lambda e: e.activation(out=PTm[pi][:], in_=bank(sbk), func=AF.Exp),
                                 r=[PT[sbk]], w=[PTm_t[pi]])
                            for j in range(4):
                                p.op("pe", lambda e: e.matmul(
                                    out=PS[:, j, 0:65], lhsT=PTm[pi][:, j * 128:(j + 1) * 128], rhs=V[:, kc, h, :],
                                    start=(kc == 0), stop=(kc == 15)), r=[PTm_t[pi], V_t], w=[PT[j]])
                            if kc == 15:
                                tick = Trk()
                                for j in range(4):
                                    p.op("dve", lambda e: e.reciprocal(out=rc4[:, j:j + 1], in_=PS[:, j, 64:65]), r=[PT[j]], w=[rc4_t])
                                    p.op("dve", lambda e: e.tensor_scalar(
                                        out=cat[:, g * 4 + j, 512 + h * 64:512 + (h + 1) * 64], in0=PS[:, j, 0:64],
                                        scalar1=rc4[:, j:j + 1], scalar2=None, op0=ALU.mult), r=[PT[j], rc4_t], w=[cat_t[g * 4 + j], tick])
                                advance_casts(1, tick)
                p.barrier()

                with ExitStack() as sa:
                    wo = sbuf(sa, "wo", [128, 8, 1024], BF16); wo_t = Trk()
                    cT = sbuf(sa, "cT", [128, 8, 128], BF16); cT_t = Trk()
                    yt = [sbuf(sa, f"yt{i}", [128, 1024], F32) for i in range(2)]
                    yt_t = tl(2, "yt")
                    catf = sbuf(sa, "catf", [128, 1024], F32); catf_t = Trk()
                    p.dma("sp", wo[:], b_wo.rearrange("(k p) m -> p k m", p=128), r=scr["wo"], w=[wo_t])
                    for ti in range(16):
                        i = ti % 2
                        row = s * 2048 + ti * 128
                        p.dma("sp", xt[i][:], h0d[row:row + 128, :], r=[h0d_t[row // 128]], w=[xt_t[i]])
                        if debug:
                            p.op("act", lambda e, ti=ti: e.copy(out=catf[:], in_=cat[:, ti, :]), r=[cat_t[ti]], w=[catf_t])
                            p.dma("sp", catd[row:row + 128, :], catf[:], r=[catf_t])
                        transpose_to(cat[:, ti, :], cat_t[ti], lambda c0, n: cT[:, c0:c0 + n, :], cT_t, 8)
                        for half in range(2):
                            b = 4 + half
                            for k in range(8):
                                p.op("pe", lambda e, k=k, half=half, b=b: e.matmul(out=bank(b), lhsT=cT[:, k, :],
                                                                                   rhs=wo[:, k, half * 512:(half + 1) * 512],
                                                                                   start=(k == 0), stop=(k == 7)),
                                     r=[cT_t, wo_t], w=[PT[b]])
                            p.op("dve", lambda e, i=i, half=half, b=b: e.scalar_tensor_tensor(
                                out=yt[i][:, half * 512:(half + 1) * 512], in0=xt[i][:, half * 512:(half + 1) * 512], scalar=ALPHA,
                                in1=bank(b), op0=ALU.mult, op1=ALU.add), r=[xt_t[i], PT[b]], w=[yt_t[i]])
                        layer_norm(yt[i][:], yt_t[i], 2)
                        p.dma("pool", h1d[row:row + 128, :], yt[i][:], r=[yt_t[i]], w=[h1_t[row // 128]])
                p.barrier()

        phA.close()
        advance_casts(1000)
        if do_b:
            load_lnp(es, 4, 6)
            phase_b(nc, p, es, sbuf, PS, PT, bank, bank_bf, NT, dict(
                h1d=h1d, h1_t=h1_t, out=out, pT=pT, bgd=bgd, iota=iota, skT=skT, b_wq=b_wq, b_u=b_u, b_v=b_v,
                b_wple=b_wple, b_wg=b_wg, scr=scr, identf=identf, identf_t=identf_t, identb=identb, identb_t=identb_t,
                layer_norm=layer_norm, transpose_to=transpose_to, LN=LN,
                lnscr=(st6, st6_t, mv, mv_t, rstd, rstd_t)))
        p.finish()
    return nc


def na_kt0(r0):
    rs = min(max(r0 - 4, 0), 24)
    bs = min(rs, 23)
    return min(bs // 2, 11)


def phase_b(nc, p, es, sbuf, PS, PT, bank, bank_bf, NT, a):
    h1d, h1_t, out, pT = a["h1d"], a["h1_t"], a["out"], a["pT"]
    scr = a["scr"]
    identf, identf_t = a["identf"], a["identf_t"]
    layer_norm, transpose_to = a["layer_norm"], a["transpose_to"]
    NG = NT // 256
    with ExitStack() as sb_:
        iota_f = sbuf(sb_, "iota_f", [128, 128], F32); iota_b = sbuf(sb_, "iota_b", [128, 128], BF16); iota_t = Trk()
        skTf = sbuf(sb_, "skTf", [128, 256], F32); skTb = sbuf(sb_, "skTb", [128, 256], BF16); sk_t = Trk()
        bg_sb = sbuf(sb_, "bg_sb", [128, 1024], F32); bg_t = Trk()
        wple = sbuf(sb_, "wple", [128, 2, 1024], BF16); wple_t = Trk()
        Gall = sbuf(sb_, "Gall", [128, 256, 128], BF16); G_t = Trk()
        h1t_ = [sbuf(sb_, f"h1t{i}", [128, 1024], F32) for i in range(2)]
        h1t = [h1t_, h1t_]
        h1tt_ = tl(2)
        h1t_t = [h1tt_, h1tt_]
        hst = sbuf(sb_, "hst", [128, 1024], F32); hst_t = Trk()
        h1b = sbuf(sb_, "h1b", [128, 1024], BF16); h1b_t = Trk()
        h1T = [sbuf(sb_, f"h1T{q}", [128, 8, 256], BF16) for q in range(3)]; h1T_t = tl(3)
        qTp = sbuf(sb_, "qTp", [128, 16, 256], BF16); qTp_t = Trk()
        wqs = [sbuf(sb_, f"wqs{i}", [128, 8, 256], BF16) for i in range(3)]; wqs_t = tl(3)
        ub = [sbuf(sb_, f"ub{i}", [128, 4096], BF16) for i in range(2)]; ub_t = tl(2)
        vb = [sbuf(sb_, f"vb{i}", [128, 4096], BF16) for i in range(2)]; vb_t = tl(2)
        scs = [sbuf(sb_, f"sc{j}", [128, 16, 128], F32) for j in range(2)]; scs_t = tl(2)
        v16 = sbuf(sb_, "v16", [128, 16, 16], F32); v16_t = Trk()
        ix = sbuf(sb_, "ix", [128, 16, 16], U32); ix_t = Trk()
        ixf = sbuf(sb_, "ixf", [128, 16, 16], F32); ixf_t = Trk()
        cv = sbuf(sb_, "cv", [128, 8, 16], F32); cv_t = Trk()
        ci = sbuf(sb_, "ci", [128, 8, 16], U32); ci_t = Trk()
        ai = sbuf(sb_, "ai", [128, 2, 128], U32); ai_t = Trk()
        af = sbuf(sb_, "af", [128, 2, 8, 16], F32); af_t = Trk()
        EEs = [sbuf(sb_, f"EE{j}", [128, 3, 128], F32) for j in range(2)]; EEs_t = tl(2)
        gs = sbuf(sb_, "gs", [128, 8], F32); gs_t = Trk()
        ETs = [sbuf(sb_, f"ETs{j}", [128, 3, 128], BF16) for j in range(2)]; ETs_t = tl(2)
        OH2 = [sbuf(sb_, f"OH2_{i}", [128, 128], BF16) for i in range(4)]; OH2_t = tl(4)
        OH1 = [sbuf(sb_, f"OH1_{i}", [128, 128], BF16) for i in range(4)]; OH1_t = tl(4)
        gl = [sbuf(sb_, f"gl{i}", [128, 256], BF16) for i in range(2)]; gl_t = tl(2)
        Am = [sbuf(sb_, f"Am{i}", [128, 256], BF16) for i in range(2)]; Am_t = tl(2)
        pTb1 = sbuf(sb_, "pTb", [128, 2, 256], BF16); pTb1_t = Trk()
        pTb = [pTb1, pTb1]; pTb_t = [pTb1_t, pTb1_t]
        gate = [sbuf(sb_, f"gate{i}", [128, 512], F32) for i in range(2)]; gate_t = tl(2)

        p.dma("sp", iota_f[:], a["iota"], w=[iota_t])
        p.op("dve", lambda e: e.tensor_copy(out=iota_b[:], in_=iota_f[:]), r=[iota_t], w=[iota_t])
        p.dma("sp", skTf[:], a["skT"], w=[sk_t])
        p.op("dve", lambda e: e.tensor_copy(out=skTb[:], in_=skTf[:]), r=[sk_t], w=[sk_t])
        p.dma("sp", bg_sb[:], a["bgd"], w=[bg_t])
        p.dma("sp", wple[:], a["b_wple"].rearrange("(k p) m -> p k m", p=128), r=scr["wple"], w=[wple_t])

        def b1_front(gi):
            row = gi * 256
            seq, tok0 = row // 2048, row % 2048
            q_ = gi % 2

            def wq_load(blk):
                p.dma("sp", wqs[blk % 3][:].rearrange("p k m -> p (k m)"), a["b_wq"][blk * 128:(blk + 1) * 128, :],
                      r=scr["wq"], w=[wqs_t[blk % 3]])

            wq_load(0)
            wq_load(1)
            for j in range(2):
                p.dma("sp", hst[:], h1d[row + j * 128:row + (j + 1) * 128, :], r=[h1_t[row // 128 + j]], w=[hst_t])
                yield
                yield
                p.op("act", lambda e: e.copy(out=h1b[:], in_=hst[:]), r=[hst_t], w=[h1b_t])
                yield
                for half in range(2):
                    bb = 6 + half
                    for k in range(4):
                        kk = half * 4 + k
                        p.op("pe", lambda e: e.transpose(out=bank_bf(bb)[:, k, :], in_=h1b[:, kk * 128:(kk + 1) * 128],
                                                         identity=a["identb"][:]), r=[h1b_t, a["identb_t"]], w=[PT[bb]])
                yield
                for half in range(2):
                    bb = 6 + half
                    p.op("act", lambda e: e.copy(out=h1T[gi % 3][:, half * 4:(half + 1) * 4, j * 128:(j + 1) * 128],
                                                 in_=bank_bf(bb)[:, 0:4, :]), r=[PT[bb]], w=[h1T_t[gi % 3]])
                yield
            pend = None
            for blk in range(8):
                i = blk % 3
                if blk + 2 < 8:
                    wq_load(blk + 2)
                for cc in range(2):
                    hc = blk * 2 + cc
                    b = 6 + hc % 2
                    for k in range(8):
                        p.op("pe", lambda e: e.matmul(out=PS[:, b, 0:256], lhsT=wqs[i][:, k, cc * 128:(cc + 1) * 128],
                                                      rhs=h1T[gi % 3][:, k, :], start=(k == 0), stop=(k == 7)),
                             r=[wqs_t[i], h1T_t[gi % 3]], w=[PT[b]])
                    if pend is not None:
                        hc0, b0 = pend
                        p.op("act", lambda e: e.copy(out=qTp[:, hc0, :], in_=PS[:, b0, 0:256]), r=[PT[b0]], w=[qTp_t])
                    pend = (hc, b)
                    yield
            hc0, b0 = pend
            p.op("act", lambda e: e.copy(out=qTp[:, hc0, :], in_=PS[:, b0, 0:256]), r=[PT[b0]], w=[qTp_t])
            yield
            pend = None
            for j in range(2):
                sc, sc_t = scs[j], scs_t[j]
                for qd in range(4):
                    b = 6 + qd % 2
                    for u in range(4):
                        hc = qd * 4 + u
                        half = hc % 2
                        p.op("pe", lambda e: e.matmul(out=PS[:, b, u * 128:(u + 1) * 128], lhsT=qTp[:, hc, j * 128:(j + 1) * 128],
                                                      rhs=skTb[:, half * 128:(half + 1) * 128], start=True, stop=True),
                             r=[qTp_t, sk_t], w=[PT[b]])
                    if pend is not None:
                        sc0, sct0, qd0, b0 = pend
                        p.op("act", lambda e: e.copy(out=sc0[:, 4 * qd0:4 * qd0 + 4, :].rearrange("p a b -> p (a b)"), in_=bank(b0)),
                             r=[PT[b0]], w=[sct0])
                    pend = (sc, sc_t, qd, b)
                    yield
            sc0, sct0, qd0, b0 = pend
            p.op("act", lambda e: e.copy(out=sc0[:, 4 * qd0:4 * qd0 + 4, :].rearrange("p a b -> p (a b)"), in_=bank(b0)),
                 r=[PT[b0]], w=[sct0])
            yield

        def b1_chains(gi):
            for j in range(2):
                sc, sc_t = scs[j], scs_t[j]
                oh, oh_t = sc[:].rearrange("p (h a) (b c) -> p h (a b) c", a=2, c=16), sc_t
                EE, EE_t = EEs[j], EEs_t[j]
                for gq in range(16):
                    for rnd in range(2):
                        vs = v16[:, gq, rnd * 8:(rnd + 1) * 8]
                        p.op("dve", lambda e: e.max(out=vs, in_=sc[:, gq, :]), r=[sc_t], w=[v16_t])
                        p.op("dve", lambda e: e.max_index(out=ix[:, gq, rnd * 8:(rnd + 1) * 8], in_max=vs, in_values=sc[:, gq, :]),
                             r=[sc_t, v16_t], w=[ix_t])
                        if rnd == 0:
                            p.op("dve", lambda e: e.match_replace(out=sc[:, gq, :], in_to_replace=vs, in_values=sc[:, gq, :],
                                                                  imm_value=-1e30), r=[v16_t, sc_t], w=[sc_t])
                    yield
                p.op("dve", lambda e: e.tensor_copy(out=ixf[:], in_=ix[:]), r=[ix_t], w=[ixf_t])
                v4 = v16[:].rearrange("p (h two) k -> p h two k", two=2)
                ix4 = ixf[:].rearrange("p (h two) k -> p h two k", two=2)
                cand4 = sc[:].rearrange("p (h a) (b c) -> p h (a b) c", a=2, c=16)
                candf = sc[:].rearrange("p (h a) b -> p h (a b)", a=2)
                p.op("dve", lambda e: e.tensor_tensor(out=cand4, in0=v4[:, :, 0, :].unsqueeze(3).broadcast_to([128, 8, 16, 16]),
                                                      in1=v4[:, :, 1, :].unsqueeze(2).broadcast_to([128, 8, 16, 16]), op=ALU.add),
                     r=[v16_t, sc_t], w=[sc_t])
                yield
                for h in range(8):
                    for rnd in range(2):
                        vs = cv[:, h, rnd * 8:(rnd + 1) * 8]
                        p.op("dve", lambda e: e.max(out=vs, in_=candf[:, h, :]), r=[sc_t], w=[cv_t])
                        p.op("dve", lambda e: e.max_index(out=ci[:, h, rnd * 8:(rnd + 1) * 8], in_max=vs, in_values=candf[:, h, :]),
                             r=[sc_t, cv_t], w=[ci_t])
                        if rnd == 0:
                            p.op("dve", lambda e: e.match_replace(out=candf[:, h, :], in_to_replace=vs, in_values=candf[:, h, :],
                                                                  imm_value=-1e30), r=[cv_t, sc_t], w=[sc_t])
                    yield
                gg3 = EE[:, 2, :].rearrange("p (h k) -> p h k", k=16)
                p.op("dve", lambda e: e.tensor_tensor(out=gg3, in0=cv[:], in1=cv[:, :, 0:1].broadcast_to([128, 8, 16]), op=ALU.subtract),
                     r=[cv_t], w=[EE_t])
                cif = ci[:].rearrange("p h k -> p (h k)")
                p.op("dve", lambda e: e.tensor_single_scalar(out=ai[:, 0, :], in_=cif, scalar=4, op=ALU.logical_shift_right),
                     r=[ci_t], w=[ai_t])
                p.op("dve", lambda e: e.tensor_single_scalar(out=ai[:, 1, :], in_=cif, scalar=15, op=ALU.bitwise_and),
                     r=[ci_t], w=[ai_t])
                p.op("dve", lambda e: e.tensor_copy(out=af[:].rearrange("p a h k -> p a (h k)"), in_=ai[:]), r=[ai_t], w=[af_t])
                yield
                for w_ in range(2):
                    p.op("dve", lambda e: e.tensor_tensor(
                        out=oh, in0=iota_f[:, 0:16].unsqueeze(1).unsqueeze(1).broadcast_to([128, 8, 16, 16]),
                        in1=af[:, w_, :, :].unsqueeze(3).broadcast_to([128, 8, 16, 16]), op=ALU.is_equal),
                        r=[iota_t, af_t], w=[oh_t])
                    yield
                    p.op("dve", lambda e: e.tensor_tensor(
                        out=oh, in0=oh, in1=ix4[:, :, w_, :].unsqueeze(2).broadcast_to([128, 8, 16, 16]), op=ALU.mult),
                        r=[oh_t, ixf_t], w=[oh_t])
                    yield
                    p.op("dve", lambda e: e.tensor_reduce(out=EE[:, 1 - w_, :].rearrange("p (h k) -> p h k", k=16), in_=oh,
                                                          axis=AX.X, op=ALU.add), r=[oh_t], w=[EE_t])
                    yield

        def b1_tail(gi):
            for _ in range(6):
                yield
            for j in range(2):
                EE, EE_t = EEs[j], EEs_t[j]
                p.op("act", lambda e: e.activation(out=EE[:, 2, :], in_=EE[:, 2, :], func=AF.Exp), r=[EE_t], w=[EE_t])
            yield
            yield
            for j in range(2):
                EE, EE_t = EEs[j], EEs_t[j]
                gg3 = EE[:, 2, :].rearrange("p (h k) -> p h k", k=16)
                p.op("dve", lambda e: e.tensor_reduce(out=gs[:], in_=gg3, axis=AX.X, op=ALU.add), r=[EE_t], w=[gs_t])
                p.op("dve", lambda e: e.reciprocal(out=gs[:], in_=gs[:]), r=[gs_t], w=[gs_t])
                p.op("dve", lambda e: e.tensor_tensor(out=gg3, in0=gg3, in1=gs[:].unsqueeze(2).broadcast_to([128, 8, 16]), op=ALU.mult),
                     r=[EE_t, gs_t], w=[EE_t])
            for _ in range(4):
                yield
            for j in range(2):
                EE, EE_t = EEs[j], EEs_t[j]
                bb = 6 + j
                for q in range(3):
                    p.op("pe", lambda e: e.transpose(out=PS[:, bb, q * 128:(q + 1) * 128], in_=EE[:, q, :], identity=identf[:]),
                         r=[EE_t, identf_t], w=[PT[bb]])
            yield
            yield
            for j in range(2):
                bb = 6 + j
                p.op("act", lambda e: e.copy(out=ETs[j][:].rearrange("p a b -> p (a b)"), in_=PS[:, bb, 0:384]), r=[PT[bb]], w=[ETs_t[j]])
            yield

        def b1e(gi):
            for j in range(2):
                for tt in range(128):
                    r4 = tt % 4
                    b = 4 + (tt // 4) % 2
                    p.op("dve", lambda e: e.tensor_scalar(out=OH2[r4][:], in0=iota_b[:], scalar1=ETs[j][:, 0, tt:tt + 1],
                                                          scalar2=None, op0=ALU.is_equal), r=[iota_t, ETs_t[j]], w=[OH2_t[r4]])
                    p.op("dve", lambda e: e.tensor_scalar(out=OH1[r4][:], in0=iota_b[:], scalar1=ETs[j][:, 1, tt:tt + 1],
                                                          scalar2=ETs[j][:, 2, tt:tt + 1], op0=ALU.is_equal, op1=ALU.mult),
                         r=[iota_t, ETs_t[j]], w=[OH1_t[r4]])
                    p.op("pe", lambda e: e.matmul(out=PS[:, b, r4 * 128:(r4 + 1) * 128], lhsT=OH2[r4][:], rhs=OH1[r4][:],
                                                  start=True, stop=True), r=[OH2_t[r4], OH1_t[r4]], w=[PT[b]])
                    if r4 == 3:
                        t0 = j * 128 + tt - 3
                        p.op("act", lambda e: e.copy(out=Gall[:, t0:t0 + 4, :].rearrange("p a b -> p (a b)"), in_=bank(b)),
                             r=[PT[b]], w=[G_t])
                        yield

        def b2_load(cb):
            i = cb % 2
            uv = ub[i][:].rearrange("p (c f) -> p c f", f=1024)
            vv = vb[i][:].rearrange("p (c f) -> p c f", f=1024)
            p.dma("sp", uv, a["b_u"][cb * 512:(cb + 1) * 512, :].rearrange("(c p) f -> p c f", p=128), r=scr["u"], w=[ub_t[i]])
            p.dma("sp", vv, a["b_v"][cb * 512:(cb + 1) * 512, :].rearrange("(c p) f -> p c f", p=128), r=scr["v"], w=[vb_t[i]])

        def b2(gi, preloaded):
            q_ = gi % 2

            def U(c):
                i = (c // 4) % 2
                uv = ub[i][:].rearrange("p (c f) -> p c f", f=1024)
                hb_ = 4 + c % 2
                for k in range(8):
                    p.op("pe", lambda e: e.matmul(out=PS[:, hb_, 0:256], lhsT=uv[:, c % 4, k * 128:(k + 1) * 128],
                                                  rhs=h1T[gi % 3][:, k, :], start=(k == 0), stop=(k == 7)),
                         r=[ub_t[i], h1T_t[gi % 3]], w=[PT[hb_]])

            if not preloaded:
                b2_load(0)
                b2_load(1)
            U(0)
            for c in range(128):
                if c + 1 < 128:
                    U(c + 1)
                i = (c // 4) % 2
                vv = vb[i][:].rearrange("p (c f) -> p c f", f=1024)
                hb_ = 4 + c % 2
                ci_ = c % 2
                p.op("act", lambda e: e.activation(out=gl[ci_][:], in_=PS[:, hb_, 0:256], func=AF.Gelu),
                     r=[PT[hb_]], w=[gl_t[ci_]])
                p.op("pool", lambda e: e.tensor_tensor(out=Am[ci_][:], in0=gl[ci_][:], in1=Gall[:, :, c], op=ALU.mult),
                     r=[gl_t[ci_], G_t], w=[Am_t[ci_]])
                for j in range(2):
                    for half in range(2):
                        ob = j * 2 + half
                        p.op("pe", lambda e: e.matmul(
                            out=bank(ob), lhsT=Am[ci_][:, j * 128:(j + 1) * 128], rhs=vv[:, c % 4, half * 512:(half + 1) * 512],
                            start=(c == 0), stop=(c == 127)), r=[Am_t[ci_], vb_t[i]], w=[PT[ob]])
                if c % 4 == 3:
                    if c // 4 + 2 < 32:
                        b2_load(c // 4 + 2)
                    elif gi + 1 < NG:
                        b2_load(c // 4 + 2 - 32)
                yield

        def b3_loads(gi):
            row = gi * 256
            seq, tok0 = row // 2048, row % 2048
            for j in range(2):
                p.dma("sp", h1t_[j][:], h1d[row + j * 128:row + (j + 1) * 128, :], r=[h1_t[row // 128 + j]], w=[h1tt_[j]])
            p.dma("pool", pTb1[:], pT[seq, :, tok0:tok0 + 256].rearrange("(k p) t -> p k t", p=128), w=[pTb1_t])
            yield

        def b3a(gi):
            q_ = gi % 2
            for j in range(2):
                for half in range(2):
                    hs = slice(half * 512, (half + 1) * 512)
                    ob = j * 2 + half
                    p.op("dve", lambda e: e.scalar_tensor_tensor(out=h1t[q_][j][:, hs], in0=h1t[q_][j][:, hs], scalar=ALPHA, in1=bank(ob),
                                                                 op0=ALU.mult, op1=ALU.add), r=[h1t_t[q_][j], PT[ob]], w=[h1t_t[q_][j]])

        def b3b(gi):
            row = gi * 256
            q_ = gi % 2

            def wg_load(qd):
                p.dma("sp", wqs[qd % 3][:].rearrange("p k m -> p (k m)"), a["b_wg"][qd * 128:(qd + 1) * 128, :],
                      r=scr["wg"], w=[wqs_t[qd % 3]])

            for qd in range(3):
                wg_load(qd)
            yield
            yield
            for qd in range(4):
                cs = slice(qd * 256, (qd + 1) * 256)
                wv = wqs[qd % 3]
                for j in range(2):
                    for k in range(8):
                        p.op("pe", lambda e: e.matmul(out=PS[:, 6, 0:256], lhsT=h1T[gi % 3][:, k, j * 128:(j + 1) * 128], rhs=wv[:, k, :],
                                                      start=(k == 0), stop=(k == 7)), r=[wqs_t[qd % 3], h1T_t[gi % 3]], w=[PT[6]])
                    for k in range(2):
                        p.op("pe", lambda e: e.matmul(out=PS[:, 7, 0:256], lhsT=pTb[q_][:, k, j * 128:(j + 1) * 128], rhs=wple[:, k, cs],
                                                      start=(k == 0), stop=(k == 1)), r=[pTb_t[q_], wple_t], w=[PT[7]])
                    yield
                    p.op("dve", lambda e: e.tensor_tensor(out=gate[j][:, 0:256], in0=PS[:, 6, 0:256], in1=bg_sb[:, cs], op=ALU.add),
                         r=[PT[6], bg_t], w=[gate_t[j]])
                    yield
                    p.op("act", lambda e: e.activation(out=gate[j][:, 0:256], in_=gate[j][:, 0:256], func=AF.Sigmoid),
                         r=[gate_t[j]], w=[gate_t[j]])
                    yield
                    p.op("dve", lambda e: e.tensor_tensor(out=gate[j][:, 0:256], in0=gate[j][:, 0:256], in1=PS[:, 7, 0:256], op=ALU.mult),
                         r=[gate_t[j], PT[7]], w=[gate_t[j]])
                    p.op("dve", lambda e: e.tensor_tensor(out=h1t[q_][j][:, cs], in0=h1t[q_][j][:, cs], in1=gate[j][:, 0:256], op=ALU.add),
                         r=[h1t_t[q_][j], gate_t[j]], w=[h1t_t[q_][j]])
                if qd == 0:
                    wg_load(3)
            yield
            for j in range(2):
                src, src_t = h1t[q_][j][:], h1t_t[q_][j]
                st6, st6_t, mv, mv_t, rstd, rstd_t = a["lnscr"]
                for c in range(2):
                    p.op("dve", lambda e: e.bn_stats(out=st6[:, c, :], in_=src[:, c * 512:(c + 1) * 512]), r=[src_t], w=[st6_t])
                p.op("dve", lambda e: e.bn_aggr(out=mv[:], in_=st6[:].rearrange("p a b -> p (a b)")), r=[st6_t], w=[mv_t])
                yield
                yield
                p.op("act", lambda e: e.activation(out=rstd[:], in_=mv[:, 1:2], func=AF.Sqrt, bias=EPS, scale=1.0), r=[mv_t], w=[rstd_t])
                yield
                LN = a["LN"]
                p.op("dve", lambda e: e.reciprocal(out=rstd[:], in_=rstd[:]), r=[rstd_t], w=[rstd_t])
                p.op("dve", lambda e: e.tensor_scalar(out=src, in0=src, scalar1=mv[:, 0:1], scalar2=rstd[:, 0:1],
                                                      op0=ALU.subtract, op1=ALU.mult), r=[src_t, mv_t, rstd_t], w=[src_t])
                p.op("dve", lambda e: e.tensor_tensor(out=src, in0=src, in1=LN["sb"][:, 4 - LN["base"], :], op=ALU.mult),
                     r=[src_t, LN["t"]], w=[src_t])
                p.op("dve", lambda e: e.tensor_tensor(out=src, in0=src, in1=LN["sb"][:, 5 - LN["base"], :], op=ALU.add),
                     r=[src_t, LN["t"]], w=[src_t])
                yield
            for _ in range(6):
                yield
            for j in range(2):
                p.dma("pool", out[row + j * 128:row + (j + 1) * 128, :], h1t[q_][j][:], r=[h1t_t[q_][j]])
            for _ in range(6):
                yield

        def run(gen):
            for _ in gen:
                pass

        def chain(*gens):
            for g_ in gens:
                if g_ is not None:
                    yield from g_

        def interleave(main, bg, n_main, n_bg):
            acc = 0.0
            alive = bg is not None
            for _ in main:
                acc += n_bg / n_main
                while alive and acc >= 1.0:
                    acc -= 1.0
                    try:
                        next(bg)
                    except StopIteration:
                        alive = False
            if alive:
                for _ in bg:
                    pass

        def merge(ga, gb):
            alive = [ga, gb]
            while alive:
                for g_ in list(alive):
                    if g_ is None:
                        alive.remove(g_)
                        continue
                    try:
                        yield next(g_)
                    except StopIteration:
                        alive.remove(g_)

        run(chain(b1_front(0), b1_chains(0), b1_tail(0), b3_loads(0)))
        run(b1e(0))
        for gi in range(NG):
            nxt = gi + 1 < NG
            bg = chain(b1_front(gi + 1) if nxt else None,
                       merge(b1_chains(gi + 1) if nxt else None, b3b(gi - 1) if gi > 0 else None),
                       b1_tail(gi + 1) if nxt else None,
                       b3_loads(gi) if gi > 0 else None)
            interleave(b2(gi, gi > 0), bg, 128, 150 if gi > 0 else 120)
            b3a(gi)
            if nxt:
                run(b1e(gi + 1))
        run(b3b(NG - 1))


def na_bias_tables(rpb):
    W, KH, KW, ROWS = 64, 8, 16, 32
    outs = []
    for r0 in (0, 2, 4, 28, 30):
        kt0 = na_kt0(r0)
        key_tok = kt0 * 128 + np.arange(640)
        krow, kcol = key_tok // W, key_tok % W
        q = np.arange(128)
        qrow, qcol = r0 + q // W, q % W
        rs = np.clip(qrow - KH // 2, 0, ROWS - KH)
        cs = np.clip(qcol - KW // 2, 0, W - KW)
        di = krow[:, None] - qrow[None, :] + (KH - 1)
        dj = kcol[:, None] - qcol[None, :] + (KW - 1)
        ok = ((krow[:, None] >= rs[None, :]) & (krow[:, None] < rs[None, :] + KH)
              & (kcol[:, None] >= cs[None, :]) & (kcol[:, None] < cs[None, :] + KW))
        dic = np.clip(di, 0, 14)
        djc = np.clip(dj, 0, 30)
        g = rpb[:, dic, djc]
        g = np.where(ok[None], g, np.float32(NEG)).astype(np.float32)
        outs.append(g.reshape(8, 5, 128, 128))
    return np.stack(outs, 0)


def rope_tables():
    t = np.arange(2048)
    row = (t // 64).astype(np.float32)
    col = (t % 64).astype(np.float32)
    inv = (10000.0 ** (-np.arange(0, 16, 2, dtype=np.float32) / 16)).astype(np.float32)
    ang = np.concatenate([row[:, None] * inv[None, :], col[:, None] * inv[None, :]], axis=-1)
    cos = np.cos(ang).astype(np.float32)
    sin = np.sin(ang).astype(np.float32)
    tab = np.zeros((2, 32, 2048), np.float32)
    tab[0] = np.repeat(cos, 2, axis=1).T
    sgn = np.tile(np.array([-1.0, 1.0], np.float32), 16)
    tab[1] = (np.repeat(sin, 2, axis=1) * sgn[None, :]).T
    return tab


def pair_swap_cols(w, cols):
    w2 = w.copy()
    w2[:, cols[0::2]] = w[:, cols[1::2]]
    w2[:, cols[1::2]] = w[:, cols[0::2]]
    return w2


def host_layout(inputs):
    f = lambda k: np.asarray(inputs[k], dtype=np.float32)
    w_in = f("w_in")[0]
    w_uq = f("w_uq")[0]
    sh = {}
    sh["lnp"] = np.ascontiguousarray(np.broadcast_to(np.stack(
        [f("emb_ln_g"), f("emb_ln_b"), f("ln1_g")[0], f("ln1_b")[0], f("ln2_g")[0], f("ln2_b")[0]], 0)[None], (128, 6, 1024)))
    sh["bg"] = np.ascontiguousarray(np.broadcast_to(f("ple_gate_b")[0][None], (128, 1024)))
    sh["ident"] = np.eye(128, dtype=np.float32)
    sh["iota"] = np.ascontiguousarray(np.broadcast_to(np.arange(128, dtype=np.float32)[None], (128, 128)))
    sh["rope"] = rope_tables()
    sh["w_in"] = np.ascontiguousarray(w_in)
    kr96 = w_in[:, 2112:2208]
    sh["w_kr"] = np.ascontiguousarray(np.concatenate([kr96, pair_swap_cols(kr96, np.arange(64, 96))], axis=1))
    rope_cols = np.concatenate([h * 96 + 64 + np.arange(32) for h in range(8)])
    sh["w_uq"] = np.ascontiguousarray(np.concatenate([w_uq, pair_swap_cols(w_uq, rope_cols)], axis=1))
    sh["w_ukv"] = np.ascontiguousarray(f("w_ukv")[0])
    sh["qg"] = np.ascontiguousarray(f("mla_q_norm_g")[0].reshape(3, 128).T)
    sh["kvg"] = np.ascontiguousarray(f("mla_kv_norm_g")[0].reshape(2, 128).T)
    sh["nab"] = np.ascontiguousarray(na_bias_tables(f("na_rpb")[0]).reshape(5 * 8 * 5 * 128, 128))
    sh["w_o"] = np.ascontiguousarray(f("w_o")[0])
    sh["w_q"] = np.ascontiguousarray(f("peer_w_q")[0].reshape(8, 128, 8, 256).transpose(2, 1, 0, 3)).reshape(1024, 2048)
    sk = f("peer_sub_keys")[0]
    sh["skT"] = np.ascontiguousarray(np.concatenate([sk[0].T, sk[1].T], axis=1))
    U = f("peer_u")[0]
    sh["u_l"] = np.ascontiguousarray(U.reshape(128, 128, 8, 128).transpose(0, 3, 2, 1)).reshape(16384, 1024)
    sh["v_l"] = np.ascontiguousarray(f("peer_v")[0])
    sh["w_ple"] = np.ascontiguousarray(f("ple_w")[0])
    sh["w_g"] = np.ascontiguousarray(f("ple_gate_w")[0].reshape(8, 128, 4, 256).transpose(2, 1, 0, 3)).reshape(512, 2048)
    return sh


def kernel(**inputs):
    sh = host_layout(inputs)
    x = np.asarray(inputs["x"], dtype=np.float32)
    pp = np.asarray(inputs["p"], dtype=np.float32)[0]
    in_maps = []
    for c in range(8):
        m = dict(sh)
        m["x"] = np.ascontiguousarray(x[2 * c:2 * c + 2].reshape(4096, 1024))
        m["pT"] = np.ascontiguousarray(pp[2 * c:2 * c + 2].transpose(0, 2, 1))
        in_maps.append(m)
    nc = build()
    res = run_bass_kernel_spmd(nc, in_maps, core_ids=list(range(8)))
    return np.concatenate([r["out"].reshape(2, 2048, 1024) for r in res.results], axis=0)
```

```python
import numpy as np
import concourse.bass as bass
import concourse.mybir as mybir
from concourse.bass_utils import run_bass_kernel_spmd
from contextlib import ExitStack

F32 = mybir.dt.float32
BF16 = mybir.dt.bfloat16
U32 = mybir.dt.uint32
ALU = mybir.AluOpType
AF = mybir.ActivationFunctionType
AX = mybir.AxisListType

ALPHA = float(2.0 ** 0.25)
EPS = 1e-5
NEG = -30000.0


class Trk:
    __slots__ = ("name", "w", "r")

    def __init__(self, name=""):
        self.name = name
        self.w = None
        self.r = {}


class _Rec:
    def __getattr__(self, name):
        def f(*args, **kw):
            self.call = (name, args, kw)
        return f


class Prog:
    ENG = ("pe", "act", "dve", "pool", "sp")

    def __init__(self, nc, es):
        self.nc = nc
        self.es = es
        self.ops = {e: [] for e in self.ENG}
        self.cnt = {}
        self.sem = {}
        self.seen = {e: {} for e in self.ENG}
        for e in self.ENG:
            self._mksem(e)
        self.ndma = {e: 0 for e in self.ENG}

    def _mksem(self, key):
        self.sem[key] = self.es.enter_context(self.nc.semaphore(name=f"s_{key}"))
        self.cnt[key] = 0

    def _deps(self, eng, r, w):
        deps = {}
        for t in r:
            if t.w is not None and t.w[1] > deps.get(t.w[0], 0):
                deps[t.w[0]] = t.w[1]
        for t in w:
            if t.w is not None and t.w[1] > deps.get(t.w[0], 0):
                deps[t.w[0]] = t.w[1]
            for k, v in t.r.items():
                if v > deps.get(k, 0):
                    deps[k] = v
        waits = []
        for k, v in deps.items():
            if k == eng and eng == "pe":
                continue
            if self.seen[eng].get(k, 0) >= v:
                continue
            self.seen[eng][k] = v
            waits.append((k, v))
        return waits

    def op(self, eng, fn, r=(), w=()):
        rec = _Rec()
        fn(rec)
        call = rec.call
        fn = lambda e, call=call: getattr(e, call[0])(*call[1], **call[2])
        waits = self._deps(eng, r, w)
        self.cnt[eng] += 1
        n = self.cnt[eng]
        self.ops[eng].append((fn, waits, eng, 1))
        for t in r:
            t.r[eng] = n
        for t in w:
            t.w = (eng, n)
            t.r = {}

    def dma(self, q, out, in_, r=(), w=(), **kw):
        waits = self._deps(q, r, w)
        key = f"d{q}{self.ndma[q] % (8 if q == 'pool' else 24)}"
        self.ndma[q] += 1
        if key not in self.sem:
            self._mksem(key)
        prev = self.cnt[key]
        if prev and self.seen[q].get(key, 0) < prev:
            self.seen[q][key] = prev
            waits.append((key, prev))
        self.cnt[key] += 16
        n = self.cnt[key]
        self.ops[q].append((lambda e: e.dma_start(out=out, in_=in_, **kw), waits, key, 16))
        for t in r:
            t.r[key] = n
        for t in w:
            t.w = (key, n)
            t.r = {}

    def barrier(self):
        snap = dict(self.cnt)
        for e in self.ENG:
            waits = []
            for k, v in snap.items():
                if v and k != e and self.seen[e].get(k, 0) < v:
                    self.seen[e][k] = v
                    waits.append((k, v))
            if waits:
                self.ops[e].append((None, waits, None, 0))

    def finish(self):
        fin = [(k, v) for k, v in self.cnt.items() if k.startswith("d") and k not in self.ENG and v > 0]
        nc = self.nc
        names = {"pe": "tensor", "act": "scalar", "dve": "vector", "pool": "gpsimd", "sp": "sync"}
        with nc.Block() as block:
            for e in self.ENG:
                def body(eng, e=e):
                    for fn, waits, key, amt in self.ops[e]:
                        for k, v in waits:
                            eng.wait_ge(self.sem[k], v)
                        if fn is not None:
                            ins = fn(eng)
                            ins.then_inc(self.sem[key], amt)
                    if e == "sp":
                        for k, v in fin:
                            eng.wait_ge(self.sem[k], v)
                getattr(block, names[e])(body)


def tl(n, name=""):
    return [Trk(f"{name}{i}") for i in range(n)]


def build(nseq=2, do_b=True, debug=False):
    nc = bass.Bass("TRN2", target_bir_lowering=False)
    NT = nseq * 2048

    def din(name, shape, dt=F32):
        return nc.dram_tensor(name, list(shape), dt, kind="ExternalInput").ap()

    def dscr(name, shape, dt):
        return nc.dram_tensor(name, list(shape), dt, kind="Internal").ap()

    x = din("x", [4096, 1024])
    pT = din("pT", [2, 256, 2048])
    lnp = din("lnp", [128, 6, 1024])
    bgd = din("bg", [128, 1024])
    ident = din("ident", [128, 128])
    iota = din("iota", [128, 128])
    rope = din("rope", [2, 32, 2048])
    w_in = din("w_in", [1024, 2208])
    w_kr = din("w_kr", [1024, 2 * 96])
    w_uq = din("w_uq", [384, 2 * 768])
    w_ukv = din("w_ukv", [256, 1024])
    qg = din("qg", [128, 3])
    kvg = din("kvg", [128, 2])
    nab = din("nab", [5 * 8 * 5 * 128, 128])
    w_o = din("w_o", [1024, 1024])
    w_q = din("w_q", [1024, 2048])
    skT = din("skT", [128, 256])
    u_l = din("u_l", [16384, 1024])
    v_l = din("v_l", [16384, 1024])
    w_ple = din("w_ple", [256, 1024])
    w_g = din("w_g", [512, 2048])
    out = nc.dram_tensor("out", [4096, 1024], F32, kind="ExternalOutput").ap()
    if debug:
        h1d = nc.dram_tensor("h1d", [4096, 1024], F32, kind="ExternalOutput").ap()
        catd = nc.dram_tensor("catd", [4096, 1024], F32, kind="ExternalOutput").ap()
    else:
        h1d = dscr("h1d", [4096, 1024], F32)
        catd = None
    h0d = dscr("h0d", [4096, 1024], F32)
    h0Td = dscr("h0Td", [2 * 4 * 128, 4096], BF16)
    b_win = dscr("b_win", [1024, 2208], BF16)
    b_wkr = dscr("b_wkr", [1024, 192], BF16)
    b_wuq = dscr("b_wuq", [384, 1536], BF16)
    b_wukv = dscr("b_wukv", [256, 1024], BF16)
    b_nab = dscr("b_nab", [5 * 8 * 5 * 128, 128], BF16)
    b_wo = dscr("b_wo", [1024, 1024], BF16)
    b_wq = dscr("b_wq", [1024, 2048], BF16)
    b_u = dscr("b_u", [16384, 1024], BF16)
    b_v = dscr("b_v", [16384, 1024], BF16)
    b_wple = dscr("b_wple", [256, 1024], BF16)
    b_wg = dscr("b_wg", [512, 2048], BF16)

    with ExitStack() as es:
        p = Prog(nc, es)

        uniq = [0]

        def sbuf(st, name, shape, dt):
            uniq[0] += 1
            return st.enter_context(nc.sbuf_tensor(f"{name}_{uniq[0]}", list(shape), dt))

        PS = es.enter_context(nc.psum_tensor("ps", [128, 8, 512], F32))
        PT = tl(8, "bank")

        def bank(i):
            return PS[:, i, :]

        def bank_bf(i):
            return PS[:, i, 0:256].bitcast(BF16).rearrange("p (a b) -> p a b", b=128)

        scr = {}

        def cast_dram(dst, src, rows, name, rows_per=512):
            scr[name] = []
            a = rows_per // 128
            dv = dst.rearrange("(n p a) m -> n p a m", p=128, a=a)
            sv = src.rearrange("(n p a) m -> n p a m", p=128, a=a)
            for i in range(rows // rows_per):
                t = Trk(name)
                scr[name].append(t)
                p.dma("pool", dv[i], sv[i], w=[t])

        cast_dram(b_win[:, 0:1104], w_in[:, 0:1104], 1024, "win", 512)
        win0 = scr["win"]
        cast_dram(b_win[:, 1104:2208], w_in[:, 1104:2208], 1024, "win", 512)
        scr["win"] = win0 + scr["win"]
        cast_dram(b_wkr, w_kr, 1024, "wkr", 1024)
        cast_dram(b_wuq, w_uq, 384, "wuq", 128)
        cast_dram(b_wukv, w_ukv, 256, "wukv", 256)
        cast_dram(b_nab, nab, 5 * 8 * 5 * 128, "nab", 1280)
        cast_dram(b_wo, w_o, 1024, "wo", 512)
        def cast_gen():
            for dst, src, rows, name, per in ((b_wq, w_q, 1024, "wq", 256), (b_wg, w_g, 512, "wg", 256),
                                              (b_wple, w_ple, 256, "wple", 256), (b_u, u_l, 16384, "u", 512),
                                              (b_v, v_l, 16384, "v", 512)):
                scr[name] = []
                a_ = per // 128
                dv = dst.rearrange("(n p a) m -> n p a m", p=128, a=a_)
                sv = src.rearrange("(n p a) m -> n p a m", p=128, a=a_)
                for i in range(rows // per):
                    t = Trk(name)
                    scr[name].append(t)
                    dep = yield
                    p.dma("pool", dv[i], sv[i], r=([dep] if dep is not None else []), w=[t])

        cg = [cast_gen() if do_b else None]
        if do_b:
            next(cg[0])

        def advance_casts(n, dep=None):
            for _ in range(n):
                if cg[0] is None:
                    return
                try:
                    cg[0].send(dep)
                except StopIteration:
                    cg[0] = None

        cs = es
        identf = sbuf(cs, "identf", [128, 128], F32); identf_t = Trk()
        identb = sbuf(cs, "identb", [128, 128], BF16); identb_t = Trk()
        onesb = sbuf(cs, "onesb", [128, 128], BF16); onesb_t = Trk()
        LN = {}
        p.dma("sp", identf[:], ident, w=[identf_t])

        def load_lnp(st, lo, hi):
            LN["sb"] = sbuf(st, "lnp_sb", [128, hi - lo, 1024], F32)
            LN["t"] = Trk()
            LN["base"] = lo
            p.dma("sp", LN["sb"][:], lnp[:, lo:hi, :], w=[LN["t"]])
        p.op("dve", lambda e: e.tensor_copy(out=identb[:], in_=identf[:]), r=[identf_t], w=[identb_t])
        p.op("dve", lambda e: e.memset(onesb[:], 1.0), w=[onesb_t])
        st6 = sbuf(cs, "st6", [128, 2, 6], F32); st6_t = Trk()
        mv = sbuf(cs, "mv", [128, 2], F32); mv_t = Trk()
        rstd = sbuf(cs, "rstd", [128, 1], F32); rstd_t = Trk()

        def layer_norm(src, src_t, gi, dst=None, dst_t=None):
            if dst is None:
                dst, dst_t = src, src_t
            for c in range(2):
                p.op("dve", lambda e, c=c: e.bn_stats(out=st6[:, c, :], in_=src[:, c * 512:(c + 1) * 512]),
                     r=[src_t], w=[st6_t])
            p.op("dve", lambda e: e.bn_aggr(out=mv[:], in_=st6[:].rearrange("p a b -> p (a b)")), r=[st6_t], w=[mv_t])
            p.op("act", lambda e: e.activation(out=rstd[:], in_=mv[:, 1:2], func=AF.Sqrt, bias=EPS, scale=1.0),
                 r=[mv_t], w=[rstd_t])
            p.op("dve", lambda e: e.reciprocal(out=rstd[:], in_=rstd[:]), r=[rstd_t], w=[rstd_t])
            p.op("dve", lambda e: e.tensor_scalar(out=dst, in0=src, scalar1=mv[:, 0:1], scalar2=rstd[:, 0:1],
                                                  op0=ALU.subtract, op1=ALU.mult), r=[src_t, mv_t, rstd_t], w=[dst_t])
            gi -= LN["base"]
            p.op("dve", lambda e: e.tensor_tensor(out=dst, in0=dst, in1=LN["sb"][:, gi, :], op=ALU.mult),
                 r=[dst_t, LN["t"]], w=[dst_t])
            p.op("dve", lambda e: e.tensor_tensor(out=dst, in0=dst, in1=LN["sb"][:, gi + 1, :], op=ALU.add),
                 r=[dst_t, LN["t"]], w=[dst_t])

        def transpose_to(src_bf, src_t, dst_fn, dst_t, nchunk, banks=(6, 7), eng="dve"):
            for i, c0 in enumerate(range(0, nchunk, 4)):
                b = banks[i % len(banks)]
                n = min(4, nchunk - c0)
                for k in range(n):
                    p.op("pe", lambda e, k=k, c0=c0, b=b: e.transpose(out=bank_bf(b)[:, k, :],
                                                                      in_=src_bf[:, (c0 + k) * 128:(c0 + k + 1) * 128],
                                                                      identity=identb[:]),
                         r=[src_t, identb_t], w=[PT[b]])
                if eng == "dve":
                    p.op("dve", lambda e, c0=c0, n=n, b=b: e.tensor_copy(out=dst_fn(c0, n), in_=bank_bf(b)[:, 0:n, :]),
                         r=[PT[b]], w=[dst_t])
                else:
                    p.op("act", lambda e, c0=c0, n=n, b=b: e.copy(out=dst_fn(c0, n), in_=bank_bf(b)[:, 0:n, :]),
                         r=[PT[b]], w=[dst_t])

        h1_t = tl(32, "h1d")
        h0d_t = tl(32, "h0d")
        h0Td_t = tl(8, "h0Td")
        phA = ExitStack()
        load_lnp(phA, 0, 4)
        for s in range(nseq):
            with ExitStack() as ss:
                cat = sbuf(ss, "cat", [128, 16, 1024], BF16)
                cat_t = tl(16, "cat")
                xt = [sbuf(ss, f"xt{i}", [128, 1024], F32) for i in range(2)]
                xt_t = tl(2, "xt")
                hb = sbuf(ss, "hb", [128, 1024], BF16); hb_t = Trk()
                h0T = sbuf(ss, "h0T", [128, 8, 512], BF16); h0T_t = Trk()
                xcnt = [0]

                def load_h0T(g):
                    for j in range(4):
                        i = xcnt[0] % 2
                        xcnt[0] += 1
                        row = s * 2048 + g * 512 + j * 128
                        p.dma("sp", xt[i][:], x[row:row + 128, :], w=[xt_t[i]])
                        layer_norm(xt[i][:], xt_t[i], 0)
                        p.dma("pool", h0d[row:row + 128, :], xt[i][:], r=[xt_t[i]], w=[h0d_t[row // 128]])
                        p.op("act", lambda e, i=i: e.copy(out=hb[:], in_=xt[i][:]), r=[xt_t[i]], w=[hb_t])
                        transpose_to(hb, hb_t, lambda c0, n, j=j: h0T[:, c0:c0 + n, j * 128:(j + 1) * 128], h0T_t, 8)
                    sg = s * 4 + g
                    p.dma("pool", h0Td[sg * 128:(sg + 1) * 128, :], h0T[:].rearrange("p k t -> p (k t)"), r=[h0T_t], w=[h0Td_t[sg]])

                def load_h0T_scratch(g):
                    sg = s * 4 + g
                    p.dma("sp", h0T[:].rearrange("p k t -> p (k t)"), h0Td[sg * 128:(sg + 1) * 128, :], r=[h0Td_t[sg]], w=[h0T_t])

                with ExitStack() as sa:
                    wna = sbuf(sa, "wna", [128, 8, 1536], BF16); wna_t = Trk()
                    qTm = [sbuf(sa, f"qTm{i}", [128, 4, 2048], BF16) for i in range(2)]; qT_t = Trk()
                    p.op("dve", lambda e: e.memset(qTm[0][64:128, :, :].rearrange("p a b -> p (a b)"), 0.0), w=[qT_t])
                    p.op("dve", lambda e: e.memset(qTm[1][0:64, :, :].rearrange("p a b -> p (a b)"), 0.0), w=[qT_t])
                    kT = sbuf(sa, "kT", [128, 4, 2048], BF16); kT_t = Trk()
                    vna = sbuf(sa, "vna", [128, 16, 8, 65], BF16); vna_t = Trk()
                    bias = [sbuf(sa, f"nbias{i}", [128, 40, 128], BF16) for i in range(2)]
                    bias_t = tl(2, "nbias")
                    PTn = [sbuf(sa, f"PTn{i}", [128, 5, 128], BF16) for i in range(2)]
                    PTn_t = tl(2, "PTn")
                    rc = sbuf(sa, "rc", [128, 8], F32); rc_t = Trk()
                    p.dma("sp", wna[:], b_win[:, 0:1536].rearrange("(k p) m -> p k m", p=128), r=scr["win"], w=[wna_t])
                    p.op("dve", lambda e: e.memset(vna[:, :, :, 64:65], 1.0), w=[vna_t])
                    for g in range(4):
                        load_h0T(g)
                        for c in range(8):
                            b = 4 + (c % 2)
                            for k in range(8):
                                p.op("pe", lambda e, c=c, k=k, b=b: e.matmul(out=bank(b), lhsT=wna[:, k, c * 128:(c + 1) * 128],
                                                                             rhs=h0T[:, k, :], start=(k == 0), stop=(k == 7)),
                                     r=[wna_t, h0T_t], w=[PT[b]])
                            if c < 4:
                                p.op("act", lambda e, c=c, b=b, g=g: e.mul(out=qTm[0][0:64, c, g * 512:(g + 1) * 512], in_=PS[0:64, b, :], mul=0.125),
                                     r=[PT[b]], w=[qT_t])
                                p.op("act", lambda e, c=c, b=b, g=g: e.mul(out=qTm[1][64:128, c, g * 512:(g + 1) * 512], in_=PS[64:128, b, :], mul=0.125),
                                     r=[PT[b]], w=[qT_t])
                            else:
                                p.op("act", lambda e, c=c, b=b, g=g: e.copy(out=kT[:, c - 4, g * 512:(g + 1) * 512], in_=bank(b)),
                                     r=[PT[b]], w=[kT_t])
                        for j in range(4):
                            b = 4 + (j % 2)
                            for k in range(8):
                                p.op("pe", lambda e, j=j, k=k, b=b: e.matmul(out=bank(b), lhsT=h0T[:, k, j * 128:(j + 1) * 128],
                                                                             rhs=wna[:, k, 1024:1536], start=(k == 0), stop=(k == 7)),
                                     r=[wna_t, h0T_t], w=[PT[b]])
                            p.op("dve", lambda e, j=j, b=b, g=g: e.tensor_copy(out=vna[:, g * 4 + j, :, 0:64],
                                                                               in_=bank(b).rearrange("p (h d) -> p h d", d=64)),
                                 r=[PT[b]], w=[vna_t])
                    def na_info(ti):
                        r0 = 2 * ti
                        return {0: 0, 2: 1, 28: 3, 30: 4}.get(r0, 2), na_kt0(r0)

                    def na_bias_load(ti):
                        typ, _ = na_info(ti)
                        bi = ti % 2
                        p.dma("sp", bias[bi][:], b_nab[typ * 5120:(typ + 1) * 5120, :].rearrange("(a p) q -> p a q", p=128),
                              r=scr["nab"], w=[bias_t[bi]])

                    def na_S(step):
                        ti, h = step // 8, step % 8
                        _, kt0 = na_info(ti)
                        bi = ti % 2
                        pr, po = h // 2, (h % 2) * 64
                        sb_ = 2 * (step % 2)
                        for kc in range(5):
                            bk = sb_ + (kc // 4)
                            oap = PS[:, bk, (kc % 4) * 128:(kc % 4 + 1) * 128]
                            p.op("pe", lambda e: e.matmul(
                                out=oap, lhsT=kT[:, pr, (kt0 + kc) * 128:(kt0 + kc + 1) * 128],
                                rhs=qTm[h % 2][:, pr, ti * 128:(ti + 1) * 128], start=True, stop=False),
                                r=[kT_t, qT_t], w=[PT[bk]])
                            p.op("pe", lambda e: e.matmul(
                                out=oap, lhsT=identb[:], rhs=bias[bi][:, h * 5 + kc, :], start=False, stop=True),
                                r=[identb_t, bias_t[bi]], w=[PT[bk]])

                    na_bias_load(0)
                    na_bias_load(1)
                    na_S(0)
                    for step in range(128):
                        ti, h = step // 8, step % 8
                        _, kt0 = na_info(ti)
                        if step + 1 < 128:
                            na_S(step + 1)
                        sb_ = 2 * (step % 2)
                        pi = step % 2
                        p.op("act", lambda e: e.activation(
                            out=PTn[pi][:].rearrange("p a b -> p (a b)"),
                            in_=PS[:, sb_:sb_ + 2, :].rearrange("p a b -> p (a b)")[:, 0:640], func=AF.Exp),
                            r=[PT[sb_], PT[sb_ + 1]], w=[PTn_t[pi]])
                        ob0 = 4 + 2 * (ti % 2)
                        ob = ob0 + h // 4
                        oo = (h % 4) * 65
                        for kc in range(5):
                            p.op("pe", lambda e: e.matmul(
                                out=PS[:, ob, oo:oo + 65], lhsT=PTn[pi][:, kc, :], rhs=vna[:, kt0 + kc, h, :],
                                start=(kc == 0), stop=(kc == 4)),
                                r=[PTn_t[pi], vna_t], w=[PT[ob]])
                        if h == 7:
                            if ti + 2 < 16:
                                na_bias_load(ti + 2)
                            tick = Trk()
                            for ob in (ob0, ob0 + 1):
                                o4 = PS[:, ob, 0:260].rearrange("p (h d) -> p h d", d=65)
                                hh = (ob - ob0) * 4
                                p.op("dve", lambda e: e.reciprocal(out=rc[:, hh:hh + 4], in_=o4[:, :, 64]),
                                     r=[PT[ob]], w=[rc_t])
                                p.op("dve", lambda e: e.tensor_tensor(
                                    out=cat[:, ti, hh * 64:(hh + 4) * 64].rearrange("p (h d) -> p h d", d=64),
                                    in0=o4[:, :, 0:64], in1=rc[:, hh:hh + 4].unsqueeze(2).broadcast_to([128, 4, 64]), op=ALU.mult),
                                    r=[PT[ob], rc_t], w=[cat_t[ti], tick])
                            advance_casts(2, tick)
                p.barrier()

                with ExitStack() as sa:
                    wcq = sbuf(sa, "wcq", [128, 8, 640], BF16); wcq_t = Trk()
                    wkr = sbuf(sa, "wkr", [128, 8, 192], BF16); wkr_t = Trk()
                    wuq = sbuf(sa, "wuq", [128, 3, 1536], BF16); wuq_t = Trk()
                    wukv = sbuf(sa, "wukv", [128, 2, 1024], BF16); wukv_t = Trk()
                    qg_sb = sbuf(sa, "qg_sb", [128, 3], F32); kvg_sb = sbuf(sa, "kvg_sb", [128, 2], F32); g_t = Trk()
                    cqT = sbuf(sa, "cqT", [128, 3, 2048], BF16); cqT_t = Trk()
                    Rq = sbuf(sa, "Rq", [128, 2048], F32); Rq_t = Trk()
                    KT = sbuf(sa, "KT", [96, 8, 2048], BF16); KT_t = Trk()
                    V = sbuf(sa, "Vm", [128, 16, 8, 65], BF16); V_t = Trk()
                    Qg = sbuf(sa, "Qg", [96, 8, 512], BF16); Qg_t = Trk()
                    ckv = sbuf(sa, "ckv", [128, 2, 512], BF16); ckv_t = Trk()
                    sq = sbuf(sa, "sq", [128, 512], BF16); sq_t = Trk()
                    Rkv = sbuf(sa, "Rkv", [128, 512], F32); Rkv_t = Trk()
                    rcol = sbuf(sa, "rcol", [128, 1], F32); rcol_t = Trk()
                    tab = sbuf(sa, "tab", [96, 2, 512], F32); tab_t = Trk()
                    c1 = sbuf(sa, "c1", [96, 2, 512], F32); c1_t = Trk()
                    t1 = sbuf(sa, "t1", [96, 512], F32); t1_t = Trk()
                    t2 = sbuf(sa, "t2", [96, 512], F32); t2_t = Trk()
                    PTm = [sbuf(sa, f"PTm{i}", [128, 512], BF16) for i in range(2)]
                    PTm_t = tl(2, "PTm")
                    rc4 = sbuf(sa, "rc4", [128, 4], F32); rc4_t = Trk()
                    p.dma("sp", wcq[:], b_win[:, 1536:2176].rearrange("(k p) m -> p k m", p=128), r=scr["win"], w=[wcq_t])
                    p.dma("sp", wkr[:], b_wkr.rearrange("(k p) m -> p k m", p=128), r=scr["wkr"], w=[wkr_t])
                    p.dma("sp", wuq[:], b_wuq.rearrange("(k p) m -> p k m", p=128), r=scr["wuq"], w=[wuq_t])
                    p.dma("sp", wukv[:], b_wukv.rearrange("(k p) m -> p k m", p=128), r=scr["wukv"], w=[wukv_t])
                    p.dma("sp", qg_sb[:], qg, w=[g_t])
                    p.dma("sp", kvg_sb[:], kvg, w=[g_t])
                    for c in range(3):
                        p.op("dve", lambda e, c=c: e.tensor_scalar(out=wuq[:, c, :], in0=wuq[:, c, :], scalar1=qg_sb[:, c:c + 1],
                                                                   scalar2=None, op0=ALU.mult), r=[wuq_t, g_t], w=[wuq_t])
                    for c in range(2):
                        p.op("dve", lambda e, c=c: e.tensor_scalar(out=wukv[:, c, :], in0=wukv[:, c, :], scalar1=kvg_sb[:, c:c + 1],
                                                                   scalar2=None, op0=ALU.mult), r=[wukv_t, g_t], w=[wukv_t])
                    p.op("dve", lambda e: e.memset(V[:, :, :, 64:65], 1.0), w=[V_t])

                    def rms_bcast(src_fn, nchunk, nfeat, dst, dst_t, extra):
                        for c in range(nchunk):
                            p.op("dve", lambda e, c=c: e.tensor_tensor(out=sq[:], in0=src_fn(c), in1=src_fn(c), op=ALU.mult),
                                 r=[cqT_t, ckv_t], w=[sq_t])
                            p.op("pe", lambda e, c=c: e.matmul(out=bank(3), lhsT=onesb[:], rhs=sq[:], start=(c == 0),
                                                               stop=(c == nchunk - 1)), r=[onesb_t, sq_t], w=[PT[3]])
                        p.op("act", lambda e: e.activation(out=dst, in_=bank(3), func=AF.Sqrt, bias=EPS, scale=1.0 / nfeat),
                             r=[PT[3]], w=[dst_t])
                        p.op("dve", lambda e: e.reciprocal(out=dst, in_=dst), r=[dst_t], w=[dst_t])
                        if extra != 1.0:
                            p.op("dve", lambda e: e.tensor_scalar(out=dst, in0=dst, scalar1=extra, scalar2=None, op0=ALU.mult),
                                 r=[dst_t], w=[dst_t])

                    def load_tab(g):
                        p.dma("sp", tab[64:96, :, :], rope[:, :, g * 512:(g + 1) * 512].rearrange("a f t -> f a t"), w=[tab_t])

                    for g in range(4):
                        gs = slice(g * 512, (g + 1) * 512)
                        load_h0T_scratch(g)
                        load_tab(g)
                        for c in range(5):
                            b = 4 + (c % 2)
                            for k in range(8):
                                p.op("pe", lambda e, c=c, k=k, b=b: e.matmul(out=bank(b), lhsT=wcq[:, k, c * 128:(c + 1) * 128],
                                                                             rhs=h0T[:, k, :], start=(k == 0), stop=(k == 7)),
                                     r=[wcq_t, h0T_t], w=[PT[b]])
                            if c < 3:
                                p.op("act", lambda e, c=c, b=b: e.copy(out=cqT[:, c, gs], in_=bank(b)), r=[PT[b]], w=[cqT_t])
                            else:
                                p.op("act", lambda e, c=c, b=b: e.copy(out=ckv[:, c - 3, :], in_=bank(b)), r=[PT[b]], w=[ckv_t])
                        rms_bcast(lambda c: cqT[:, c, gs], 3, 384.0, Rq[:, gs], Rq_t, 96.0 ** -0.5)
                        rms_bcast(lambda c: ckv[:, c, :], 2, 256.0, Rkv[:], Rkv_t, 1.0)
                        for h in range(8):
                            b = 4 + (h % 2)
                            for c in range(2):
                                p.op("pe", lambda e, h=h, c=c, b=b: e.matmul(out=PS[0:64, b, :], lhsT=wukv[:, c, h * 128:h * 128 + 64],
                                                                             rhs=ckv[:, c, :], start=(c == 0), stop=(c == 1)),
                                     r=[wukv_t, ckv_t], w=[PT[b]])
                            p.op("dve", lambda e, h=h, b=b: e.tensor_tensor(out=KT[0:64, h, gs], in0=PS[0:64, b, :], in1=Rkv[0:64, :],
                                                                            op=ALU.mult), r=[PT[b], Rkv_t], w=[KT_t])
                        for v_ in range(2):
                            b = 4 + v_
                            for k in range(8):
                                p.op("pe", lambda e, v_=v_, k=k, b=b: e.matmul(out=PS[0:96, b, :], lhsT=wkr[:, k, v_ * 96:(v_ + 1) * 96],
                                                                               rhs=h0T[:, k, :], start=(k == 0), stop=(k == 7)),
                                     r=[wkr_t, h0T_t], w=[PT[b]])
                        p.op("dve", lambda e: e.tensor_tensor(out=t1[64:96, :], in0=PS[64:96, 4, :], in1=tab[64:96, 0, :], op=ALU.mult),
                             r=[PT[4], tab_t], w=[t1_t])
                        p.op("dve", lambda e: e.tensor_tensor(out=t2[64:96, :], in0=PS[64:96, 5, :], in1=tab[64:96, 1, :], op=ALU.mult),
                             r=[PT[5], tab_t], w=[t2_t])
                        p.op("dve", lambda e: e.tensor_tensor(out=t1[64:96, :], in0=t1[64:96, :], in1=t2[64:96, :], op=ALU.add),
                             r=[t1_t, t2_t], w=[t1_t])
                        for h in range(8):
                            p.op("act", lambda e, h=h: e.copy(out=KT[64:96, h, gs], in_=t1[64:96, :]), r=[t1_t], w=[KT_t])
                        for j in range(4):
                            p.op("pe", lambda e, j=j: e.transpose(out=PS[:, 6, 0:128], in_=Rkv[:, j * 128:(j + 1) * 128], identity=identf[:]),
                                 r=[Rkv_t, identf_t], w=[PT[6]])
                            p.op("dve", lambda e: e.tensor_copy(out=rcol[:], in_=PS[:, 6, 0:1]), r=[PT[6]], w=[rcol_t])
                            b = 4 + (j % 2)
                            for c in range(2):
                                p.op("pe", lambda e, j=j, c=c, b=b: e.matmul(
                                    out=bank(b), lhsT=ckv[:, c, j * 128:(j + 1) * 128],
                                    rhs=wukv[:, c, :].rearrange("p (h d) -> p h d", d=128)[:, :, 64:128],
                                    start=(c == 0), stop=(c == 1)), r=[wukv_t, ckv_t], w=[PT[b]])
                            p.op("dve", lambda e, j=j, b=b, g=g: e.tensor_scalar(
                                out=V[:, g * 4 + j, :, 0:64], in0=bank(b).rearrange("p (h d) -> p h d", d=64),
                                scalar1=rcol[:, 0:1], scalar2=None, op0=ALU.mult), r=[PT[b], rcol_t], w=[V_t])
                    for g in range(4):
                        gs = slice(g * 512, (g + 1) * 512)
                        load_tab(g)
                        for a in range(2):
                            p.op("dve", lambda e, a=a: e.tensor_tensor(out=c1[64:96, a, :], in0=tab[64:96, a, :], in1=Rq[64:96, gs],
                                                                       op=ALU.mult), r=[tab_t, Rq_t], w=[c1_t])
                        for h in range(8):
                            for v_ in range(2):
                                b = 6 + v_
                                for c in range(3):
                                    p.op("pe", lambda e, h=h, v_=v_, c=c, b=b: e.matmul(
                                        out=PS[0:96, b, :], lhsT=wuq[:, c, v_ * 768 + h * 96:v_ * 768 + (h + 1) * 96],
                                        rhs=cqT[:, c, gs], start=(c == 0), stop=(c == 2)), r=[wuq_t, cqT_t], w=[PT[b]])
                            p.op("dve", lambda e, h=h: e.tensor_tensor(out=Qg[0:64, h, :], in0=PS[0:64, 6, :], in1=Rq[0:64, gs], op=ALU.mult),
                                 r=[PT[6], Rq_t], w=[Qg_t])
                            p.op("dve", lambda e: e.tensor_tensor(out=t1[64:96, :], in0=PS[64:96, 6, :], in1=c1[64:96, 0, :], op=ALU.mult),
                                 r=[PT[6], c1_t], w=[t1_t])
                            p.op("dve", lambda e: e.tensor_tensor(out=t2[64:96, :], in0=PS[64:96, 7, :], in1=c1[64:96, 1, :], op=ALU.mult),
                                 r=[PT[7], c1_t], w=[t2_t])
                            p.op("dve", lambda e, h=h: e.tensor_tensor(out=Qg[64:96, h, :], in0=t1[64:96, :], in1=t2[64:96, :], op=ALU.add),
                                 r=[t1_t, t2_t], w=[Qg_t])
                        def mla_S(st_):
                            h, kc = st_ // 16, st_ % 16
                            sbk = 4 + (st_ % 3)
                            p.op("pe", lambda e: e.matmul(
                                out=bank(sbk), lhsT=KT[0:96, h, kc * 128:(kc + 1) * 128], rhs=Qg[0:96, h, :], start=True, stop=True),
                                r=[KT_t, Qg_t], w=[PT[sbk]])

                        mla_S(0)
                        mla_S(1)
                        for st_ in range(128):
                            h, kc = st_ // 16, st_ % 16
                            if st_ + 2 < 128:
                                mla_S(st_ + 2)
                            sbk = 4 + (st_ % 3)
                            pi = st_ % 2
                            p.op("act", lambda e: e.activation(out=PTm[pi][:], in_=bank(sbk), func=AF.Exp),
                                 r=[PT[sbk]], w=[PTm_t[pi]])
                            for j in range(4):
                                p.op("pe", lambda e: e.matmul(
                                    out=PS[:, j, 0:65], lhsT=PTm[pi][:, j * 128:(j + 1) * 128], rhs=V[:, kc, h, :],
                                    start=(kc == 0), stop=(kc == 15)), r=[PTm_t[pi], V_t], w=[PT[j]])
                            if kc == 15:
                                tick = Trk()
                                p.op("dve", lambda e: e.reciprocal(out=rc4[:, 0:4], in_=PS[:, 0:4, 64]), r=PT[0:4], w=[rc4_t])
                                p.op("dve", lambda e: e.tensor_tensor(
                                    out=cat[:, g * 4:(g + 1) * 4, 512 + h * 64:512 + (h + 1) * 64], in0=PS[:, 0:4, 0:64],
                                    in1=rc4[:, 0:4].unsqueeze(2).broadcast_to([128, 4, 64]), op=ALU.mult),
                                    r=PT[0:4] + [rc4_t], w=cat_t[g * 4:(g + 1) * 4] + [tick])
                                advance_casts(4, tick)
                p.barrier()

                with ExitStack() as sa:
                    wo = sbuf(sa, "wo", [128, 8, 1024], BF16); wo_t = Trk()
                    cT = sbuf(sa, "cT", [128, 8, 128], BF16); cT_t = Trk()
                    yt = [sbuf(sa, f"yt{i}", [128, 1024], F32) for i in range(2)]
                    yt_t = tl(2, "yt")
                    catf = sbuf(sa, "catf", [128, 1024], F32); catf_t = Trk()
                    p.dma("sp", wo[:], b_wo.rearrange("(k p) m -> p k m", p=128), r=scr["wo"], w=[wo_t])
                    for ti in range(16):
                        i = ti % 2
                        row = s * 2048 + ti * 128
                        p.dma("sp", xt[i][:], h0d[row:row + 128, :], r=[h0d_t[row // 128]], w=[xt_t[i]])
                        if debug:
                            p.op("act", lambda e, ti=ti: e.copy(out=catf[:], in_=cat[:, ti, :]), r=[cat_t[ti]], w=[catf_t])
                            p.dma("sp", catd[row:row + 128, :], catf[:], r=[catf_t])
                        transpose_to(cat[:, ti, :], cat_t[ti], lambda c0, n: cT[:, c0:c0 + n, :], cT_t, 8)
                        for half in range(2):
                            b = 4 + half
                            for k in range(8):
                                p.op("pe", lambda e, k=k, half=half, b=b: e.matmul(out=bank(b), lhsT=cT[:, k, :],
                                                                                   rhs=wo[:, k, half * 512:(half + 1) * 512],
                                                                                   start=(k == 0), stop=(k == 7)),
                                     r=[cT_t, wo_t], w=[PT[b]])
                            p.op("dve", lambda e, i=i, half=half, b=b: e.scalar_tensor_tensor(
                                out=yt[i][:, half * 512:(half + 1) * 512], in0=xt[i][:, half * 512:(half + 1) * 512], scalar=ALPHA,
                                in1=bank(b), op0=ALU.mult, op1=ALU.add), r=[xt_t[i], PT[b]], w=[yt_t[i]])
                        layer_norm(yt[i][:], yt_t[i], 2)
                        p.dma("pool", h1d[row:row + 128, :], yt[i][:], r=[yt_t[i]], w=[h1_t[row // 128]])
                p.barrier()

        phA.close()
        advance_casts(1000)
        if do_b:
            load_lnp(es, 4, 6)
            phase_b(nc, p, es, sbuf, PS, PT, bank, bank_bf, NT, dict(
                h1d=h1d, h1_t=h1_t, out=out, pT=pT, bgd=bgd, iota=iota, skT=skT, b_wq=b_wq, b_u=b_u, b_v=b_v,
                b_wple=b_wple, b_wg=b_wg, scr=scr, identf=identf, identf_t=identf_t, identb=identb, identb_t=identb_t,
                layer_norm=layer_norm, transpose_to=transpose_to, LN=LN,
                lnscr=(st6, st6_t, mv, mv_t, rstd, rstd_t)))
        p.finish()
    return nc


def na_kt0(r0):
    rs = min(max(r0 - 4, 0), 24)
    bs = min(rs, 23)
    return min(bs // 2, 11)


def phase_b(nc, p, es, sbuf, PS, PT, bank, bank_bf, NT, a):
    h1d, h1_t, out, pT = a["h1d"], a["h1_t"], a["out"], a["pT"]
    scr = a["scr"]
    identf, identf_t = a["identf"], a["identf_t"]
    layer_norm, transpose_to = a["layer_norm"], a["transpose_to"]
    NG = NT // 256
    with ExitStack() as sb_:
        iota_f = sbuf(sb_, "iota_f", [128, 128], F32); iota_b = sbuf(sb_, "iota_b", [128, 128], BF16); iota_t = Trk()
        skTf = sbuf(sb_, "skTf", [128, 256], F32); skTb = sbuf(sb_, "skTb", [128, 256], BF16); sk_t = Trk()
        bg_sb = sbuf(sb_, "bg_sb", [128, 1024], F32); bg_t = Trk()
        wple = sbuf(sb_, "wple", [128, 2, 1024], BF16); wple_t = Trk()
        Gall = sbuf(sb_, "Gall", [128, 256, 128], BF16); G_t = Trk()
        h1t_ = [sbuf(sb_, f"h1t{i}", [128, 1024], F32) for i in range(2)]
        h1t = [h1t_, h1t_]
        h1tt_ = tl(2)
        h1t_t = [h1tt_, h1tt_]
        hst = sbuf(sb_, "hst", [128, 1024], F32); hst_t = Trk()
        h1b = sbuf(sb_, "h1b", [128, 1024], BF16); h1b_t = Trk()
        h1T = [sbuf(sb_, f"h1T{q}", [128, 8, 256], BF16) for q in range(3)]; h1T_t = tl(3)
        qTp = sbuf(sb_, "qTp", [128, 16, 256], BF16); qTp_t = Trk()
        wqs = [sbuf(sb_, f"wqs{i}", [128, 8, 256], BF16) for i in range(3)]; wqs_t = tl(3)
        ub = [sbuf(sb_, f"ub{i}", [128, 4096], BF16) for i in range(2)]; ub_t = tl(2)
        vb = [sbuf(sb_, f"vb{i}", [128, 4096], BF16) for i in range(2)]; vb_t = tl(2)
        scs = [sbuf(sb_, f"sc{j}", [128, 16, 128], F32) for j in range(2)]; scs_t = tl(2)
        v16 = sbuf(sb_, "v16", [128, 16, 16], F32); v16_t = Trk()
        ix = sbuf(sb_, "ix", [128, 16, 16], U32); ix_t = Trk()
        ixf = sbuf(sb_, "ixf", [128, 16, 16], F32); ixf_t = Trk()
        cv = sbuf(sb_, "cv", [128, 8, 16], F32); cv_t = Trk()
        ci = sbuf(sb_, "ci", [128, 8, 16], U32); ci_t = Trk()
        ai = sbuf(sb_, "ai", [128, 2, 128], U32); ai_t = Trk()
        af = sbuf(sb_, "af", [128, 2, 8, 16], F32); af_t = Trk()
        EEs = [sbuf(sb_, f"EE{j}", [128, 3, 128], F32) for j in range(2)]; EEs_t = tl(2)
        gs = sbuf(sb_, "gs", [128, 8], F32); gs_t = Trk()
        ETs = [sbuf(sb_, f"ETs{j}", [128, 3, 128], BF16) for j in range(2)]; ETs_t = tl(2)
        OH2 = [sbuf(sb_, f"OH2_{i}", [128, 128], BF16) for i in range(4)]; OH2_t = tl(4)
        OH1 = [sbuf(sb_, f"OH1_{i}", [128, 128], BF16) for i in range(4)]; OH1_t = tl(4)
        gl = [sbuf(sb_, f"gl{i}", [128, 256], BF16) for i in range(2)]; gl_t = tl(2)
        Am = [sbuf(sb_, f"Am{i}", [128, 256], BF16) for i in range(2)]; Am_t = tl(2)
        pTb1 = sbuf(sb_, "pTb", [128, 2, 256], BF16); pTb1_t = Trk()
        pTb = [pTb1, pTb1]; pTb_t = [pTb1_t, pTb1_t]
        gate = [sbuf(sb_, f"gate{i}", [128, 512], F32) for i in range(2)]; gate_t = tl(2)

        p.dma("sp", iota_f[:], a["iota"], w=[iota_t])
        p.op("dve", lambda e: e.tensor_copy(out=iota_b[:], in_=iota_f[:]), r=[iota_t], w=[iota_t])
        p.dma("sp", skTf[:], a["skT"], w=[sk_t])
        p.op("dve", lambda e: e.tensor_copy(out=skTb[:], in_=skTf[:]), r=[sk_t], w=[sk_t])
        p.dma("sp", bg_sb[:], a["bgd"], w=[bg_t])
        p.dma("sp", wple[:], a["b_wple"].rearrange("(k p) m -> p k m", p=128), r=scr["wple"], w=[wple_t])

        def b1_front(gi):
            row = gi * 256
            seq, tok0 = row // 2048, row % 2048
            q_ = gi % 2

            def wq_load(blk):
                p.dma("sp", wqs[blk % 3][:].rearrange("p k m -> p (k m)"), a["b_wq"][blk * 128:(blk + 1) * 128, :],
                      r=scr["wq"], w=[wqs_t[blk % 3]])

            wq_load(0)
            wq_load(1)
            for j in range(2):
                p.dma("sp", hst[:], h1d[row + j * 128:row + (j + 1) * 128, :], r=[h1_t[row // 128 + j]], w=[hst_t])
                yield
                yield
                p.op("act", lambda e: e.copy(out=h1b[:], in_=hst[:]), r=[hst_t], w=[h1b_t])
                yield
                for half in range(2):
                    bb = 6 + half
                    for k in range(4):
                        kk = half * 4 + k
                        p.op("pe", lambda e: e.transpose(out=bank_bf(bb)[:, k, :], in_=h1b[:, kk * 128:(kk + 1) * 128],
                                                         identity=a["identb"][:]), r=[h1b_t, a["identb_t"]], w=[PT[bb]])
                yield
                for half in range(2):
                    bb = 6 + half
                    p.op("act", lambda e: e.copy(out=h1T[gi % 3][:, half * 4:(half + 1) * 4, j * 128:(j + 1) * 128],
                                                 in_=bank_bf(bb)[:, 0:4, :]), r=[PT[bb]], w=[h1T_t[gi % 3]])
                yield
            pend = None
            for blk in range(8):
                i = blk % 3
                if blk + 2 < 8:
                    wq_load(blk + 2)
                for cc in range(2):
                    hc = blk * 2 + cc
                    b = 6 + hc % 2
                    for k in range(8):
                        p.op("pe", lambda e: e.matmul(out=PS[:, b, 0:256], lhsT=wqs[i][:, k, cc * 128:(cc + 1) * 128],
                                                      rhs=h1T[gi % 3][:, k, :], start=(k == 0), stop=(k == 7)),
                             r=[wqs_t[i], h1T_t[gi % 3]], w=[PT[b]])
                    if pend is not None:
                        hc0, b0 = pend
                        p.op("act", lambda e: e.copy(out=qTp[:, hc0, :], in_=PS[:, b0, 0:256]), r=[PT[b0]], w=[qTp_t])
                    pend = (hc, b)
                    yield
            hc0, b0 = pend
            p.op("act", lambda e: e.copy(out=qTp[:, hc0, :], in_=PS[:, b0, 0:256]), r=[PT[b0]], w=[qTp_t])
            yield
            pend = None
            for j in range(2):
                sc, sc_t = scs[j], scs_t[j]
                for qd in range(4):
                    b = 6 + qd % 2
                    for u in range(4):
                        hc = qd * 4 + u
                        half = hc % 2
                        p.op("pe", lambda e: e.matmul(out=PS[:, b, u * 128:(u + 1) * 128], lhsT=qTp[:, hc, j * 128:(j + 1) * 128],
                                                      rhs=skTb[:, half * 128:(half + 1) * 128], start=True, stop=True),
                             r=[qTp_t, sk_t], w=[PT[b]])
                    if pend is not None:
                        sc0, sct0, qd0, b0 = pend
                        p.op("act", lambda e: e.copy(out=sc0[:, 4 * qd0:4 * qd0 + 4, :].rearrange("p a b -> p (a b)"), in_=bank(b0)),
                             r=[PT[b0]], w=[sct0])
                    pend = (sc, sc_t, qd, b)
                    yield
            sc0, sct0, qd0, b0 = pend
            p.op("act", lambda e: e.copy(out=sc0[:, 4 * qd0:4 * qd0 + 4, :].rearrange("p a b -> p (a b)"), in_=bank(b0)),
                 r=[PT[b0]], w=[sct0])
            yield

        def b1_chains(gi):
            for j in range(2):
                sc, sc_t = scs[j], scs_t[j]
                oh, oh_t = sc[:].rearrange("p (h a) (b c) -> p h (a b) c", a=2, c=16), sc_t
                EE, EE_t = EEs[j], EEs_t[j]
                for gq in range(16):
                    for rnd in range(2):
                        vs = v16[:, gq, rnd * 8:(rnd + 1) * 8]
                        p.op("dve", lambda e: e.max(out=vs, in_=sc[:, gq, :]), r=[sc_t], w=[v16_t])
                        p.op("dve", lambda e: e.max_index(out=ix[:, gq, rnd * 8:(rnd + 1) * 8], in_max=vs, in_values=sc[:, gq, :]),
                             r=[sc_t, v16_t], w=[ix_t])
                        if rnd == 0:
                            p.op("dve", lambda e: e.match_replace(out=sc[:, gq, :], in_to_replace=vs, in_values=sc[:, gq, :],
                                                                  imm_value=-1e30), r=[v16_t, sc_t], w=[sc_t])
                    yield
                p.op("dve", lambda e: e.tensor_copy(out=ixf[:], in_=ix[:]), r=[ix_t], w=[ixf_t])
                v4 = v16[:].rearrange("p (h two) k -> p h two k", two=2)
                ix4 = ixf[:].rearrange("p (h two) k -> p h two k", two=2)
                cand4 = sc[:].rearrange("p (h a) (b c) -> p h (a b) c", a=2, c=16)
                candf = sc[:].rearrange("p (h a) b -> p h (a b)", a=2)
                p.op("dve", lambda e: e.tensor_tensor(out=cand4, in0=v4[:, :, 0, :].unsqueeze(3).broadcast_to([128, 8, 16, 16]),
                                                      in1=v4[:, :, 1, :].unsqueeze(2).broadcast_to([128, 8, 16, 16]), op=ALU.add),
                     r=[v16_t, sc_t], w=[sc_t])
                yield
                for h in range(8):
                    for rnd in range(2):
                        vs = cv[:, h, rnd * 8:(rnd + 1) * 8]
                        p.op("dve", lambda e: e.max(out=vs, in_=candf[:, h, :]), r=[sc_t], w=[cv_t])
                        p.op("dve", lambda e: e.max_index(out=ci[:, h, rnd * 8:(rnd + 1) * 8], in_max=vs, in_values=candf[:, h, :]),
                             r=[sc_t, cv_t], w=[ci_t])
                        if rnd == 0:
                            p.op("dve", lambda e: e.match_replace(out=candf[:, h, :], in_to_replace=vs, in_values=candf[:, h, :],
                                                                  imm_value=-1e30), r=[cv_t, sc_t], w=[sc_t])
                    yield
                gg3 = EE[:, 2, :].rearrange("p (h k) -> p h k", k=16)
                p.op("dve", lambda e: e.tensor_tensor(out=gg3, in0=cv[:], in1=cv[:, :, 0:1].broadcast_to([128, 8, 16]), op=ALU.subtract),
                     r=[cv_t], w=[EE_t])
                cif = ci[:].rearrange("p h k -> p (h k)")
                p.op("dve", lambda e: e.tensor_single_scalar(out=ai[:, 0, :], in_=cif, scalar=4, op=ALU.logical_shift_right),
                     r=[ci_t], w=[ai_t])
                p.op("dve", lambda e: e.tensor_single_scalar(out=ai[:, 1, :], in_=cif, scalar=15, op=ALU.bitwise_and),
                     r=[ci_t], w=[ai_t])
                p.op("dve", lambda e: e.tensor_copy(out=af[:].rearrange("p a h k -> p a (h k)"), in_=ai[:]), r=[ai_t], w=[af_t])
                yield
                for w_ in range(2):
                    p.op("dve", lambda e: e.tensor_tensor(
                        out=oh, in0=iota_f[:, 0:16].unsqueeze(1).unsqueeze(1).broadcast_to([128, 8, 16, 16]),
                        in1=af[:, w_, :, :].unsqueeze(3).broadcast_to([128, 8, 16, 16]), op=ALU.is_equal),
                        r=[iota_t, af_t], w=[oh_t])
                    yield
                    p.op("dve", lambda e: e.tensor_tensor(
                        out=oh, in0=oh, in1=ix4[:, :, w_, :].unsqueeze(2).broadcast_to([128, 8, 16, 16]), op=ALU.mult),
                        r=[oh_t, ixf_t], w=[oh_t])
                    yield
                    p.op("dve", lambda e: e.tensor_reduce(out=EE[:, 1 - w_, :].rearrange("p (h k) -> p h k", k=16), in_=oh,
                                                          axis=AX.X, op=ALU.add), r=[oh_t], w=[EE_t])
                    yield

        def b1_tail(gi):
            for _ in range(6):
                yield
            for j in range(2):
                EE, EE_t = EEs[j], EEs_t[j]
                p.op("act", lambda e: e.activation(out=EE[:, 2, :], in_=EE[:, 2, :], func=AF.Exp), r=[EE_t], w=[EE_t])
            yield
            yield
            for j in range(2):
                EE, EE_t = EEs[j], EEs_t[j]
                gg3 = EE[:, 2, :].rearrange("p (h k) -> p h k", k=16)
                p.op("dve", lambda e: e.tensor_reduce(out=gs[:], in_=gg3, axis=AX.X, op=ALU.add), r=[EE_t], w=[gs_t])
                p.op("dve", lambda e: e.reciprocal(out=gs[:], in_=gs[:]), r=[gs_t], w=[gs_t])
                p.op("dve", lambda e: e.tensor_tensor(out=gg3, in0=gg3, in1=gs[:].unsqueeze(2).broadcast_to([128, 8, 16]), op=ALU.mult),
                     r=[EE_t, gs_t], w=[EE_t])
            for _ in range(4):
                yield
            for j in range(2):
                EE, EE_t = EEs[j], EEs_t[j]
                bb = 6 + j
                for q in range(3):
                    p.op("pe", lambda e: e.transpose(out=PS[:, bb, q * 128:(q + 1) * 128], in_=EE[:, q, :], identity=identf[:]),
                         r=[EE_t, identf_t], w=[PT[bb]])
            yield
            yield
            for j in range(2):
                bb = 6 + j
                p.op("act", lambda e: e.copy(out=ETs[j][:].rearrange("p a b -> p (a b)"), in_=PS[:, bb, 0:384]), r=[PT[bb]], w=[ETs_t[j]])
            yield

        def b1e(gi):
            for j in range(2):
                for tt in range(128):
                    r4 = tt % 4
                    b = 4 + (tt // 4) % 2
                    p.op("dve", lambda e: e.tensor_scalar(out=OH2[r4][:], in0=iota_b[:], scalar1=ETs[j][:, 0, tt:tt + 1],
                                                          scalar2=None, op0=ALU.is_equal), r=[iota_t, ETs_t[j]], w=[OH2_t[r4]])
                    p.op("dve", lambda e: e.tensor_scalar(out=OH1[r4][:], in0=iota_b[:], scalar1=ETs[j][:, 1, tt:tt + 1],
                                                          scalar2=ETs[j][:, 2, tt:tt + 1], op0=ALU.is_equal, op1=ALU.mult),
                         r=[iota_t, ETs_t[j]], w=[OH1_t[r4]])
                    p.op("pe", lambda e: e.matmul(out=PS[:, b, r4 * 128:(r4 + 1) * 128], lhsT=OH2[r4][:], rhs=OH1[r4][:],
                                                  start=True, stop=True), r=[OH2_t[r4], OH1_t[r4]], w=[PT[b]])
                    if r4 == 3:
                        t0 = j * 128 + tt - 3
                        p.op("act", lambda e: e.copy(out=Gall[:, t0:t0 + 4, :].rearrange("p a b -> p (a b)"), in_=bank(b)),
                             r=[PT[b]], w=[G_t])
                        yield

        def b2_load(cb):
            i = cb % 2
            uv = ub[i][:].rearrange("p (c f) -> p c f", f=1024)
            vv = vb[i][:].rearrange("p (c f) -> p c f", f=1024)
            p.dma("sp", uv, a["b_u"][cb * 512:(cb + 1) * 512, :].rearrange("(c p) f -> p c f", p=128), r=scr["u"], w=[ub_t[i]])
            p.dma("sp", vv, a["b_v"][cb * 512:(cb + 1) * 512, :].rearrange("(c p) f -> p c f", p=128), r=scr["v"], w=[vb_t[i]])

        def b2(gi, preloaded):
            q_ = gi % 2

            def U(c):
                i = (c // 4) % 2
                uv = ub[i][:].rearrange("p (c f) -> p c f", f=1024)
                hb_ = 4 + c % 2
                for k in range(8):
                    p.op("pe", lambda e: e.matmul(out=PS[:, hb_, 0:256], lhsT=uv[:, c % 4, k * 128:(k + 1) * 128],
                                                  rhs=h1T[gi % 3][:, k, :], start=(k == 0), stop=(k == 7)),
                         r=[ub_t[i], h1T_t[gi % 3]], w=[PT[hb_]])

            if not preloaded:
                b2_load(0)
                b2_load(1)
            U(0)
            for c in range(128):
                if c + 1 < 128:
                    U(c + 1)
                i = (c // 4) % 2
                vv = vb[i][:].rearrange("p (c f) -> p c f", f=1024)
                hb_ = 4 + c % 2
                ci_ = c % 2
                p.op("act", lambda e: e.activation(out=gl[ci_][:], in_=PS[:, hb_, 0:256], func=AF.Gelu),
                     r=[PT[hb_]], w=[gl_t[ci_]])
                p.op("pool", lambda e: e.tensor_tensor(out=Am[ci_][:], in0=gl[ci_][:], in1=Gall[:, :, c], op=ALU.mult),
                     r=[gl_t[ci_], G_t], w=[Am_t[ci_]])
                for j in range(2):
                    for half in range(2):
                        ob = j * 2 + half
                        p.op("pe", lambda e: e.matmul(
                            out=bank(ob), lhsT=Am[ci_][:, j * 128:(j + 1) * 128], rhs=vv[:, c % 4, half * 512:(half + 1) * 512],
                            start=(c == 0), stop=(c == 127)), r=[Am_t[ci_], vb_t[i]], w=[PT[ob]])
                if c % 4 == 3:
                    if c // 4 + 2 < 32:
                        b2_load(c // 4 + 2)
                    elif gi + 1 < NG:
                        b2_load(c // 4 + 2 - 32)
                yield

        def b3_loads(gi):
            row = gi * 256
            seq, tok0 = row // 2048, row % 2048
            for j in range(2):
                p.dma("sp", h1t_[j][:], h1d[row + j * 128:row + (j + 1) * 128, :], r=[h1_t[row // 128 + j]], w=[h1tt_[j]])
            p.dma("pool", pTb1[:], pT[seq, :, tok0:tok0 + 256].rearrange("(k p) t -> p k t", p=128), w=[pTb1_t])
            yield

        def b3a(gi):
            q_ = gi % 2
            for j in range(2):
                for half in range(2):
                    hs = slice(half * 512, (half + 1) * 512)
                    ob = j * 2 + half
                    p.op("dve", lambda e: e.scalar_tensor_tensor(out=h1t[q_][j][:, hs], in0=h1t[q_][j][:, hs], scalar=ALPHA, in1=bank(ob),
                                                                 op0=ALU.mult, op1=ALU.add), r=[h1t_t[q_][j], PT[ob]], w=[h1t_t[q_][j]])

        def b3b(gi):
            row = gi * 256
            q_ = gi % 2

            def wg_load(qd):
                p.dma("sp", wqs[qd % 3][:].rearrange("p k m -> p (k m)"), a["b_wg"][qd * 128:(qd + 1) * 128, :],
                      r=scr["wg"], w=[wqs_t[qd % 3]])

            for qd in range(3):
                wg_load(qd)
            yield
            yield
            for qd in range(4):
                cs = slice(qd * 256, (qd + 1) * 256)
                wv = wqs[qd % 3]
                for j in range(2):
                    for k in range(8):
                        p.op("pe", lambda e: e.matmul(out=PS[:, 6, 0:256], lhsT=h1T[gi % 3][:, k, j * 128:(j + 1) * 128], rhs=wv[:, k, :],
                                                      start=(k == 0), stop=(k == 7)), r=[wqs_t[qd % 3], h1T_t[gi % 3]], w=[PT[6]])
                    for k in range(2):
                        p.op("pe", lambda e: e.matmul(out=PS[:, 7, 0:256], lhsT=pTb[q_][:, k, j * 128:(j + 1) * 128], rhs=wple[:, k, cs],
                                                      start=(k == 0), stop=(k == 1)), r=[pTb_t[q_], wple_t], w=[PT[7]])
                    yield
                    p.op("dve", lambda e: e.tensor_tensor(out=gate[j][:, 0:256], in0=PS[:, 6, 0:256], in1=bg_sb[:, cs], op=ALU.add),
                         r=[PT[6], bg_t], w=[gate_t[j]])
                    yield
                    p.op("act", lambda e: e.activation(out=gate[j][:, 0:256], in_=gate[j][:, 0:256], func=AF.Sigmoid),
                         r=[gate_t[j]], w=[gate_t[j]])
                    yield
                    p.op("dve", lambda e: e.tensor_tensor(out=gate[j][:, 0:256], in0=gate[j][:, 0:256], in1=PS[:, 7, 0:256], op=ALU.mult),
                         r=[gate_t[j], PT[7]], w=[gate_t[j]])
                    p.op("dve", lambda e: e.tensor_tensor(out=h1t[q_][j][:, cs], in0=h1t[q_][j][:, cs], in1=gate[j][:, 0:256], op=ALU.add),
                         r=[h1t_t[q_][j], gate_t[j]], w=[h1t_t[q_][j]])
                if qd == 0:
                    wg_load(3)
            yield
            for j in range(2):
                src, src_t = h1t[q_][j][:], h1t_t[q_][j]
                st6, st6_t, mv, mv_t, rstd, rstd_t = a["lnscr"]
                for c in range(2):
                    p.op("dve", lambda e: e.bn_stats(out=st6[:, c, :], in_=src[:, c * 512:(c + 1) * 512]), r=[src_t], w=[st6_t])
                p.op("dve", lambda e: e.bn_aggr(out=mv[:], in_=st6[:].rearrange("p a b -> p (a b)")), r=[st6_t], w=[mv_t])
                yield
                yield
                p.op("act", lambda e: e.activation(out=rstd[:], in_=mv[:, 1:2], func=AF.Sqrt, bias=EPS, scale=1.0), r=[mv_t], w=[rstd_t])
                yield
                LN = a["LN"]
                p.op("dve", lambda e: e.reciprocal(out=rstd[:], in_=rstd[:]), r=[rstd_t], w=[rstd_t])
                p.op("dve", lambda e: e.tensor_scalar(out=src, in0=src, scalar1=mv[:, 0:1], scalar2=rstd[:, 0:1],
                                                      op0=ALU.subtract, op1=ALU.mult), r=[src_t, mv_t, rstd_t], w=[src_t])
                p.op("dve", lambda e: e.tensor_tensor(out=src, in0=src, in1=LN["sb"][:, 4 - LN["base"], :], op=ALU.mult),
                     r=[src_t, LN["t"]], w=[src_t])
                p.op("dve", lambda e: e.tensor_tensor(out=src, in0=src, in1=LN["sb"][:, 5 - LN["base"], :], op=ALU.add),
                     r=[src_t, LN["t"]], w=[src_t])
                yield
            for _ in range(6):
                yield
            for j in range(2):
                p.dma("pool", out[row + j * 128:row + (j + 1) * 128, :], h1t[q_][j][:], r=[h1t_t[q_][j]])
            for _ in range(6):
                yield

        def run(gen):
            for _ in gen:
                pass

        def chain(*gens):
            for g_ in gens:
                if g_ is not None:
                    yield from g_

        def interleave(main, bg, n_main, n_bg):
            acc = 0.0
            alive = bg is not None
            for _ in main:
                acc += n_bg / n_main
                while alive and acc >= 1.0:
                    acc -= 1.0
                    try:
                        next(bg)
                    except StopIteration:
                        alive = False
            if alive:
                for _ in bg:
                    pass

        def merge(ga, gb):
            alive = [ga, gb]
            while alive:
                for g_ in list(alive):
                    if g_ is None:
                        alive.remove(g_)
                        continue
                    try:
                        yield next(g_)
                    except StopIteration:
                        alive.remove(g_)

        run(chain(b1_front(0), b1_chains(0), b1_tail(0), b3_loads(0)))
        run(b1e(0))
        for gi in range(NG):
            nxt = gi + 1 < NG
            bg = chain(b1_front(gi + 1) if nxt else None,
                       merge(b1_chains(gi + 1) if nxt else None, b3b(gi - 1) if gi > 0 else None),
                       b1_tail(gi + 1) if nxt else None,
                       b3_loads(gi) if gi > 0 else None)
            interleave(b2(gi, gi > 0), bg, 128, 150 if gi > 0 else 120)
            b3a(gi)
            if nxt:
                run(b1e(gi + 1))
        run(b3b(NG - 1))


def na_bias_tables(rpb):
    W, KH, KW, ROWS = 64, 8, 16, 32
    outs = []
    for r0 in (0, 2, 4, 28, 30):
        kt0 = na_kt0(r0)
        key_tok = kt0 * 128 + np.arange(640)
        krow, kcol = key_tok // W, key_tok % W
        q = np.arange(128)
        qrow, qcol = r0 + q // W, q % W
        rs = np.clip(qrow - KH // 2, 0, ROWS - KH)
        cs = np.clip(qcol - KW // 2, 0, W - KW)
        di = krow[:, None] - qrow[None, :] + (KH - 1)
        dj = kcol[:, None] - qcol[None, :] + (KW - 1)
        ok = ((krow[:, None] >= rs[None, :]) & (krow[:, None] < rs[None, :] + KH)
              & (kcol[:, None] >= cs[None, :]) & (kcol[:, None] < cs[None, :] + KW))
        dic = np.clip(di, 0, 14)
        djc = np.clip(dj, 0, 30)
        g = rpb[:, dic, djc]
        g = np.where(ok[None], g, np.float32(NEG)).astype(np.float32)
        outs.append(g.reshape(8, 5, 128, 128))
    return np.stack(outs, 0)


def rope_tables():
    t = np.arange(2048)
    row = (t // 64).astype(np.float32)
    col = (t % 64).astype(np.float32)
    inv = (10000.0 ** (-np.arange(0, 16, 2, dtype=np.float32) / 16)).astype(np.float32)
    ang = np.concatenate([row[:, None] * inv[None, :], col[:, None] * inv[None, :]], axis=-1)
    cos = np.cos(ang).astype(np.float32)
    sin = np.sin(ang).astype(np.float32)
    tab = np.zeros((2, 32, 2048), np.float32)
    tab[0] = np.repeat(cos, 2, axis=1).T
    sgn = np.tile(np.array([-1.0, 1.0], np.float32), 16)
    tab[1] = (np.repeat(sin, 2, axis=1) * sgn[None, :]).T
    return tab


def pair_swap_cols(w, cols):
    w2 = w.copy()
    w2[:, cols[0::2]] = w[:, cols[1::2]]
    w2[:, cols[1::2]] = w[:, cols[0::2]]
    return w2


def host_layout(inputs):
    f = lambda k: np.asarray(inputs[k], dtype=np.float32)
    w_in = f("w_in")[0]
    w_uq = f("w_uq")[0]
    sh = {}
    sh["lnp"] = np.ascontiguousarray(np.broadcast_to(np.stack(
        [f("emb_ln_g"), f("emb_ln_b"), f("ln1_g")[0], f("ln1_b")[0], f("ln2_g")[0], f("ln2_b")[0]], 0)[None], (128, 6, 1024)))
    sh["bg"] = np.ascontiguousarray(np.broadcast_to(f("ple_gate_b")[0][None], (128, 1024)))
    sh["ident"] = np.eye(128, dtype=np.float32)
    sh["iota"] = np.ascontiguousarray(np.broadcast_to(np.arange(128, dtype=np.float32)[None], (128, 128)))
    sh["rope"] = rope_tables()
    sh["w_in"] = np.ascontiguousarray(w_in)
    kr96 = w_in[:, 2112:2208]
    sh["w_kr"] = np.ascontiguousarray(np.concatenate([kr96, pair_swap_cols(kr96, np.arange(64, 96))], axis=1))
    rope_cols = np.concatenate([h * 96 + 64 + np.arange(32) for h in range(8)])
    sh["w_uq"] = np.ascontiguousarray(np.concatenate([w_uq, pair_swap_cols(w_uq, rope_cols)], axis=1))
    sh["w_ukv"] = np.ascontiguousarray(f("w_ukv")[0])
    sh["qg"] = np.ascontiguousarray(f("mla_q_norm_g")[0].reshape(3, 128).T)
    sh["kvg"] = np.ascontiguousarray(f("mla_kv_norm_g")[0].reshape(2, 128).T)
    sh["nab"] = np.ascontiguousarray(na_bias_tables(f("na_rpb")[0]).reshape(5 * 8 * 5 * 128, 128))
    sh["w_o"] = np.ascontiguousarray(f("w_o")[0])
    sh["w_q"] = np.ascontiguousarray(f("peer_w_q")[0].reshape(8, 128, 8, 256).transpose(2, 1, 0, 3)).reshape(1024, 2048)
    sk = f("peer_sub_keys")[0]
    sh["skT"] = np.ascontiguousarray(np.concatenate([sk[0].T, sk[1].T], axis=1))
    U = f("peer_u")[0]
    sh["u_l"] = np.ascontiguousarray(U.reshape(128, 128, 8, 128).transpose(0, 3, 2, 1)).reshape(16384, 1024)
    sh["v_l"] = np.ascontiguousarray(f("peer_v")[0])
    sh["w_ple"] = np.ascontiguousarray(f("ple_w")[0])
    sh["w_g"] = np.ascontiguousarray(f("ple_gate_w")[0].reshape(8, 128, 4, 256).transpose(2, 1, 0, 3)).reshape(512, 2048)
    return sh


def kernel(**inputs):
    sh = host_layout(inputs)
    x = np.asarray(inputs["x"], dtype=np.float32)
    pp = np.asarray(inputs["p"], dtype=np.float32)[0]
    in_maps = []
    for c in range(8):
        m = dict(sh)
        m["x"] = np.ascontiguousarray(x[2 * c:2 * c + 2].reshape(4096, 1024))
        m["pT"] = np.ascontiguousarray(pp[2 * c:2 * c + 2].transpose(0, 2, 1))
        in_maps.append(m)
    nc = build()
    res = run_bass_kernel_spmd(nc, in_maps, core_ids=list(range(8)))
    return np.concatenate([r["out"].reshape(2, 2048, 1024) for r in res.results], axis=0)
```

```python
import numpy as np
import concourse.bass as bass
import concourse.mybir as mybir
from concourse.bass_utils import run_bass_kernel_spmd
from contextlib import ExitStack

F32 = mybir.dt.float32
BF16 = mybir.dt.bfloat16
U32 = mybir.dt.uint32
ALU = mybir.AluOpType
AF = mybir.ActivationFunctionType
AX = mybir.AxisListType

ALPHA = float(2.0 ** 0.25)
EPS = 1e-5
NEG = -30000.0


class Trk:
    __slots__ = ("name", "w", "r")

    def __init__(self, name=""):
        self.name = name
        self.w = None
        self.r = {}


class _Rec:
    def __getattr__(self, name):
        def f(*args, **kw):
            self.call = (name, args, kw)
        return f


class Prog:
    ENG = ("pe", "act", "dve", "pool", "sp")

    def __init__(self, nc, es):
        self.nc = nc
        self.es = es
        self.ops = {e: [] for e in self.ENG}
        self.cnt = {}
        self.sem = {}
        self.seen = {e: {} for e in self.ENG}
        for e in self.ENG:
            self._mksem(e)
        self.ndma = {e: 0 for e in self.ENG}

    def _mksem(self, key):
        self.sem[key] = self.es.enter_context(self.nc.semaphore(name=f"s_{key}"))
        self.cnt[key] = 0

    def _deps(self, eng, r, w):
        deps = {}
        for t in r:
            if t.w is not None and t.w[1] > deps.get(t.w[0], 0):
                deps[t.w[0]] = t.w[1]
        for t in w:
            if t.w is not None and t.w[1] > deps.get(t.w[0], 0):
                deps[t.w[0]] = t.w[1]
            for k, v in t.r.items():
                if v > deps.get(k, 0):
                    deps[k] = v
        waits = []
        for k, v in deps.items():
            if k == eng and eng == "pe":
                continue
            if self.seen[eng].get(k, 0) >= v:
                continue
            self.seen[eng][k] = v
            waits.append((k, v))
        return waits

    def op(self, eng, fn, r=(), w=()):
        rec = _Rec()
        fn(rec)
        call = rec.call
        fn = lambda e, call=call: getattr(e, call[0])(*call[1], **call[2])
        waits = self._deps(eng, r, w)
        self.cnt[eng] += 1
        n = self.cnt[eng]
        self.ops[eng].append((fn, waits, eng, 1))
        for t in r:
            t.r[eng] = n
        for t in w:
            t.w = (eng, n)
            t.r = {}

    def dma(self, q, out, in_, r=(), w=(), **kw):
        waits = self._deps(q, r, w)
        key = f"d{q}{self.ndma[q] % (8 if q == 'pool' else 24)}"
        self.ndma[q] += 1
        if key not in self.sem:
            self._mksem(key)
        prev = self.cnt[key]
        if prev and self.seen[q].get(key, 0) < prev:
            self.seen[q][key] = prev
            waits.append((key, prev))
        self.cnt[key] += 16
        n = self.cnt[key]
        self.ops[q].append((lambda e: e.dma_start(out=out, in_=in_, **kw), waits, key, 16))
        for t in r:
            t.r[key] = n
        for t in w:
            t.w = (key, n)
            t.r = {}

    def barrier(self):
        snap = dict(self.cnt)
        for e in self.ENG:
            waits = []
            for k, v in snap.items():
                if v and k != e and self.seen[e].get(k, 0) < v:
                    self.seen[e][k] = v
                    waits.append((k, v))
            if waits:
                self.ops[e].append((None, waits, None, 0))

    def finish(self):
        fin = [(k, v) for k, v in self.cnt.items() if k.startswith("d") and k not in self.ENG and v > 0]
        nc = self.nc
        names = {"pe": "tensor", "act": "scalar", "dve": "vector", "pool": "gpsimd", "sp": "sync"}
        with nc.Block() as block:
            for e in self.ENG:
                def body(eng, e=e):
                    for fn, waits, key, amt in self.ops[e]:
                        for k, v in waits:
                            eng.wait_ge(self.sem[k], v)
                        if fn is not None:
                            ins = fn(eng)
                            ins.then_inc(self.sem[key], amt)
                    if e == "sp":
                        for k, v in fin:
                            eng.wait_ge(self.sem[k], v)
                getattr(block, names[e])(body)


def tl(n, name=""):
    return [Trk(f"{name}{i}") for i in range(n)]


def build(nseq=2, do_b=True, debug=False):
    nc = bass.Bass("TRN2", target_bir_lowering=False)
    NT = nseq * 2048

    def din(name, shape, dt=F32):
        return nc.dram_tensor(name, list(shape), dt, kind="ExternalInput").ap()

    def dscr(name, shape, dt):
        return nc.dram_tensor(name, list(shape), dt, kind="Internal").ap()

    x = din("x", [4096, 1024])
    pT = din("pT", [2, 256, 2048])
    lnp = din("lnp", [128, 6, 1024])
    bgd = din("bg", [128, 1024])
    ident = din("ident", [128, 128])
    iota = din("iota", [128, 128])
    rope = din("rope", [2, 32, 2048])
    w_in = din("w_in", [1024, 2208])
    w_kr = din("w_kr", [1024, 2 * 96])
    w_uq = din("w_uq", [384, 2 * 768])
    w_ukv = din("w_ukv", [256, 1024])
    qg = din("qg", [128, 3])
    kvg = din("kvg", [128, 2])
    nab = din("nab", [5 * 8 * 5 * 128, 128])
    w_o = din("w_o", [1024, 1024])
    w_q = din("w_q", [1024, 2048])
    skT = din("skT", [128, 256])
    u_l = din("u_l", [16384, 1024])
    v_l = din("v_l", [16384, 1024])
    w_ple = din("w_ple", [256, 1024])
    w_g = din("w_g", [512, 2048])
    out = nc.dram_tensor("out", [4096, 1024], F32, kind="ExternalOutput").ap()
    if debug:
        h1d = nc.dram_tensor("h1d", [4096, 1024], F32, kind="ExternalOutput").ap()
        catd = nc.dram_tensor("catd", [4096, 1024], F32, kind="ExternalOutput").ap()
    else:
        h1d = dscr("h1d", [4096, 1024], F32)
        catd = None
    h0d = dscr("h0d", [4096, 1024], F32)
    h0Td = dscr("h0Td", [2 * 4 * 128, 4096], BF16)
    b_win = dscr("b_win", [1024, 2208], BF16)
    b_wkr = dscr("b_wkr", [1024, 192], BF16)
    b_wuq = dscr("b_wuq", [384, 1536], BF16)
    b_wukv = dscr("b_wukv", [256, 1024], BF16)
    b_nab = dscr("b_nab", [5 * 8 * 5 * 128, 128], BF16)
    b_wo = dscr("b_wo", [1024, 1024], BF16)
    b_wq = dscr("b_wq", [1024, 2048], BF16)
    b_u = dscr("b_u", [16384, 1024], BF16)
    b_v = dscr("b_v", [16384, 1024], BF16)
    b_wple = dscr("b_wple", [256, 1024], BF16)
    b_wg = dscr("b_wg", [512, 2048], BF16)

    with ExitStack() as es:
        p = Prog(nc, es)

        uniq = [0]

        def sbuf(st, name, shape, dt):
            uniq[0] += 1
            return st.enter_context(nc.sbuf_tensor(f"{name}_{uniq[0]}", list(shape), dt))

        PS = es.enter_context(nc.psum_tensor("ps", [128, 8, 512], F32))
        PT = tl(8, "bank")

        def bank(i):
            return PS[:, i, :]

        def bank_bf(i):
            return PS[:, i, 0:256].bitcast(BF16).rearrange("p (a b) -> p a b", b=128)

        scr = {}

        def cast_dram(dst, src, rows, name, rows_per=512):
            scr[name] = []
            a = rows_per // 128
            dv = dst.rearrange("(n p a) m -> n p a m", p=128, a=a)
            sv = src.rearrange("(n p a) m -> n p a m", p=128, a=a)
            for i in range(rows // rows_per):
                t = Trk(name)
                scr[name].append(t)
                p.dma("pool", dv[i], sv[i], w=[t])

        cast_dram(b_win[:, 0:1104], w_in[:, 0:1104], 1024, "win", 512)
        win0 = scr["win"]
        cast_dram(b_win[:, 1104:2208], w_in[:, 1104:2208], 1024, "win", 512)
        scr["win"] = win0 + scr["win"]
        cast_dram(b_wkr, w_kr, 1024, "wkr", 1024)
        cast_dram(b_wuq, w_uq, 384, "wuq", 128)
        cast_dram(b_wukv, w_ukv, 256, "wukv", 256)
        cast_dram(b_nab, nab, 5 * 8 * 5 * 128, "nab", 1280)
        cast_dram(b_wo, w_o, 1024, "wo", 512)
        def cast_gen():
            for dst, src, rows, name, per in ((b_wq, w_q, 1024, "wq", 256), (b_wg, w_g, 512, "wg", 256),
                                              (b_wple, w_ple, 256, "wple", 256), (b_u, u_l, 16384, "u", 512),
                                              (b_v, v_l, 16384, "v", 512)):
                scr[name] = []
                a_ = per // 128
                dv = dst.rearrange("(n p a) m -> n p a m", p=128, a=a_)
                sv = src.rearrange("(n p a) m -> n p a m", p=128, a=a_)
                for i in range(rows // per):
                    t = Trk(name)
                    scr[name].append(t)
                    dep = yield
                    p.dma("pool", dv[i], sv[i], r=([dep] if dep is not None else []), w=[t])

        cg = [cast_gen() if do_b else None]
        if do_b:
            next(cg[0])

        def advance_casts(n, dep=None):
            for _ in range(n):
                if cg[0] is None:
                    return
                try:
                    cg[0].send(dep)
                except StopIteration:
                    cg[0] = None

        cs = es
        identf = sbuf(cs, "identf", [128, 128], F32); identf_t = Trk()
        identb = sbuf(cs, "identb", [128, 128], BF16); identb_t = Trk()
        onesb = sbuf(cs, "onesb", [128, 128], BF16); onesb_t = Trk()
        LN = {}
        p.dma("sp", identf[:], ident, w=[identf_t])

        def load_lnp(st, lo, hi):
            LN["sb"] = sbuf(st, "lnp_sb", [128, hi - lo, 1024], F32)
            LN["t"] = Trk()
            LN["base"] = lo
            p.dma("sp", LN["sb"][:], lnp[:, lo:hi, :], w=[LN["t"]])
        p.op("dve", lambda e: e.tensor_copy(out=identb[:], in_=identf[:]), r=[identf_t], w=[identb_t])
        p.op("dve", lambda e: e.memset(onesb[:], 1.0), w=[onesb_t])
        st6 = sbuf(cs, "st6", [128, 2, 6], F32); st6_t = Trk()
        mv = sbuf(cs, "mv", [128, 2], F32); mv_t = Trk()
        rstd = sbuf(cs, "rstd", [128, 1], F32); rstd_t = Trk()

        def layer_norm(src, src_t, gi, dst=None, dst_t=None):
            if dst is None:
                dst, dst_t = src, src_t
            for c in range(2):
                p.op("dve", lambda e, c=c: e.bn_stats(out=st6[:, c, :], in_=src[:, c * 512:(c + 1) * 512]),
                     r=[src_t], w=[st6_t])
            p.op("dve", lambda e: e.bn_aggr(out=mv[:], in_=st6[:].rearrange("p a b -> p (a b)")), r=[st6_t], w=[mv_t])
            p.op("act", lambda e: e.activation(out=rstd[:], in_=mv[:, 1:2], func=AF.Sqrt, bias=EPS, scale=1.0),
                 r=[mv_t], w=[rstd_t])
            p.op("dve", lambda e: e.reciprocal(out=rstd[:], in_=rstd[:]), r=[rstd_t], w=[rstd_t])
            p.op("dve", lambda e: e.tensor_scalar(out=dst, in0=src, scalar1=mv[:, 0:1], scalar2=rstd[:, 0:1],
                                                  op0=ALU.subtract, op1=ALU.mult), r=[src_t, mv_t, rstd_t], w=[dst_t])
            gi -= LN["base"]
            p.op("dve", lambda e: e.tensor_tensor(out=dst, in0=dst, in1=LN["sb"][:, gi, :], op=ALU.mult),
                 r=[dst_t, LN["t"]], w=[dst_t])
            p.op("dve", lambda e: e.tensor_tensor(out=dst, in0=dst, in1=LN["sb"][:, gi + 1, :], op=ALU.add),
                 r=[dst_t, LN["t"]], w=[dst_t])

        def transpose_to(src_bf, src_t, dst_fn, dst_t, nchunk, banks=(6, 7), eng="dve"):
            for i, c0 in enumerate(range(0, nchunk, 4)):
                b = banks[i % len(banks)]
                n = min(4, nchunk - c0)
                for k in range(n):
                    p.op("pe", lambda e, k=k, c0=c0, b=b: e.transpose(out=bank_bf(b)[:, k, :],
                                                                      in_=src_bf[:, (c0 + k) * 128:(c0 + k + 1) * 128],
                                                                      identity=identb[:]),
                         r=[src_t, identb_t], w=[PT[b]])
                if eng == "dve":
                    p.op("dve", lambda e, c0=c0, n=n, b=b: e.tensor_copy(out=dst_fn(c0, n), in_=bank_bf(b)[:, 0:n, :]),
                         r=[PT[b]], w=[dst_t])
                else:
                    p.op("act", lambda e, c0=c0, n=n, b=b: e.copy(out=dst_fn(c0, n), in_=bank_bf(b)[:, 0:n, :]),
                         r=[PT[b]], w=[dst_t])

        h1_t = tl(32, "h1d")
        h0d_t = tl(32, "h0d")
        h0Td_t = tl(8, "h0Td")
        phA = ExitStack()
        load_lnp(phA, 0, 4)
        for s in range(nseq):
            with ExitStack() as ss:
                cat = sbuf(ss, "cat", [128, 16, 1024], BF16)
                cat_t = tl(16, "cat")
                xt = [sbuf(ss, f"xt{i}", [128, 1024], F32) for i in range(2)]
                xt_t = tl(2, "xt")
                hb = sbuf(ss, "hb", [128, 1024], BF16); hb_t = Trk()
                h0T = sbuf(ss, "h0T", [128, 8, 512], BF16); h0T_t = Trk()
                xcnt = [0]

                def load_h0T(g):
                    for j in range(4):
                        i = xcnt[0] % 2
                        xcnt[0] += 1
                        row = s * 2048 + g * 512 + j * 128
                        p.dma("sp", xt[i][:], x[row:row + 128, :], w=[xt_t[i]])
                        layer_norm(xt[i][:], xt_t[i], 0)
                        p.dma("pool", h0d[row:row + 128, :], xt[i][:], r=[xt_t[i]], w=[h0d_t[row // 128]])
                        p.op("act", lambda e, i=i: e.copy(out=hb[:], in_=xt[i][:]), r=[xt_t[i]], w=[hb_t])
                        transpose_to(hb, hb_t, lambda c0, n, j=j: h0T[:, c0:c0 + n, j * 128:(j + 1) * 128], h0T_t, 8)
                    sg = s * 4 + g
                    p.dma("pool", h0Td[sg * 128:(sg + 1) * 128, :], h0T[:].rearrange("p k t -> p (k t)"), r=[h0T_t], w=[h0Td_t[sg]])

                def load_h0T_scratch(g):
                    sg = s * 4 + g
                    p.dma("sp", h0T[:].rearrange("p k t -> p (k t)"), h0Td[sg * 128:(sg + 1) * 128, :], r=[h0Td_t[sg]], w=[h0T_t])

                with ExitStack() as sa:
                    wna = sbuf(sa, "wna", [128, 8, 1536], BF16); wna_t = Trk()
                    qTm = [sbuf(sa, f"qTm{i}", [128, 4, 2048], BF16) for i in range(2)]; qT_t = Trk()
                    p.op("dve", lambda e: e.memset(qTm[0][64:128, :, :].rearrange("p a b -> p (a b)"), 0.0), w=[qT_t])
                    p.op("dve", lambda e: e.memset(qTm[1][0:64, :, :].rearrange("p a b -> p (a b)"), 0.0), w=[qT_t])
                    kT = sbuf(sa, "kT", [128, 4, 2048], BF16); kT_t = Trk()
                    vna = sbuf(sa, "vna", [128, 16, 8, 65], BF16); vna_t = Trk()
                    bias = [sbuf(sa, f"nbias{i}", [128, 40, 128], BF16) for i in range(2)]
                    bias_t = tl(2, "nbias")
                    PTn = [sbuf(sa, f"PTn{i}", [128, 5, 128], BF16) for i in range(2)]
                    PTn_t = tl(2, "PTn")
                    rc = sbuf(sa, "rc", [128, 8], F32); rc_t = Trk()
                    p.dma("sp", wna[:], b_win[:, 0:1536].rearrange("(k p) m -> p k m", p=128), r=scr["win"], w=[wna_t])
                    p.op("dve", lambda e: e.memset(vna[:, :, :, 64:65], 1.0), w=[vna_t])
                    for g in range(4):
                        load_h0T(g)
                        for c in range(8):
                            b = 4 + (c % 2)
                            for k in range(8):
                                p.op("pe", lambda e, c=c, k=k, b=b: e.matmul(out=bank(b), lhsT=wna[:, k, c * 128:(c + 1) * 128],
                                                                             rhs=h0T[:, k, :], start=(k == 0), stop=(k == 7)),
                                     r=[wna_t, h0T_t], w=[PT[b]])
                            if c < 4:
                                p.op("act", lambda e, c=c, b=b, g=g: e.mul(out=qTm[0][0:64, c, g * 512:(g + 1) * 512], in_=PS[0:64, b, :], mul=0.125),
                                     r=[PT[b]], w=[qT_t])
                                p.op("act", lambda e, c=c, b=b, g=g: e.mul(out=qTm[1][64:128, c, g * 512:(g + 1) * 512], in_=PS[64:128, b, :], mul=0.125),
                                     r=[PT[b]], w=[qT_t])
                            else:
                                p.op("act", lambda e, c=c, b=b, g=g: e.copy(out=kT[:, c - 4, g * 512:(g + 1) * 512], in_=bank(b)),
                                     r=[PT[b]], w=[kT_t])
                        for j in range(4):
                            b = 4 + (j % 2)
                            for k in range(8):
                                p.op("pe", lambda e, j=j, k=k, b=b: e.matmul(out=bank(b), lhsT=h0T[:, k, j * 128:(j + 1) * 128],
                                                                             rhs=wna[:, k, 1024:1536], start=(k == 0), stop=(k == 7)),
                                     r=[wna_t, h0T_t], w=[PT[b]])
                            p.op("dve", lambda e, j=j, b=b, g=g: e.tensor_copy(out=vna[:, g * 4 + j, :, 0:64],
                                                                               in_=bank(b).rearrange("p (h d) -> p h d", d=64)),
                                 r=[PT[b]], w=[vna_t])
                    def na_info(ti):
                        r0 = 2 * ti
                        return {0: 0, 2: 1, 28: 3, 30: 4}.get(r0, 2), na_kt0(r0)

                    def na_bias_load(ti):
                        typ, _ = na_info(ti)
                        bi = ti % 2
                        p.dma("sp", bias[bi][:], b_nab[typ * 5120:(typ + 1) * 5120, :].rearrange("(a p) q -> p a q", p=128),
                              r=scr["nab"], w=[bias_t[bi]])

                    def na_S(step):
                        ti, h = step // 8, step % 8
                        _, kt0 = na_info(ti)
                        bi = ti % 2
                        pr, po = h // 2, (h % 2) * 64
                        sb_ = 2 * (step % 2)
                        for kc in range(5):
                            bk = sb_ + (kc // 4)
                            oap = PS[:, bk, (kc % 4) * 128:(kc % 4 + 1) * 128]
                            p.op("pe", lambda e: e.matmul(
                                out=oap, lhsT=kT[:, pr, (kt0 + kc) * 128:(kt0 + kc + 1) * 128],
                                rhs=qTm[h % 2][:, pr, ti * 128:(ti + 1) * 128], start=True, stop=False),
                                r=[kT_t, qT_t], w=[PT[bk]])
                            p.op("pe", lambda e: e.matmul(
                                out=oap, lhsT=identb[:], rhs=bias[bi][:, h * 5 + kc, :], start=False, stop=True),
                                r=[identb_t, bias_t[bi]], w=[PT[bk]])

                    na_bias_load(0)
                    na_bias_load(1)
                    na_S(0)
                    for step in range(128):
                        ti, h = step // 8, step % 8
                        _, kt0 = na_info(ti)
                        if step + 1 < 128:
                            na_S(step + 1)
                        sb_ = 2 * (step % 2)
                        pi = step % 2
                        p.op("act", lambda e: e.activation(
                            out=PTn[pi][:].rearrange("p a b -> p (a b)"),
                            in_=PS[:, sb_:sb_ + 2, :].rearrange("p a b -> p (a b)")[:, 0:640], func=AF.Exp),
                            r=[PT[sb_], PT[sb_ + 1]], w=[PTn_t[pi]])
                        ob0 = 4 + 2 * (ti % 2)
                        ob = ob0 + h // 4
                        oo = (h % 4) * 65
                        for kc in range(5):
                            p.op("pe", lambda e: e.matmul(
                                out=PS[:, ob, oo:oo + 65], lhsT=PTn[pi][:, kc, :], rhs=vna[:, kt0 + kc, h, :],
                                start=(kc == 0), stop=(kc == 4)),
                                r=[PTn_t[pi], vna_t], w=[PT[ob]])
                        if h == 7:
                            if ti + 2 < 16:
                                na_bias_load(ti + 2)
                            tick = Trk()
                            for ob in (ob0, ob0 + 1):
                                o4 = PS[:, ob, 0:260].rearrange("p (h d) -> p h d", d=65)
                                hh = (ob - ob0) * 4
                                p.op("dve", lambda e: e.reciprocal(out=rc[:, hh:hh + 4], in_=o4[:, :, 64]),
                                     r=[PT[ob]], w=[rc_t])
                                p.op("dve", lambda e: e.tensor_tensor(
                                    out=cat[:, ti, hh * 64:(hh + 4) * 64].rearrange("p (h d) -> p h d", d=64),
                                    in0=o4[:, :, 0:64], in1=rc[:, hh:hh + 4].unsqueeze(2).broadcast_to([128, 4, 64]), op=ALU.mult),
                                    r=[PT[ob], rc_t], w=[cat_t[ti], tick])
                            advance_casts(2, tick)
                p.barrier()

                with ExitStack() as sa:
                    wcq = sbuf(sa, "wcq", [128, 8, 640], BF16); wcq_t = Trk()
                    wkr = sbuf(sa, "wkr", [128, 8, 192], BF16); wkr_t = Trk()
                    wuq = sbuf(sa, "wuq", [128, 3, 1536], BF16); wuq_t = Trk()
                    wukv = sbuf(sa, "wukv", [128, 2, 1024], BF16); wukv_t = Trk()
                    qg_sb = sbuf(sa, "qg_sb", [128, 3], F32); kvg_sb = sbuf(sa, "kvg_sb", [128, 2], F32); g_t = Trk()
                    cqT = sbuf(sa, "cqT", [128, 3, 2048], BF16); cqT_t = Trk()
                    Rq = sbuf(sa, "Rq", [128, 2048], F32); Rq_t = Trk()
                    KT = sbuf(sa, "KT", [96, 8, 2048], BF16); KT_t = Trk()
                    V = sbuf(sa, "Vm", [128, 16, 8, 65], BF16); V_t = Trk()
                    Qg = sbuf(sa, "Qg", [96, 8, 512], BF16); Qg_t = Trk()
                    ckv = sbuf(sa, "ckv", [128, 2, 512], BF16); ckv_t = Trk()
                    sq = sbuf(sa, "sq", [128, 512], BF16); sq_t = Trk()
                    Rkv = sbuf(sa, "Rkv", [128, 512], F32); Rkv_t = Trk()
                    rcol = sbuf(sa, "rcol", [128, 1], F32); rcol_t = Trk()
                    tab = sbuf(sa, "tab", [96, 2, 512], F32); tab_t = Trk()
                    c1 = sbuf(sa, "c1", [96, 2, 512], F32); c1_t = Trk()
                    t1 = sbuf(sa, "t1", [96, 512], F32); t1_t = Trk()
                    t2 = sbuf(sa, "t2", [96, 512], F32); t2_t = Trk()
                    PTm = [sbuf(sa, f"PTm{i}", [128, 512], BF16) for i in range(3)]
                    PTm_t = tl(3, "PTm")
                    rc4 = sbuf(sa, "rc4", [128, 4], F32); rc4_t = Trk()
                    p.dma("sp", wcq[:], b_win[:, 1536:2176].rearrange("(k p) m -> p k m", p=128), r=scr["win"], w=[wcq_t])
                    p.dma("sp", wkr[:], b_wkr.rearrange("(k p) m -> p k m", p=128), r=scr["wkr"], w=[wkr_t])
                    p.dma("sp", wuq[:], b_wuq.rearrange("(k p) m -> p k m", p=128), r=scr["wuq"], w=[wuq_t])
                    p.dma("sp", wukv[:], b_wukv.rearrange("(k p) m -> p k m", p=128), r=scr["wukv"], w=[wukv_t])
                    p.dma("sp", qg_sb[:], qg, w=[g_t])
                    p.dma("sp", kvg_sb[:], kvg, w=[g_t])
                    for c in range(3):
                        p.op("dve", lambda e, c=c: e.tensor_scalar(out=wuq[:, c, :], in0=wuq[:, c, :], scalar1=qg_sb[:, c:c + 1],
                                                                   scalar2=None, op0=ALU.mult), r=[wuq_t, g_t], w=[wuq_t])
                    for c in range(2):
                        p.op("dve", lambda e, c=c: e.tensor_scalar(out=wukv[:, c, :], in0=wukv[:, c, :], scalar1=kvg_sb[:, c:c + 1],
                                                                   scalar2=None, op0=ALU.mult), r=[wukv_t, g_t], w=[wukv_t])
                    p.op("dve", lambda e: e.memset(V[:, :, :, 64:65], 1.0), w=[V_t])

                    def rms_bcast(src_fn, nchunk, nfeat, dst, dst_t, extra):
                        for c in range(nchunk):
                            p.op("dve", lambda e, c=c: e.tensor_tensor(out=sq[:], in0=src_fn(c), in1=src_fn(c), op=ALU.mult),
                                 r=[cqT_t, ckv_t], w=[sq_t])
                            p.op("pe", lambda e, c=c: e.matmul(out=bank(3), lhsT=onesb[:], rhs=sq[:], start=(c == 0),
                                                               stop=(c == nchunk - 1)), r=[onesb_t, sq_t], w=[PT[3]])
                        p.op("act", lambda e: e.activation(out=dst, in_=bank(3), func=AF.Sqrt, bias=EPS, scale=1.0 / nfeat),
                             r=[PT[3]], w=[dst_t])
                        p.op("dve", lambda e: e.reciprocal(out=dst, in_=dst), r=[dst_t], w=[dst_t])
                        if extra != 1.0:
                            p.op("dve", lambda e: e.tensor_scalar(out=dst, in0=dst, scalar1=extra, scalar2=None, op0=ALU.mult),
                                 r=[dst_t], w=[dst_t])

                    def load_tab(g):
                        p.dma("sp", tab[64:96, :, :], rope[:, :, g * 512:(g + 1) * 512].rearrange("a f t -> f a t"), w=[tab_t])

                    for g in range(4):
                        gs = slice(g * 512, (g + 1) * 512)
                        load_h0T_scratch(g)
                        load_tab(g)
                        for c in range(5):
                            b = 4 + (c % 2)
                            for k in range(8):
                                p.op("pe", lambda e, c=c, k=k, b=b: e.matmul(out=bank(b), lhsT=wcq[:, k, c * 128:(c + 1) * 128],
                                                                             rhs=h0T[:, k, :], start=(k == 0), stop=(k == 7)),
                                     r=[wcq_t, h0T_t], w=[PT[b]])
                            if c < 3:
                                p.op("act", lambda e, c=c, b=b: e.copy(out=cqT[:, c, gs], in_=bank(b)), r=[PT[b]], w=[cqT_t])
                            else:
                                p.op("act", lambda e, c=c, b=b: e.copy(out=ckv[:, c - 3, :], in_=bank(b)), r=[PT[b]], w=[ckv_t])
                        rms_bcast(lambda c: cqT[:, c, gs], 3, 384.0, Rq[:, gs], Rq_t, 96.0 ** -0.5)
                        rms_bcast(lambda c: ckv[:, c, :], 2, 256.0, Rkv[:], Rkv_t, 1.0)
                        for h in range(8):
                            b = 4 + (h % 2)
                            for c in range(2):
                                p.op("pe", lambda e, h=h, c=c, b=b: e.matmul(out=PS[0:64, b, :], lhsT=wukv[:, c, h * 128:h * 128 + 64],
                                                                             rhs=ckv[:, c, :], start=(c == 0), stop=(c == 1)),
                                     r=[wukv_t, ckv_t], w=[PT[b]])
                            p.op("dve", lambda e, h=h, b=b: e.tensor_tensor(out=KT[0:64, h, gs], in0=PS[0:64, b, :], in1=Rkv[0:64, :],
                                                                            op=ALU.mult), r=[PT[b], Rkv_t], w=[KT_t])
                        for v_ in range(2):
                            b = 4 + v_
                            for k in range(8):
                                p.op("pe", lambda e, v_=v_, k=k, b=b: e.matmul(out=PS[0:96, b, :], lhsT=wkr[:, k, v_ * 96:(v_ + 1) * 96],
                                                                               rhs=h0T[:, k, :], start=(k == 0), stop=(k == 7)),
                                     r=[wkr_t, h0T_t], w=[PT[b]])
                        p.op("dve", lambda e: e.tensor_tensor(out=t1[64:96, :], in0=PS[64:96, 4, :], in1=tab[64:96, 0, :], op=ALU.mult),
                             r=[PT[4], tab_t], w=[t1_t])
                        p.op("dve", lambda e: e.tensor_tensor(out=t2[64:96, :], in0=PS[64:96, 5, :], in1=tab[64:96, 1, :], op=ALU.mult),
                             r=[PT[5], tab_t], w=[t2_t])
                        p.op("dve", lambda e: e.tensor_tensor(out=t1[64:96, :], in0=t1[64:96, :], in1=t2[64:96, :], op=ALU.add),
                             r=[t1_t, t2_t], w=[t1_t])
                        for h in range(8):
                            p.op("act", lambda e, h=h: e.copy(out=KT[64:96, h, gs], in_=t1[64:96, :]), r=[t1_t], w=[KT_t])
                        for j in range(4):
                            p.op("pe", lambda e, j=j: e.transpose(out=PS[:, 6, 0:128], in_=Rkv[:, j * 128:(j + 1) * 128], identity=identf[:]),
                                 r=[Rkv_t, identf_t], w=[PT[6]])
                            p.op("dve", lambda e: e.tensor_copy(out=rcol[:], in_=PS[:, 6, 0:1]), r=[PT[6]], w=[rcol_t])
                            b = 4 + (j % 2)
                            for c in range(2):
                                p.op("pe", lambda e, j=j, c=c, b=b: e.matmul(
                                    out=bank(b), lhsT=ckv[:, c, j * 128:(j + 1) * 128],
                                    rhs=wukv[:, c, :].rearrange("p (h d) -> p h d", d=128)[:, :, 64:128],
                                    start=(c == 0), stop=(c == 1)), r=[wukv_t, ckv_t], w=[PT[b]])
                            p.op("dve", lambda e, j=j, b=b, g=g: e.tensor_scalar(
                                out=V[:, g * 4 + j, :, 0:64], in0=bank(b).rearrange("p (h d) -> p h d", d=64),
                                scalar1=rcol[:, 0:1], scalar2=None, op0=ALU.mult), r=[PT[b], rcol_t], w=[V_t])
                    for g in range(4):
                        gs = slice(g * 512, (g + 1) * 512)
                        load_tab(g)
                        for a in range(2):
                            p.op("dve", lambda e, a=a: e.tensor_tensor(out=c1[64:96, a, :], in0=tab[64:96, a, :], in1=Rq[64:96, gs],
                                                                       op=ALU.mult), r=[tab_t, Rq_t], w=[c1_t])
                        for h in range(8):
                            for v_ in range(2):
                                b = 6 + v_
                                for c in range(3):
                                    p.op("pe", lambda e, h=h, v_=v_, c=c, b=b: e.matmul(
                                        out=PS[0:96, b, :], lhsT=wuq[:, c, v_ * 768 + h * 96:v_ * 768 + (h + 1) * 96],
                                        rhs=cqT[:, c, gs], start=(c == 0), stop=(c == 2)), r=[wuq_t, cqT_t], w=[PT[b]])
                            p.op("dve", lambda e, h=h: e.tensor_tensor(out=Qg[0:64, h, :], in0=PS[0:64, 6, :], in1=Rq[0:64, gs], op=ALU.mult),
                                 r=[PT[6], Rq_t], w=[Qg_t])
                            p.op("dve", lambda e: e.tensor_tensor(out=t1[64:96, :], in0=PS[64:96, 6, :], in1=c1[64:96, 0, :], op=ALU.mult),
                                 r=[PT[6], c1_t], w=[t1_t])
                            p.op("dve", lambda e: e.tensor_tensor(out=t2[64:96, :], in0=PS[64:96, 7, :], in1=c1[64:96, 1, :], op=ALU.mult),
                                 r=[PT[7], c1_t], w=[t2_t])
                            p.op("dve", lambda e, h=h: e.tensor_tensor(out=Qg[64:96, h, :], in0=t1[64:96, :], in1=t2[64:96, :], op=ALU.add),
                                 r=[t1_t, t2_t], w=[Qg_t])
                        def mla_S(st_):
                            h, kc = st_ // 16, st_ % 16
                            sbk = 4 + (st_ % 3)
                            p.op("pe", lambda e: e.matmul(
                                out=bank(sbk), lhsT=KT[0:96, h, kc * 128:(kc + 1) * 128], rhs=Qg[0:96, h, :], start=True, stop=True),
                                r=[KT_t, Qg_t], w=[PT[sbk]])

                        mla_S(0)
                        mla_S(1)
                        for st_ in range(128):
                            h, kc = st_ // 16, st_ % 16
                            if st_ + 2 < 128:
                                mla_S(st_ + 2)
                            sbk = 4 + (st_ % 3)
                            pi = st_ % 3
                            p.op("act", lambda e: e.activation(out=PTm[pi][:], in_=bank(sbk), func=AF.Exp),
                                 r=[PT[sbk]], w=[PTm_t[pi]])
                            for j in range(4):
                                p.op("pe", lambda e: e.matmul(
                                    out=PS[:, j, 0:65], lhsT=PTm[pi][:, j * 128:(j + 1) * 128], rhs=V[:, kc, h, :],
                                    start=(kc == 0), stop=(kc == 15)), r=[PTm_t[pi], V_t], w=[PT[j]])
                            if kc == 15:
                                tick = Trk()
                                p.op("dve", lambda e: e.reciprocal(out=rc4[:, 0:4], in_=PS[:, 0:4, 64]), r=PT[0:4], w=[rc4_t])
                                p.op("dve", lambda e: e.tensor_tensor(
                                    out=cat[:, g * 4:(g + 1) * 4, 512 + h * 64:512 + (h + 1) * 64], in0=PS[:, 0:4, 0:64],
                                    in1=rc4[:, 0:4].unsqueeze(2).broadcast_to([128, 4, 64]), op=ALU.mult),
                                    r=PT[0:4] + [rc4_t], w=cat_t[g * 4:(g + 1) * 4] + [tick])
                                advance_casts(4, tick)
                p.barrier()

                with ExitStack() as sa:
                    wo = sbuf(sa, "wo", [128, 8, 1024], BF16); wo_t = Trk()
                    cT = sbuf(sa, "cT", [128, 8, 128], BF16); cT_t = Trk()
                    yt = [sbuf(sa, f"yt{i}", [128, 1024], F32) for i in range(2)]
                    yt_t = tl(2, "yt")
                    catf = sbuf(sa, "catf", [128, 1024], F32); catf_t = Trk()
                    p.dma("sp", wo[:], b_wo.rearrange("(k p) m -> p k m", p=128), r=scr["wo"], w=[wo_t])
                    for ti in range(16):
                        i = ti % 2
                        row = s * 2048 + ti * 128
                        p.dma("sp", xt[i][:], h0d[row:row + 128, :], r=[h0d_t[row // 128]], w=[xt_t[i]])
                        if debug:
                            p.op("act", lambda e, ti=ti: e.copy(out=catf[:], in_=cat[:, ti, :]), r=[cat_t[ti]], w=[catf_t])
                            p.dma("sp", catd[row:row + 128, :], catf[:], r=[catf_t])
                        transpose_to(cat[:, ti, :], cat_t[ti], lambda c0, n: cT[:, c0:c0 + n, :], cT_t, 8)
                        for half in range(2):
                            b = 4 + half
                            for k in range(8):
                                p.op("pe", lambda e, k=k, half=half, b=b: e.matmul(out=bank(b), lhsT=cT[:, k, :],
                                                                                   rhs=wo[:, k, half * 512:(half + 1) * 512],
                                                                                   start=(k == 0), stop=(k == 7)),
                                     r=[cT_t, wo_t], w=[PT[b]])
                            p.op("dve", lambda e, i=i, half=half, b=b: e.scalar_tensor_tensor(
                                out=yt[i][:, half * 512:(half + 1) * 512], in0=xt[i][:, half * 512:(half + 1) * 512], scalar=ALPHA,
                                in1=bank(b), op0=ALU.mult, op1=ALU.add), r=[xt_t[i], PT[b]], w=[yt_t[i]])
                        layer_norm(yt[i][:], yt_t[i], 2)
                        p.dma("pool", h1d[row:row + 128, :], yt[i][:], r=[yt_t[i]], w=[h1_t[row // 128]])
                p.barrier()

        phA.close()
        advance_casts(1000)
        if do_b:
            load_lnp(es, 4, 6)
            phase_b(nc, p, es, sbuf, PS, PT, bank, bank_bf, NT, dict(
                h1d=h1d, h1_t=h1_t, out=out, pT=pT, bgd=bgd, iota=iota, skT=skT, b_wq=b_wq, b_u=b_u, b_v=b_v,
                b_wple=b_wple, b_wg=b_wg, scr=scr, identf=identf, identf_t=identf_t, identb=identb, identb_t=identb_t,
                layer_norm=layer_norm, transpose_to=transpose_to, LN=LN,
                lnscr=(st6, st6_t, mv, mv_t, rstd, rstd_t)))
        p.finish()
    return nc


def na_kt0(r0):
    rs = min(max(r0 - 4, 0), 24)
    bs = min(rs, 23)
    return min(bs // 2, 11)


def phase_b(nc, p, es, sbuf, PS, PT, bank, bank_bf, NT, a):
    h1d, h1_t, out, pT = a["h1d"], a["h1_t"], a["out"], a["pT"]
    scr = a["scr"]
    identf, identf_t = a["identf"], a["identf_t"]
    layer_norm, transpose_to = a["layer_norm"], a["transpose_to"]
    NG = NT // 256
    with ExitStack() as sb_:
        iota_f = sbuf(sb_, "iota_f", [128, 128], F32); iota_b = sbuf(sb_, "iota_b", [128, 128], BF16); iota_t = Trk()
        skTf = sbuf(sb_, "skTf", [128, 256], F32); skTb = sbuf(sb_, "skTb", [128, 256], BF16); sk_t = Trk()
        bg_sb = sbuf(sb_, "bg_sb", [128, 1024], F32); bg_t = Trk()
        wple = sbuf(sb_, "wple", [128, 2, 1024], BF16); wple_t = Trk()
        Gall = sbuf(sb_, "Gall", [128, 256, 128], BF16); G_t = Trk()
        h1t_ = [sbuf(sb_, f"h1t{i}", [128, 1024], F32) for i in range(2)]
        h1t = [h1t_, h1t_]
        h1tt_ = tl(2)
        h1t_t = [h1tt_, h1tt_]
        hst = sbuf(sb_, "hst", [128, 1024], F32); hst_t = Trk()
        h1b = sbuf(sb_, "h1b", [128, 1024], BF16); h1b_t = Trk()
        h1T = [sbuf(sb_, f"h1T{q}", [128, 8, 256], BF16) for q in range(3)]; h1T_t = tl(3)
        qTp = sbuf(sb_, "qTp", [128, 16, 256], BF16); qTp_t = Trk()
        wqs = [sbuf(sb_, f"wqs{i}", [128, 8, 256], BF16) for i in range(3)]; wqs_t = tl(3)
        ub = [sbuf(sb_, f"ub{i}", [128, 4096], BF16) for i in range(2)]; ub_t = tl(2)
        vb = [sbuf(sb_, f"vb{i}", [128, 4096], BF16) for i in range(2)]; vb_t = tl(2)
        scs = [sbuf(sb_, f"sc{j}", [128, 16, 128], F32) for j in range(2)]; scs_t = tl(2)
        v16 = sbuf(sb_, "v16", [128, 16, 16], F32); v16_t = Trk()
        ix = sbuf(sb_, "ix", [128, 16, 16], U32); ix_t = Trk()
        ixf = sbuf(sb_, "ixf", [128, 16, 16], F32); ixf_t = Trk()
        cv = sbuf(sb_, "cv", [128, 8, 16], F32); cv_t = Trk()
        ci = sbuf(sb_, "ci", [128, 8, 16], U32); ci_t = Trk()
        ai = sbuf(sb_, "ai", [128, 2, 128], U32); ai_t = Trk()
        af = sbuf(sb_, "af", [128, 2, 8, 16], F32); af_t = Trk()
        EEs = [sbuf(sb_, f"EE{j}", [128, 3, 128], F32) for j in range(2)]; EEs_t = tl(2)
        gs = sbuf(sb_, "gs", [128, 8], F32); gs_t = Trk()
        ETs = [sbuf(sb_, f"ETs{j}", [128, 3, 128], BF16) for j in range(2)]; ETs_t = tl(2)
        OH2 = [sbuf(sb_, f"OH2_{i}", [128, 128], BF16) for i in range(4)]; OH2_t = tl(4)
        OH1 = [sbuf(sb_, f"OH1_{i}", [128, 128], BF16) for i in range(4)]; OH1_t = tl(4)
        gl = [sbuf(sb_, f"gl{i}", [128, 256], BF16) for i in range(2)]; gl_t = tl(2)
        Am = [sbuf(sb_, f"Am{i}", [128, 256], BF16) for i in range(2)]; Am_t = tl(2)
        pTb1 = sbuf(sb_, "pTb", [128, 2, 256], BF16); pTb1_t = Trk()
        pTb = [pTb1, pTb1]; pTb_t = [pTb1_t, pTb1_t]
        gate = [sbuf(sb_, f"gate{i}", [128, 512], F32) for i in range(2)]; gate_t = tl(2)

        p.dma("sp", iota_f[:], a["iota"], w=[iota_t])
        p.op("dve", lambda e: e.tensor_copy(out=iota_b[:], in_=iota_f[:]), r=[iota_t], w=[iota_t])
        p.dma("sp", skTf[:], a["skT"], w=[sk_t])
        p.op("dve", lambda e: e.tensor_copy(out=skTb[:], in_=skTf[:]), r=[sk_t], w=[sk_t])
        p.dma("sp", bg_sb[:], a["bgd"], w=[bg_t])
        p.dma("sp", wple[:], a["b_wple"].rearrange("(k p) m -> p k m", p=128), r=scr["wple"], w=[wple_t])

        def b1_front(gi):
            row = gi * 256
            seq, tok0 = row // 2048, row % 2048
            q_ = gi % 2

            def wq_load(blk):
                p.dma("sp", wqs[blk % 3][:].rearrange("p k m -> p (k m)"), a["b_wq"][blk * 128:(blk + 1) * 128, :],
                      r=scr["wq"], w=[wqs_t[blk % 3]])

            wq_load(0)
            wq_load(1)
            for j in range(2):
                p.dma("sp", hst[:], h1d[row + j * 128:row + (j + 1) * 128, :], r=[h1_t[row // 128 + j]], w=[hst_t])
                yield
                yield
                p.op("act", lambda e: e.copy(out=h1b[:], in_=hst[:]), r=[hst_t], w=[h1b_t])
                yield
                for half in range(2):
                    bb = 6 + half
                    for k in range(4):
                        kk = half * 4 + k
                        p.op("pe", lambda e: e.transpose(out=bank_bf(bb)[:, k, :], in_=h1b[:, kk * 128:(kk + 1) * 128],
                                                         identity=a["identb"][:]), r=[h1b_t, a["identb_t"]], w=[PT[bb]])
                yield
                for half in range(2):
                    bb = 6 + half
                    p.op("act", lambda e: e.copy(out=h1T[gi % 3][:, half * 4:(half + 1) * 4, j * 128:(j + 1) * 128],
                                                 in_=bank_bf(bb)[:, 0:4, :]), r=[PT[bb]], w=[h1T_t[gi % 3]])
                yield
            pend = None
            for blk in range(8):
                i = blk % 3
                if blk + 2 < 8:
                    wq_load(blk + 2)
                for cc in range(2):
                    hc = blk * 2 + cc
                    b = 6 + hc % 2
                    for k in range(8):
                        p.op("pe", lambda e: e.matmul(out=PS[:, b, 0:256], lhsT=wqs[i][:, k, cc * 128:(cc + 1) * 128],
                                                      rhs=h1T[gi % 3][:, k, :], start=(k == 0), stop=(k == 7)),
                             r=[wqs_t[i], h1T_t[gi % 3]], w=[PT[b]])
                    if pend is not None:
                        hc0, b0 = pend
                        p.op("act", lambda e: e.copy(out=qTp[:, hc0, :], in_=PS[:, b0, 0:256]), r=[PT[b0]], w=[qTp_t])
                    pend = (hc, b)
                    yield
            hc0, b0 = pend
            p.op("act", lambda e: e.copy(out=qTp[:, hc0, :], in_=PS[:, b0, 0:256]), r=[PT[b0]], w=[qTp_t])
            yield
            pend = None
            for j in range(2):
                sc, sc_t = scs[j], scs_t[j]
                for qd in range(4):
                    b = 6 + qd % 2
                    for u in range(4):
                        hc = qd * 4 + u
                        half = hc % 2
                        p.op("pe", lambda e: e.matmul(out=PS[:, b, u * 128:(u + 1) * 128], lhsT=qTp[:, hc, j * 128:(j + 1) * 128],
                                                      rhs=skTb[:, half * 128:(half + 1) * 128], start=True, stop=True),
                             r=[qTp_t, sk_t], w=[PT[b]])
                    if pend is not None:
                        sc0, sct0, qd0, b0 = pend
                        p.op("act", lambda e: e.copy(out=sc0[:, 4 * qd0:4 * qd0 + 4, :].rearrange("p a b -> p (a b)"), in_=bank(b0)),
                             r=[PT[b0]], w=[sct0])
                    pend = (sc, sc_t, qd, b)
                    yield
            sc0, sct0, qd0, b0 = pend
            p.op("act", lambda e: e.copy(out=sc0[:, 4 * qd0:4 * qd0 + 4, :].rearrange("p a b -> p (a b)"), in_=bank(b0)),
                 r=[PT[b0]], w=[sct0])
            yield

        def b1_chains(gi):
            for j in range(2):
                sc, sc_t = scs[j], scs_t[j]
                oh, oh_t = sc[:].rearrange("p (h a) (b c) -> p h (a b) c", a=2, c=16), sc_t
                EE, EE_t = EEs[j], EEs_t[j]
                for gq in range(16):
                    for rnd in range(2):
                        vs = v16[:, gq, rnd * 8:(rnd + 1) * 8]
                        p.op("dve", lambda e: e.max(out=vs, in_=sc[:, gq, :]), r=[sc_t], w=[v16_t])
                        p.op("dve", lambda e: e.max_index(out=ix[:, gq, rnd * 8:(rnd + 1) * 8], in_max=vs, in_values=sc[:, gq, :]),
                             r=[sc_t, v16_t], w=[ix_t])
                        if rnd == 0:
                            p.op("dve", lambda e: e.match_replace(out=sc[:, gq, :], in_to_replace=vs, in_values=sc[:, gq, :],
                                                                  imm_value=-1e30), r=[v16_t, sc_t], w=[sc_t])
                    yield
                p.op("dve", lambda e: e.tensor_copy(out=ixf[:], in_=ix[:]), r=[ix_t], w=[ixf_t])
                v4 = v16[:].rearrange("p (h two) k -> p h two k", two=2)
                ix4 = ixf[:].rearrange("p (h two) k -> p h two k", two=2)
                cand4 = sc[:].rearrange("p (h a) (b c) -> p h (a b) c", a=2, c=16)
                candf = sc[:].rearrange("p (h a) b -> p h (a b)", a=2)
                p.op("dve", lambda e: e.tensor_tensor(out=cand4, in0=v4[:, :, 0, :].unsqueeze(3).broadcast_to([128, 8, 16, 16]),
                                                      in1=v4[:, :, 1, :].unsqueeze(2).broadcast_to([128, 8, 16, 16]), op=ALU.add),
                     r=[v16_t, sc_t], w=[sc_t])
                yield
                for h in range(8):
                    for rnd in range(2):
                        vs = cv[:, h, rnd * 8:(rnd + 1) * 8]
                        p.op("dve", lambda e: e.max(out=vs, in_=candf[:, h, :]), r=[sc_t], w=[cv_t])
                        p.op("dve", lambda e: e.max_index(out=ci[:, h, rnd * 8:(rnd + 1) * 8], in_max=vs, in_values=candf[:, h, :]),
                             r=[sc_t, cv_t], w=[ci_t])
                        if rnd == 0:
                            p.op("dve", lambda e: e.match_replace(out=candf[:, h, :], in_to_replace=vs, in_values=candf[:, h, :],
                                                                  imm_value=-1e30), r=[cv_t, sc_t], w=[sc_t])
                    yield
                gg3 = EE[:, 2, :].rearrange("p (h k) -> p h k", k=16)
                p.op("dve", lambda e: e.tensor_tensor(out=gg3, in0=cv[:], in1=cv[:, :, 0:1].broadcast_to([128, 8, 16]), op=ALU.subtract),
                     r=[cv_t], w=[EE_t])
                cif = ci[:].rearrange("p h k -> p (h k)")
                p.op("dve", lambda e: e.tensor_single_scalar(out=ai[:, 0, :], in_=cif, scalar=4, op=ALU.logical_shift_right),
                     r=[ci_t], w=[ai_t])
                p.op("dve", lambda e: e.tensor_single_scalar(out=ai[:, 1, :], in_=cif, scalar=15, op=ALU.bitwise_and),
                     r=[ci_t], w=[ai_t])
                p.op("dve", lambda e: e.tensor_copy(out=af[:].rearrange("p a h k -> p a (h k)"), in_=ai[:]), r=[ai_t], w=[af_t])
                yield
                for w_ in range(2):
                    p.op("dve", lambda e: e.tensor_tensor(
                        out=oh, in0=iota_f[:, 0:16].unsqueeze(1).unsqueeze(1).broadcast_to([128, 8, 16, 16]),
                        in1=af[:, w_, :, :].unsqueeze(3).broadcast_to([128, 8, 16, 16]), op=ALU.is_equal),
                        r=[iota_t, af_t], w=[oh_t])
                    yield
                    p.op("dve", lambda e: e.tensor_tensor(
                        out=oh, in0=oh, in1=ix4[:, :, w_, :].unsqueeze(2).broadcast_to([128, 8, 16, 16]), op=ALU.mult),
                        r=[oh_t, ixf_t], w=[oh_t])
                    yield
                    p.op("dve", lambda e: e.tensor_reduce(out=EE[:, 1 - w_, :].rearrange("p (h k) -> p h k", k=16), in_=oh,
                                                          axis=AX.X, op=ALU.add), r=[oh_t], w=[EE_t])
                    yield

        def b1_tail(gi):
            for _ in range(6):
                yield
            for j in range(2):
                EE, EE_t = EEs[j], EEs_t[j]
                p.op("act", lambda e: e.activation(out=EE[:, 2, :], in_=EE[:, 2, :], func=AF.Exp), r=[EE_t], w=[EE_t])
            yield
            yield
            for j in range(2):
                EE, EE_t = EEs[j], EEs_t[j]
                gg3 = EE[:, 2, :].rearrange("p (h k) -> p h k", k=16)
                p.op("dve", lambda e: e.tensor_reduce(out=gs[:], in_=gg3, axis=AX.X, op=ALU.add), r=[EE_t], w=[gs_t])
                p.op("dve", lambda e: e.reciprocal(out=gs[:], in_=gs[:]), r=[gs_t], w=[gs_t])
                p.op("dve", lambda e: e.tensor_tensor(out=gg3, in0=gg3, in1=gs[:].unsqueeze(2).broadcast_to([128, 8, 16]), op=ALU.mult),
                     r=[EE_t, gs_t], w=[EE_t])
            for _ in range(4):
                yield
            for j in range(2):
                EE, EE_t = EEs[j], EEs_t[j]
                bb = 6 + j
                for q in range(3):
                    p.op("pe", lambda e: e.transpose(out=PS[:, bb, q * 128:(q + 1) * 128], in_=EE[:, q, :], identity=identf[:]),
                         r=[EE_t, identf_t], w=[PT[bb]])
            yield
            yield
            for j in range(2):
                bb = 6 + j
                p.op("act", lambda e: e.copy(out=ETs[j][:].rearrange("p a b -> p (a b)"), in_=PS[:, bb, 0:384]), r=[PT[bb]], w=[ETs_t[j]])
            yield

        def b1e(gi):
            for j in range(2):
                for tt in range(128):
                    r4 = tt % 4
                    b = 4 + (tt // 4) % 2
                    p.op("dve", lambda e: e.tensor_scalar(out=OH2[r4][:], in0=iota_b[:], scalar1=ETs[j][:, 0, tt:tt + 1],
                                                          scalar2=None, op0=ALU.is_equal), r=[iota_t, ETs_t[j]], w=[OH2_t[r4]])
                    p.op("dve", lambda e: e.tensor_scalar(out=OH1[r4][:], in0=iota_b[:], scalar1=ETs[j][:, 1, tt:tt + 1],
                                                          scalar2=ETs[j][:, 2, tt:tt + 1], op0=ALU.is_equal, op1=ALU.mult),
                         r=[iota_t, ETs_t[j]], w=[OH1_t[r4]])
                    p.op("pe", lambda e: e.matmul(out=PS[:, b, r4 * 128:(r4 + 1) * 128], lhsT=OH2[r4][:], rhs=OH1[r4][:],
                                                  start=True, stop=True), r=[OH2_t[r4], OH1_t[r4]], w=[PT[b]])
                    if r4 == 3:
                        t0 = j * 128 + tt - 3
                        p.op("act", lambda e: e.copy(out=Gall[:, t0:t0 + 4, :].rearrange("p a b -> p (a b)"), in_=bank(b)),
                             r=[PT[b]], w=[G_t])
                        yield

        def b2_load(cb):
            i = cb % 2
            uv = ub[i][:].rearrange("p (c f) -> p c f", f=1024)
            vv = vb[i][:].rearrange("p (c f) -> p c f", f=1024)
            p.dma("sp", uv, a["b_u"][cb * 512:(cb + 1) * 512, :].rearrange("(c p) f -> p c f", p=128), r=scr["u"], w=[ub_t[i]])
            p.dma("sp", vv, a["b_v"][cb * 512:(cb + 1) * 512, :].rearrange("(c p) f -> p c f", p=128), r=scr["v"], w=[vb_t[i]])

        def b2(gi, preloaded):
            q_ = gi % 2

            def U(c):
                i = (c // 4) % 2
                uv = ub[i][:].rearrange("p (c f) -> p c f", f=1024)
                hb_ = 4 + c % 2
                for k in range(8):
                    p.op("pe", lambda e: e.matmul(out=PS[:, hb_, 0:256], lhsT=uv[:, c % 4, k * 128:(k + 1) * 128],
                                                  rhs=h1T[gi % 3][:, k, :], start=(k == 0), stop=(k == 7)),
                         r=[ub_t[i], h1T_t[gi % 3]], w=[PT[hb_]])

            if not preloaded:
                b2_load(0)
                b2_load(1)
            U(0)
            for c in range(128):
                if c + 1 < 128:
                    U(c + 1)
                i = (c // 4) % 2
                vv = vb[i][:].rearrange("p (c f) -> p c f", f=1024)
                hb_ = 4 + c % 2
                ci_ = c % 2
                p.op("act", lambda e: e.activation(out=gl[ci_][:], in_=PS[:, hb_, 0:256], func=AF.Gelu),
                     r=[PT[hb_]], w=[gl_t[ci_]])
                p.op("pool", lambda e: e.tensor_tensor(out=Am[ci_][:], in0=gl[ci_][:], in1=Gall[:, :, c], op=ALU.mult),
                     r=[gl_t[ci_], G_t], w=[Am_t[ci_]])
                for j in range(2):
                    for half in range(2):
                        ob = j * 2 + half
                        p.op("pe", lambda e: e.matmul(
                            out=bank(ob), lhsT=Am[ci_][:, j * 128:(j + 1) * 128], rhs=vv[:, c % 4, half * 512:(half + 1) * 512],
                            start=(c == 0), stop=(c == 127)), r=[Am_t[ci_], vb_t[i]], w=[PT[ob]])
                if c % 4 == 3:
                    if c // 4 + 2 < 32:
                        b2_load(c // 4 + 2)
                    elif gi + 1 < NG:
                        b2_load(c // 4 + 2 - 32)
                yield

        def b3_loads(gi):
            row = gi * 256
            seq, tok0 = row // 2048, row % 2048
            for j in range(2):
                p.dma("sp", h1t_[j][:], h1d[row + j * 128:row + (j + 1) * 128, :], r=[h1_t[row // 128 + j]], w=[h1tt_[j]])
            p.dma("pool", pTb1[:], pT[seq, :, tok0:tok0 + 256].rearrange("(k p) t -> p k t", p=128), w=[pTb1_t])
            yield

        def b3a(gi):
            q_ = gi % 2
            for j in range(2):
                for half in range(2):
                    hs = slice(half * 512, (half + 1) * 512)
                    ob = j * 2 + half
                    p.op("dve", lambda e: e.scalar_tensor_tensor(out=h1t[q_][j][:, hs], in0=h1t[q_][j][:, hs], scalar=ALPHA, in1=bank(ob),
                                                                 op0=ALU.mult, op1=ALU.add), r=[h1t_t[q_][j], PT[ob]], w=[h1t_t[q_][j]])

        def b3b(gi):
            row = gi * 256
            q_ = gi % 2

            def wg_load(qd):
                p.dma("sp", wqs[qd % 3][:].rearrange("p k m -> p (k m)"), a["b_wg"][qd * 128:(qd + 1) * 128, :],
                      r=scr["wg"], w=[wqs_t[qd % 3]])

            for qd in range(3):
                wg_load(qd)
            yield
            yield
            for qd in range(4):
                cs = slice(qd * 256, (qd + 1) * 256)
                wv = wqs[qd % 3]
                for j in range(2):
                    for k in range(8):
                        p.op("pe", lambda e: e.matmul(out=PS[:, 6, 0:256], lhsT=h1T[gi % 3][:, k, j * 128:(j + 1) * 128], rhs=wv[:, k, :],
                                                      start=(k == 0), stop=(k == 7)), r=[wqs_t[qd % 3], h1T_t[gi % 3]], w=[PT[6]])
                    for k in range(2):
                        p.op("pe", lambda e: e.matmul(out=PS[:, 7, 0:256], lhsT=pTb[q_][:, k, j * 128:(j + 1) * 128], rhs=wple[:, k, cs],
                                                      start=(k == 0), stop=(k == 1)), r=[pTb_t[q_], wple_t], w=[PT[7]])
                    yield
                    p.op("dve", lambda e: e.tensor_tensor(out=gate[j][:, 0:256], in0=PS[:, 6, 0:256], in1=bg_sb[:, cs], op=ALU.add),
                         r=[PT[6], bg_t], w=[gate_t[j]])
                    yield
                    p.op("act", lambda e: e.activation(out=gate[j][:, 0:256], in_=gate[j][:, 0:256], func=AF.Sigmoid),
                         r=[gate_t[j]], w=[gate_t[j]])
                    yield
                    p.op("dve", lambda e: e.tensor_tensor(out=gate[j][:, 0:256], in0=gate[j][:, 0:256], in1=PS[:, 7, 0:256], op=ALU.mult),
                         r=[gate_t[j], PT[7]], w=[gate_t[j]])
                    p.op("dve", lambda e: e.tensor_tensor(out=h1t[q_][j][:, cs], in0=h1t[q_][j][:, cs], in1=gate[j][:, 0:256], op=ALU.add),
                         r=[h1t_t[q_][j], gate_t[j]], w=[h1t_t[q_][j]])
                if qd == 0:
                    wg_load(3)
            yield
            for j in range(2):
                src, src_t = h1t[q_][j][:], h1t_t[q_][j]
                st6, st6_t, mv, mv_t, rstd, rstd_t = a["lnscr"]
                for c in range(2):
                    p.op("dve", lambda e: e.bn_stats(out=st6[:, c, :], in_=src[:, c * 512:(c + 1) * 512]), r=[src_t], w=[st6_t])
                p.op("dve", lambda e: e.bn_aggr(out=mv[:], in_=st6[:].rearrange("p a b -> p (a b)")), r=[st6_t], w=[mv_t])
                yield
                yield
                p.op("act", lambda e: e.activation(out=rstd[:], in_=mv[:, 1:2], func=AF.Sqrt, bias=EPS, scale=1.0), r=[mv_t], w=[rstd_t])
                yield
                LN = a["LN"]
                p.op("dve", lambda e: e.reciprocal(out=rstd[:], in_=rstd[:]), r=[rstd_t], w=[rstd_t])
                p.op("dve", lambda e: e.tensor_scalar(out=src, in0=src, scalar1=mv[:, 0:1], scalar2=rstd[:, 0:1],
                                                      op0=ALU.subtract, op1=ALU.mult), r=[src_t, mv_t, rstd_t], w=[src_t])
                p.op("dve", lambda e: e.tensor_tensor(out=src, in0=src, in1=LN["sb"][:, 4 - LN["base"], :], op=ALU.mult),
                     r=[src_t, LN["t"]], w=[src_t])
                p.op("dve", lambda e: e.tensor_tensor(out=src, in0=src, in1=LN["sb"][:, 5 - LN["base"], :], op=ALU.add),
                     r=[src_t, LN["t"]], w=[src_t])
                yield
            for _ in range(6):
                yield
            for j in range(2):
                p.dma("pool", out[row + j * 128:row + (j + 1) * 128, :], h1t[q_][j][:], r=[h1t_t[q_][j]])
            for _ in range(6):
                yield

        def run(gen):
            for _ in gen:
                pass

        def chain(*gens):
            for g_ in gens:
                if g_ is not None:
                    yield from g_

        def interleave(main, bg, n_main, n_bg):
            acc = 0.0
            alive = bg is not None
            for _ in main:
                acc += n_bg / n_main
                while alive and acc >= 1.0:
                    acc -= 1.0
                    try:
                        next(bg)
                    except StopIteration:
                        alive = False
            if alive:
                for _ in bg:
                    pass

        def merge(ga, gb):
            alive = [ga, gb]
            while alive:
                for g_ in list(alive):
                    if g_ is None:
                        alive.remove(g_)
                        continue
                    try:
                        yield next(g_)
                    except StopIteration:
                        alive.remove(g_)

        run(chain(b1_front(0), b1_chains(0), b1_tail(0), b3_loads(0)))
        run(b1e(0))
        for gi in range(NG):
            nxt = gi + 1 < NG
            bg = chain(b1_front(gi + 1) if nxt else None,
                       merge(b1_chains(gi + 1) if nxt else None, b3b(gi - 1) if gi > 0 else None),
                       b1_tail(gi + 1) if nxt else None,
                       b3_loads(gi) if gi > 0 else None)
            interleave(b2(gi, gi > 0), bg, 128, 150 if gi > 0 else 120)
            b3a(gi)
            if nxt:
                run(b1e(gi + 1))
        run(b3b(NG - 1))


def na_bias_tables(rpb):
    W, KH, KW, ROWS = 64, 8, 16, 32
    outs = []
    for r0 in (0, 2, 4, 28, 30):
        kt0 = na_kt0(r0)
        key_tok = kt0 * 128 + np.arange(640)
        krow, kcol = key_tok // W, key_tok % W
        q = np.arange(128)
        qrow, qcol = r0 + q // W, q % W
        rs = np.clip(qrow - KH // 2, 0, ROWS - KH)
        cs = np.clip(qcol - KW // 2, 0, W - KW)
        di = krow[:, None] - qrow[None, :] + (KH - 1)
        dj = kcol[:, None] - qcol[None, :] + (KW - 1)
        ok = ((krow[:, None] >= rs[None, :]) & (krow[:, None] < rs[None, :] + KH)
              & (kcol[:, None] >= cs[None, :]) & (kcol[:, None] < cs[None, :] + KW))
        dic = np.clip(di, 0, 14)
        djc = np.clip(dj, 0, 30)
        g = rpb[:, dic, djc]
        g = np.where(ok[None], g, np.float32(NEG)).astype(np.float32)
        outs.append(g.reshape(8, 5, 128, 128))
    return np.stack(outs, 0)


def rope_tables():
    t = np.arange(2048)
    row = (t // 64).astype(np.float32)
    col = (t % 64).astype(np.float32)
    inv = (10000.0 ** (-np.arange(0, 16, 2, dtype=np.float32) / 16)).astype(np.float32)
    ang = np.concatenate([row[:, None] * inv[None, :], col[:, None] * inv[None, :]], axis=-1)
    cos = np.cos(ang).astype(np.float32)
    sin = np.sin(ang).astype(np.float32)
    tab = np.zeros((2, 32, 2048), np.float32)
    tab[0] = np.repeat(cos, 2, axis=1).T
    sgn = np.tile(np.array([-1.0, 1.0], np.float32), 16)
    tab[1] = (np.repeat(sin, 2, axis=1) * sgn[None, :]).T
    return tab


def pair_swap_cols(w, cols):
    w2 = w.copy()
    w2[:, cols[0::2]] = w[:, cols[1::2]]
    w2[:, cols[1::2]] = w[:, cols[0::2]]
    return w2


def host_layout(inputs):
    f = lambda k: np.asarray(inputs[k], dtype=np.float32)
    w_in = f("w_in")[0]
    w_uq = f("w_uq")[0]
    sh = {}
    sh["lnp"] = np.ascontiguousarray(np.broadcast_to(np.stack(
        [f("emb_ln_g"), f("emb_ln_b"), f("ln1_g")[0], f("ln1_b")[0], f("ln2_g")[0], f("ln2_b")[0]], 0)[None], (128, 6, 1024)))
    sh["bg"] = np.ascontiguousarray(np.broadcast_to(f("ple_gate_b")[0][None], (128, 1024)))
    sh["ident"] = np.eye(128, dtype=np.float32)
    sh["iota"] = np.ascontiguousarray(np.broadcast_to(np.arange(128, dtype=np.float32)[None], (128, 128)))
    sh["rope"] = rope_tables()
    sh["w_in"] = np.ascontiguousarray(w_in)
    kr96 = w_in[:, 2112:2208]
    sh["w_kr"] = np.ascontiguousarray(np.concatenate([kr96, pair_swap_cols(kr96, np.arange(64, 96))], axis=1))
    rope_cols = np.concatenate([h * 96 + 64 + np.arange(32) for h in range(8)])
    sh["w_uq"] = np.ascontiguousarray(np.concatenate([w_uq, pair_swap_cols(w_uq, rope_cols)], axis=1))
    sh["w_ukv"] = np.ascontiguousarray(f("w_ukv")[0])
    sh["qg"] = np.ascontiguousarray(f("mla_q_norm_g")[0].reshape(3, 128).T)
    sh["kvg"] = np.ascontiguousarray(f("mla_kv_norm_g")[0].reshape(2, 128).T)
    sh["nab"] = np.ascontiguousarray(na_bias_tables(f("na_rpb")[0]).reshape(5 * 8 * 5 * 128, 128))
    sh["w_o"] = np.ascontiguousarray(f("w_o")[0])
    sh["w_q"] = np.ascontiguousarray(f("peer_w_q")[0].reshape(8, 128, 8, 256).transpose(2, 1, 0, 3)).reshape(1024, 2048)
    sk = f("peer_sub_keys")[0]
    sh["skT"] = np.ascontiguousarray(np.concatenate([sk[0].T, sk[1].T], axis=1))
    U = f("peer_u")[0]
    sh["u_l"] = np.ascontiguousarray(U.reshape(128, 128, 8, 128).transpose(0, 3, 2, 1)).reshape(16384, 1024)
    sh["v_l"] = np.ascontiguousarray(f("peer_v")[0])
    sh["w_ple"] = np.ascontiguousarray(f("ple_w")[0])
    sh["w_g"] = np.ascontiguousarray(f("ple_gate_w")[0].reshape(8, 128, 4, 256).transpose(2, 1, 0, 3)).reshape(512, 2048)
    return sh


def kernel(**inputs):
    sh = host_layout(inputs)
    x = np.asarray(inputs["x"], dtype=np.float32)
    pp = np.asarray(inputs["p"], dtype=np.float32)[0]
    in_maps = []
    for c in range(8):
        m = dict(sh)
        m["x"] = np.ascontiguousarray(x[2 * c:2 * c + 2].reshape(4096, 1024))
        m["pT"] = np.ascontiguousarray(pp[2 * c:2 * c + 2].transpose(0, 2, 1))
        in_maps.append(m)
    nc = build()
    res = run_bass_kernel_spmd(nc, in_maps, core_ids=list(range(8)))
    return np.concatenate([r["out"].reshape(2, 2048, 1024) for r in res.results], axis=0)
```
